# Optimizing a Trainium2 kernel written in Bass

```python
import jax, jax.numpy as jnp
from jax import lax
import numpy as np

D_MODEL = 1024
BATCH = 1
SEQ = 16384
DEPTH = 4
DEC_BATCH = 16
DEC_SEQ = 32
PAST_LEN = 1024

CHUNK = 64
D_A = D_MODEL // 2
D_B = D_MODEL // 2
D_C = D_MODEL
CONV_A_WIDTH = 31
CONV_C_WIDTH = 3
POOL_WINDOWS = (2, 4, 8, 16)
N_POOL_GROUPS = len(POOL_WINDOWS)
POOL_GROUP = D_B // N_POOL_GROUPS
POOL_HIST = max(POOL_WINDOWS) - 1
D_FF = ((8 * D_MODEL // 3 + 127) // 128) * 128
N_EVEN = (DEPTH + 1) // 2
N_ODD = DEPTH // 2
N_MOD = 9
FFN_RES = 0.5
EPS = 1e-6

kernel_name = "hybrid_streaming_conformer_pool_shortconv_step"


def rmsnorm(x, g):
    xf = x.astype(jnp.float32)
    y = xf * lax.rsqrt(jnp.mean(xf * xf, axis=-1, keepdims=True) + EPS)
    return (y * g.astype(jnp.float32)).astype(x.dtype)


def layernorm(x, g, b):
    xf = x.astype(jnp.float32)
    mu = jnp.mean(xf, axis=-1, keepdims=True)
    xc = xf - mu
    y = xc * lax.rsqrt(jnp.mean(xc * xc, axis=-1, keepdims=True) + EPS)
    return (y * g.astype(jnp.float32) + b.astype(jnp.float32)).astype(x.dtype)


def modulate(h, shift, scale):
    return h * (1 + scale[:, None, :]) + shift[:, None, :]


def causal_dwconv(xpad, w):
    C = xpad.shape[-1]
    return lax.conv_general_dilated(
        xpad, w[:, None, :].astype(xpad.dtype), window_strides=(1,), padding='VALID',
        dimension_numbers=('NWC', 'WIO', 'NWC'), feature_group_count=C)


def swiglu(h, w_gu, w_down):
    g, u = jnp.split(h @ w_gu, 2, axis=-1)
    return (jax.nn.silu(g) * u) @ w_down


def multiscale_pool(u, prev, start):
    L = u.shape[1]
    z = jnp.concatenate([prev, u], axis=1).astype(jnp.float32)
    cs = jnp.pad(jnp.cumsum(z, axis=1), ((0, 0), (1, 0), (0, 0)))
    pos = start + jnp.arange(L)
    end = POOL_HIST + 1
    outs = []
    for g, w in enumerate(POOL_WINDOWS):
        sl = slice(g * POOL_GROUP, (g + 1) * POOL_GROUP)
        s = cs[:, end:end + L, sl] - cs[:, end - w:end - w + L, sl]
        cnt = jnp.minimum(w, pos + 1).astype(jnp.float32)
        outs.append(s / cnt[None, :, None])
    mean = jnp.concatenate(outs, axis=-1)
    return (mean - u.astype(jnp.float32)).astype(u.dtype)


def mixer_ab(h, prev_a, prev_b, start, w_in, conv_w, conv_b, ln_g, ln_b, w_group, scale, w_out):
    B, L, _ = h.shape
    p = h @ w_in
    a_u, a_g, b_u = jnp.split(p, [D_A, 2 * D_A], axis=-1)
    a = a_u * jax.nn.sigmoid(a_g)
    a_pad = jnp.concatenate([prev_a, a], axis=1)
    a = jax.nn.silu(layernorm(causal_dwconv(a_pad, conv_w) + conv_b, ln_g, ln_b))
    d = multiscale_pool(b_u, prev_b, start)
    bo = jnp.einsum('blgc,gcd->blgd', d.reshape(B, L, N_POOL_GROUPS, POOL_GROUP), w_group)
    bo = bo.reshape(B, L, D_B) * scale
    y = jnp.concatenate([a, bo], axis=-1) @ w_out
    new_a = a_pad[:, -(CONV_A_WIDTH - 1):]
    new_b = jnp.concatenate([prev_b, b_u], axis=1)[:, -POOL_HIST:]
    return y, new_a, new_b


def mixer_c(h, prev_c, w_in, conv_w, w_out):
    bg, cg, v = jnp.split(h @ w_in, 3, axis=-1)
    z = jnp.concatenate([prev_c, cg * v], axis=1)
    y = (bg * causal_dwconv(z, conv_w)) @ w_out
    return y, z[:, -(CONV_C_WIDTH - 1):]


def setup_inputs(seed: int = 0) -> dict:
    key = jax.random.key(seed)
    ks = jax.random.split(key, 26)
    nrm = jax.random.normal
    f32 = jnp.float32
    return {
        "x_prompt": nrm(ks[0], (BATCH, SEQ, D_MODEL), f32),
        "x_sample": nrm(ks[1], (DEC_BATCH, DEC_SEQ, D_MODEL), f32),
        "state_conv_a": 0.5 * nrm(ks[2], (N_EVEN, DEC_BATCH, CONV_A_WIDTH - 1, D_A), f32),
        "state_pool_b": nrm(ks[3], (N_EVEN, DEC_BATCH, POOL_HIST, D_B), f32),
        "state_conv_c": 0.5 * nrm(ks[4], (N_ODD, DEC_BATCH, CONV_C_WIDTH - 1, D_C), f32),
        "c_prompt": nrm(ks[5], (BATCH, D_MODEL), f32),
        "c_sample": nrm(ks[6], (DEC_BATCH, D_MODEL), f32),
        "ada_w": 0.5 * D_MODEL ** -0.5 * nrm(ks[7], (DEPTH, D_MODEL, N_MOD * D_MODEL), f32),
        "ada_b": 0.01 * nrm(ks[8], (DEPTH, N_MOD * D_MODEL), f32),
        "norm_g": 1.0 + 0.01 * nrm(ks[9], (DEPTH, 3, D_MODEL), f32),
        "ffn_w_gu": D_MODEL ** -0.5 * nrm(ks[10], (DEPTH, 2, D_MODEL, 2 * D_FF), f32),
        "ffn_w_down": D_FF ** -0.5 * nrm(ks[11], (DEPTH, 2, D_FF, D_MODEL), f32),
        "ab_w_in": D_MODEL ** -0.5 * nrm(ks[12], (N_EVEN, D_MODEL, 2 * D_A + D_B), f32),
        "a_conv_w": CONV_A_WIDTH ** -0.5 * nrm(ks[13], (N_EVEN, CONV_A_WIDTH, D_A), f32),
        "a_conv_b": 0.01 * nrm(ks[14], (N_EVEN, D_A), f32),
        "a_ln_g": 1.0 + 0.01 * nrm(ks[15], (N_EVEN, D_A), f32),
        "a_ln_b": 0.01 * nrm(ks[16], (N_EVEN, D_A), f32),
        "b_w_group": POOL_GROUP ** -0.5 * nrm(ks[17], (N_EVEN, N_POOL_GROUPS, POOL_GROUP, POOL_GROUP), f32),
        "b_scale": 1.0 + 0.01 * nrm(ks[18], (N_EVEN, D_B), f32),
        "ab_w_out": (D_A + D_B) ** -0.5 * nrm(ks[19], (N_EVEN, D_A + D_B, D_MODEL), f32),
        "c_w_in": D_MODEL ** -0.5 * nrm(ks[20], (N_ODD, D_MODEL, 3 * D_C), f32),
        "c_conv_w": CONV_C_WIDTH ** -0.5 * nrm(ks[21], (N_ODD, CONV_C_WIDTH, D_C), f32),
        "c_w_out": D_C ** -0.5 * nrm(ks[22], (N_ODD, D_C, D_MODEL), f32),
        "final_g": 1.0 + 0.01 * nrm(ks[23], (D_MODEL,), f32),
    }


def reference(x_prompt, x_sample, state_conv_a, state_pool_b, state_conv_c, c_prompt, c_sample,
              ada_w, ada_b, norm_g, ffn_w_gu, ffn_w_down, ab_w_in, a_conv_w, a_conv_b, a_ln_g,
              a_ln_b, b_w_group, b_scale, ab_w_out, c_w_in, c_conv_w, c_w_out, final_g):

    def run(x, c, prev_a, prev_b, prev_c, start):
        new_a, new_b, new_c = [], [], []
        c_act = jax.nn.silu(c)
        for l in range(DEPTH):
            mod = c_act @ ada_w[l] + ada_b[l]
            sh1, sc1, g1, sh2, sc2, g2, sh3, sc3, g3 = jnp.split(mod, N_MOD, axis=-1)
            h = modulate(rmsnorm(x, norm_g[l, 0]), sh1, sc1)
            x = x + FFN_RES * g1[:, None, :] * swiglu(h, ffn_w_gu[l, 0], ffn_w_down[l, 0])
            h = modulate(rmsnorm(x, norm_g[l, 1]), sh2, sc2)
            if l % 2 == 0:
                e = l // 2
                y, na, nb = mixer_ab(h, prev_a[e], prev_b[e], start, ab_w_in[e], a_conv_w[e],
                                     a_conv_b[e], a_ln_g[e], a_ln_b[e], b_w_group[e],
                                     b_scale[e], ab_w_out[e])
                new_a.append(na)
                new_b.append(nb)
            else:
                o = l // 2
                y, nc = mixer_c(h, prev_c[o], c_w_in[o], c_conv_w[o], c_w_out[o])
                new_c.append(nc)
            x = x + g2[:, None, :] * y
            h = modulate(rmsnorm(x, norm_g[l, 2]), sh3, sc3)
            x = x + FFN_RES * g3[:, None, :] * swiglu(h, ffn_w_gu[l, 1], ffn_w_down[l, 1])
        return rmsnorm(x, final_g), jnp.stack(new_a), jnp.stack(new_b), jnp.stack(new_c)

    nb_p = x_prompt.shape[0]
    dt = x_prompt.dtype
    zero_a = jnp.zeros((N_EVEN, nb_p, CONV_A_WIDTH - 1, D_A), dt)
    zero_b = jnp.zeros((N_EVEN, nb_p, POOL_HIST, D_B), dt)
    zero_c = jnp.zeros((N_ODD, nb_p, CONV_C_WIDTH - 1, D_C), dt)
    y_prompt, pa, pb, pc = run(x_prompt, c_prompt, zero_a, zero_b, zero_c, 0)
    y_sample, sa, sb, sc = run(x_sample, c_sample, state_conv_a, state_pool_b, state_conv_c, PAST_LEN)
    return (y_prompt, y_sample, pa, pb, pc, sa, sb, sc)
```

```python
import numpy as np
from contextlib import ExitStack
import concourse.bass as bass
import concourse.mybir as mybir
from concourse.bass_utils import run_bass_kernel_spmd

F32 = mybir.dt.float32
BF16 = mybir.dt.bfloat16
AF = mybir.ActivationFunctionType
ALU = mybir.AluOpType

NCORES = 8
D = 1024
KC = 8
DFF = 2816
NJ = 22
DEPTH = 4
HALO = 64
MAIN = 2048
LS = 32
T = HALO + MAIN + 2 * LS
SEGS = [(0, HALO + MAIN), (HALO + MAIN, HALO + MAIN + LS), (HALO + MAIN + LS, T)]
TILES = [(0, 384), (384, 768), (768, 1152), (1152, 1536), (1536, 1856), (1856, 2176)]
NMAX = 384
EPS = 1e-6
NSLOT = 6
SLOTW = 2048
ACTK = 4
POOL_EVAC = True
POOL_EVAC_MOD = 2

_off = {}
_o = 0
for _n, _w in [("normg", 96), ("finalg", 8), ("adab", 288), ("aconvw", 248), ("aconvb", 8), ("alng", 8),
               ("alnb", 8), ("bscale", 8), ("cconvw", 48), ("hmask", 64), ("invcnt", 64), ("epsc", 1)]:
    _off[_n] = _o
    _o += _w
NPRM = _o


def pieces(t0, t1):
    out = []
    for s, (a, b) in enumerate(SEGS):
        lo, hi = max(a, t0), min(b, t1)
        if lo < hi:
            out.append((s, lo, hi))
    return out


class _FakeIns:
    def then_inc(self, *a, **k):
        return self


class _FakeEng:
    def __init__(self):
        self.cost = 0.0
        self.tag = None
        self.nbytes = 0

    @staticmethod
    def _free(ap):
        n = 1
        for d in ap.shape[1:]:
            n *= int(d)
        return n

    def matmul(self, out, lhsT=None, rhs=None, **k):
        c = max(self._free(out) / 2400.0 + 0.004, 0.035)
        if lhsT is not None and lhsT.dtype == F32:
            c *= 2.2
        self.cost += c
        return _FakeIns()

    def activation(self, out=None, in_=None, func=None, **k):
        self.cost += 0.22 + self._free(out) * 0.001
        if func in (AF.Silu, AF.Tanh):
            self.tag = "A"
        elif func == AF.Sqrt:
            self.tag = "B"
        return _FakeIns()

    def reciprocal(self, out=None, in_=None, **k):
        self.cost += 0.06 + self._free(out) * 0.0064
        return _FakeIns()

    def scalar_tensor_tensor(self, out=None, **k):
        self.cost += 0.06 + self._free(out) * 0.00146
        return _FakeIns()

    def dma_start(self, out=None, in_=None, **k):
        n = 1
        for d in out.shape:
            n *= int(d)
        self.nbytes += n * (4 if in_.dtype == F32 else 2)
        return _FakeIns()

    def __getattr__(self, name):
        def generic(*a, **k):
            out = k.get("out", a[0] if a else None)
            self.cost += 0.06 + (self._free(out) * 0.0012 if out is not None else 0.0)
            return _FakeIns()
        return generic


class Prog:
    ENGS = ("pe", "act", "dve", "pool", "sp")

    def __init__(self, nc, ctx):
        self.nc = nc
        self.ctx = ctx
        self.semh = {}
        for n in ("pe", "act", "dve", "pool"):
            self.semh[n] = ctx.enter_context(nc.semaphore("s_" + n))
        self.ops = []
        self.lastw = {}
        self.readers = {}
        self.final_sems = []
        self.order = None

    def dma_sem(self, name):
        if name not in self.semh:
            self.semh[name] = self.ctx.enter_context(self.nc.semaphore("d_" + name))
        return name

    def _add(self, eng, fn, sem, inc, reads, writes, dur, xfer, tag):
        idx = len(self.ops)
        deps = set()
        for k in reads:
            w = self.lastw.get(k)
            if w is not None:
                deps.add(w)
        for k in writes:
            w = self.lastw.get(k)
            if w is not None:
                deps.add(w)
            deps.update(self.readers.get(k, ()))
        deps.discard(idx)
        self.ops.append({"eng": eng, "fn": fn, "sem": sem, "inc": inc, "deps": deps, "dur": dur, "xfer": xfer, "tag": tag})
        for k in reads:
            self.readers.setdefault(k, []).append(idx)
        for k in writes:
            self.lastw[k] = idx
            self.readers[k] = []
        return idx

    def op(self, eng, fn, reads=(), writes=()):
        fe = _FakeEng()
        fn(fe)
        cost = fe.cost * (1.6 if eng == "pool" else 1.0)
        return self._add(eng, fn, eng, 1, reads, writes, cost, None, fe.tag)

    def dma(self, queue, semname, fn, reads=(), writes=()):
        self.dma_sem(semname)
        fe = _FakeEng()
        fn(fe)
        issue = 1.0 if queue == "pool" else 0.15
        return self._add(queue, fn, semname, 16, reads, writes, issue, fe.nbytes, None)

    def final_wait(self, queue, semnames):
        self.final_sems = [(queue, s) for s in semnames]

    def finalize(self):
        import heapq
        ops = self.ops
        n = len(ops)
        nd = [len(o["deps"]) for o in ops]
        users = [[] for _ in range(n)]
        for i, o in enumerate(ops):
            for d in o["deps"]:
                users[d].append(i)
        LAT = 0.2
        done_t = [0.0] * n
        start_t = [0.0] * n
        free = {e: 0.0 for e in self.ENGS}
        pend = {e: [] for e in self.ENGS}
        avail = {e: [] for e in self.ENGS}
        act_set = [None]
        dma_pipe = [0.0]
        order = {e: [] for e in self.ENGS}
        for i, o in enumerate(ops):
            if nd[i] == 0:
                heapq.heappush(pend[o["eng"]], (0.0, i))
        left = n
        while left:
            best = None
            for e in self.ENGS:
                t = free[e]
                while pend[e] and pend[e][0][0] <= t:
                    heapq.heappush(avail[e], heapq.heappop(pend[e])[1])
                if avail[e]:
                    cand = (t, avail[e][0], e, True)
                elif pend[e]:
                    cand = (pend[e][0][0], pend[e][0][1], e, False)
                else:
                    continue
                if best is None or cand[:2] < best[:2]:
                    best = cand
            st_, i, e, from_avail = best
            if from_avail:
                heapq.heappop(avail[e])
            else:
                heapq.heappop(pend[e])
            o = ops[i]
            start_t[i] = st_
            dur = o["dur"]
            if e == "act" and o["tag"] is not None:
                if act_set[0] is not None and act_set[0] != o["tag"]:
                    dur += 1.8
                act_set[0] = o["tag"]
            if o["xfer"] is not None:
                free[e] = st_ + dur
                xs = max(st_ + dur, dma_pipe[0])
                dma_pipe[0] = xs + o["xfer"] / 200e3
                done_t[i] = dma_pipe[0] + 2.0
            else:
                free[e] = st_ + dur
                done_t[i] = st_ + dur
            order[e].append(i)
            left -= 1
            for u in users[i]:
                nd[u] -= 1
                if nd[u] == 0:
                    rt = max(done_t[d] for d in ops[u]["deps"]) + LAT
                    heapq.heappush(pend[ops[u]["eng"]], (rt, u))
        self.order = order
        self.start_t = start_t
        self.done_t = done_t
        self.sim_time = max(done_t) if n else 0.0
        cnt = {}
        self.val = [0] * n
        for e in self.ENGS:
            for i in order[e]:
                sname = ops[i]["sem"]
                cnt[sname] = cnt.get(sname, 0) + ops[i]["inc"]
                self.val[i] = cnt[sname]
        self.cnt = cnt
        return self.sim_time

    def run(self, eng_name, eng):
        ops = self.ops
        waited = {}
        for i in self.order[eng_name]:
            o = ops[i]
            need = {}
            for d in o["deps"]:
                sd = ops[d]["sem"]
                v = self.val[d]
                if need.get(sd, 0) < v:
                    need[sd] = v
            for sd, v in need.items():
                if waited.get(sd, 0) < v:
                    waited[sd] = v
                    eng.wait_ge(self.semh[sd], v)
            ins = o["fn"](eng)
            ins.then_inc(self.semh[o["sem"]], o["inc"])
        for (q, sname) in self.final_sems:
            if q == eng_name and self.cnt.get(sname, 0) > 0:
                eng.wait_ge(self.semh[sname], self.cnt[sname])


_SIM = {}


def build_nc():
    nc = bass.Bass("TRN2", target_bir_lowering=False)
    ctx = ExitStack()
    with ctx:
        _build(nc, ctx)
    return nc


def _build(nc, ctx):
    def din(name, shape):
        return nc.dram_tensor(name, list(shape), F32, kind="ExternalInput").ap()

    def dout(name, shape):
        return nc.dram_tensor(name, list(shape), F32, kind="ExternalOutput").ap()

    xT_d = din("xT", [128, KC, T])
    cv_d = din("cvec", [128, KC, 3])
    prm_d = din("prm", [128, NPRM])
    sa_d = din("sa", [128, 2 * 2 * 4 * 30])
    sb_d = din("sb", [128, 2 * 2 * 4 * 15])
    sc_d = din("scn", [128, 2 * 2 * 8 * 2])
    id_d = din("ident", [128, 128])
    ada_w = din("ada_w", [DEPTH, D, 9 * D])
    w_gu = din("ffn_w_gu", [DEPTH, 2, D, 2 * DFF])
    w_dn = din("ffn_w_down", [DEPTH, 2, DFF, D])
    ab_in = din("ab_w_in", [2, D, 1536])
    b_wg = din("b_w_group", [2, 4, 128, 128])
    ab_out = din("ab_w_out", [2, D, D])
    c_in = din("c_w_in", [2, D, 3 * D])
    c_out = din("c_w_out", [2, D, D])
    yT_d = dout("yT", [128, KC, MAIN + 2 * LS])
    oa_d = dout("oa", [128, 2, 3, 4, 30])
    ob_d = dout("ob", [128, 2, 3, 4, 15])
    oc_d = dout("oc", [128, 2, 3, 8, 2])
    dgd = nc.dram_tensor("dgd", [2, 4, 2, 128, 16 * 128], BF16, kind="Internal").ap()

    def sb_(name, shape, dt=F32):
        return ctx.enter_context(nc.sbuf_tensor(name, list(shape), dt))

    NT = len(TILES)
    x = sb_("x", [128, KC, T])
    h = sb_("h", [128, KC, T], BF16)
    act = sb_("act", [128, ACTK, T], BF16)
    slots = sb_("slots", [128, NSLOT, 16, 128], BF16)
    sqb = sb_("sqb", [128, 2, KC, NMAX], BF16)
    rbuf = sb_("rbuf", [128, 2, NMAX])
    NTMP = 6
    TMPW = 400
    tmp = sb_("tmp", [128, NTMP, TMPW])
    abf = sb_("abf", [128, 4, 30 + 384], BF16)
    astg = sb_("astg", [128, 3, 4, 30])
    dg = sb_("dg", [128, 31, 128], BF16)
    cob = sb_("cob", [128, 4, 384])
    lnm = sb_("lnm", [128, 2, 384])
    bub = sb_("bub", [128, 2, 45 + 384])
    dbf = sb_("dbf", [128, 2, 384], BF16)
    ztl = sb_("ztl", [128, 2, 6 + 384])
    prm = sb_("prm_s", [128, NPRM])
    cv = sb_("cv_s", [128, KC, 3])
    cact = sb_("cact", [128, KC, 3], BF16)
    sa = sb_("sa_s", [128, 2, 2, 4, 30])
    sbs = sb_("sb_s", [128, 2, 2, 4, 15])
    scs = sb_("sc_s", [128, 2, 2, 8, 2])
    modfm = sb_("modfm", [128, 2, 72, 3])
    gsb = sb_("gsb", [128, 2, 3, KC, 3])
    ghb = sb_("ghb", [128, 2, 3, KC, 3])
    ones_bf = sb_("ones_bf", [128, 128], BF16)
    ones_f = sb_("ones_f", [128, 128])
    ident_bf = sb_("ident_bf", [128, 128], BF16)
    awh = sb_("awh", [128, 124])
    ps = [ctx.enter_context(nc.psum_tensor("ps%d" % i, [128, 512], F32)) for i in range(8)]

    pg = Prog(nc, ctx)
    st = {"bank": 0, "slot": 0, "tmp": 0, "sq": 0, "r": 0, "dbf": 0, "bub": 0}

    def bank():
        b = st["bank"]
        st["bank"] = (b + 1) % 8
        return b

    def tmpbuf():
        i = st["tmp"]
        st["tmp"] = (i + 1) % NTMP
        return i

    def P(name, w=1, i=0):
        o = _off[name] + i
        return prm[:, o:o + w]

    held = set()

    def alloc_slot():
        for _ in range(NSLOT):
            s = st["slot"]
            st["slot"] = (s + 1) % NSLOT
            if s not in held:
                return s
        raise RuntimeError("no free weight slot")

    def load_piece(parts):
        s = alloc_slot()
        for (b0, nb, src) in parts:
            def f(e, s=s, b0=b0, nb=nb, src=src):
                dst = slots[:, s, b0:b0 + nb, :]
                if len(src.shape) == 3 and src.shape[2] == 1024:
                    dst = dst.rearrange("p (j m) c -> p j (m c)", m=8)
                return e.dma_start(out=dst, in_=src)
            pg.dma("pool", "slot%d" % s, f, reads=(), writes=(("slot", s),))
        return s

    def wcols(w2d, c0, n=128):
        return w2d.rearrange("(kc p) n -> p kc n", p=128)[:, :, c0:c0 + n]

    def wrows(w2d, r0, nr):
        return w2d.rearrange("(j p) n -> p j n", p=128)[:, r0:r0 + nr, :]

    pe_defer = []

    def pe_tick():
        for d in list(pe_defer):
            d[0] -= 1
            if d[0] <= 0:
                pe_defer.remove(d)
                d[1]()

    def pe_flush():
        while pe_defer:
            d = pe_defer.pop(0)
            d[1]()

    mod_pending = []

    def mod_piece(l, i):
        src = ada_w[l].rearrange("(kc p) n -> p kc n", p=128)[:, :, i * 256:(i + 1) * 256]
        s = alloc_slot()

        def f(e, s=s, src=src):
            return e.dma_start(out=slots[:, s, :, :].rearrange("p (k a) c -> p k (a c)", a=2), in_=src)
        pg.dma("pool", "slot%d" % s, f, writes=(("slot", s),))
        b = bank()

        def mm(e, s=s, b=b):
            last = None
            for ql in range(2):
                for k in range(KC):
                    last = e.matmul(ps[b][:, ql * 3:ql * 3 + 3], lhsT=slots[:, s, k * 2 + ql, :],
                                    rhs=cact[:, k, :], start=(k == 0), stop=(k == KC - 1))
            return last
        pg.op("pe", mm, reads=(("slot", s), "cact"), writes=(("ps", b),))
        for ql in range(2):
            q = 2 * i + ql

            def ev(e, b=b, ql=ql, q=q, l=l):
                return e.activation(out=modfm[:, l % 2, q, :], in_=ps[b][:, ql * 3:ql * 3 + 3],
                                    func=AF.Identity, bias=P("adab", 1, l * 72 + q), scale=1.0)
            pg.op("act", ev, reads=(("ps", b), "prm"), writes=(("mod", l % 2, q // 8),))

    def mod_derive(l, s_):
        lb = l % 2
        for m in range(KC):
            def f(e, m=m):
                o = _off["normg"] + (l * 3 + s_) * 8 + m
                return e.tensor_scalar(out=gsb[:, lb, s_, m, :], in0=modfm[:, lb, (3 * s_ + 1) * 8 + m, :],
                                       scalar1=1.0, scalar2=prm[:, o:o + 1], op0=ALU.add, op1=ALU.mult)
            pg.op("dve", f, reads=(("mod", lb, 3 * s_ + 1), "prm"), writes=(("gs", lb, s_),))

        def f2(e):
            return e.tensor_scalar(out=ghb[:, lb, s_, :, :], in0=modfm[:, lb, (3 * s_ + 2) * 8:(3 * s_ + 3) * 8, :],
                                   scalar1=(1.0 if s_ == 1 else 0.5), scalar2=None, op0=ALU.mult)
        pg.op("dve", f2, reads=(("mod", lb, 3 * s_ + 2),), writes=(("gh", lb, s_),))

    def pump_mod(n=1):
        for _ in range(n):
            if mod_pending:
                l, i = mod_pending.pop(0)
                mod_piece(l, i)
                if i % 12 == 11:
                    mod_derive(l, i // 12)

    def norm_tile(l, s_, ti, final=False, defer=0):
        t0, t1 = TILES[ti]
        N = t1 - t0
        lb = l % 2
        sq = st["sq"]
        st["sq"] = 1 - sq
        for m in range(KC):
            def f(e, m=m):
                return e.activation(out=sqb[:, sq, m, 0:N], in_=x[:, m, t0:t1], func=AF.Square)
            pg.op("act", f, reads=(("x", m, ti),), writes=(("sqb", sq),))

        def rest():
            b = bank()
            ri = st["r"]
            st["r"] = 1 - ri

            def mm(e):
                last = None
                for m in range(KC):
                    last = e.matmul(ps[b][:, 0:N], lhsT=ones_bf[:, :], rhs=sqb[:, sq, m, 0:N],
                                    start=(m == 0), stop=(m == KC - 1))
                return last
            pg.op("pe", mm, reads=(("sqb", sq), "ones"), writes=(("ps", b),))

            def fr0(e):
                return e.activation(out=rbuf[:, ri, 0:N], in_=ps[b][:, 0:N], func=AF.Sqrt, bias=P("epsc", 1), scale=1.0 / D)
            pg.op("act", fr0, reads=(("ps", b), "prm"), writes=(("r", ri),))

            def fr(e):
                return e.reciprocal(out=rbuf[:, ri, 0:N], in_=rbuf[:, ri, 0:N])
            pg.op("dve", fr, reads=(("r", ri),), writes=(("r", ri),))
            for m in range(KC):
                if final:
                    def fy(e, m=m):
                        o = _off["finalg"] + m
                        return e.scalar_tensor_tensor(out=x[:, m, t0:t1], in0=x[:, m, t0:t1], scalar=prm[:, o:o + 1],
                                                      in1=rbuf[:, ri, 0:N], op0=ALU.mult, op1=ALU.mult)
                    pg.op("dve", fy, reads=(("r", ri), "prm"), writes=(("x", m, ti),))
                    continue
                tb = tmpbuf()

                def ft(e, m=m, tb=tb):
                    return e.tensor_tensor(out=tmp[:, tb, 0:N], in0=x[:, m, t0:t1], in1=rbuf[:, ri, 0:N], op=ALU.mult)
                pg.op("dve", ft, reads=(("x", m, ti), ("r", ri)), writes=(("tmp", tb),))
                for (sg, lo, hi) in pieces(t0, t1):
                    def fh(e, m=m, tb=tb, sg=sg, lo=lo, hi=hi):
                        return e.activation(out=h[:, m, lo:hi], in_=tmp[:, tb, lo - t0:hi - t0], func=AF.Identity,
                                            bias=modfm[:, lb, (3 * s_) * 8 + m, sg:sg + 1],
                                            scale=gsb[:, lb, s_, m, sg:sg + 1])
                    pg.op("act", fh, reads=(("tmp", tb), ("gs", lb, s_), ("mod", lb, 3 * s_)), writes=(("h", m, ti),))
            if final:
                lo_ = max(t0, HALO)

                def fyo(e):
                    return e.dma_start(out=yT_d[:, :, lo_ - HALO:t1 - HALO], in_=x[:, :, lo_:t1])
                pg.dma("sp", "outs", fyo, reads=tuple(("x", m, ti) for m in range(KC)))
        if defer > 0:
            pe_defer.append([defer, rest])
        else:
            rest()

    def out_phase(l, s_, wsrc2d, r0, nk, pre_loop=None, after_tile=None):
        lb = l % 2
        sl = []
        k = 0
        while k < nk:
            n = min(2, nk - k)
            sl.append((load_piece([(0, n * 8, wrows(wsrc2d, r0 + k, n))]), k, n))
            k += n
        if pre_loop is not None:
            pre_loop()
        for ti, (t0, t1) in enumerate(TILES):
            N = t1 - t0
            for m in range(KC):
                b = bank()

                def mm(e, m=m, b=b, t0=t0, t1=t1, N=N):
                    last = None
                    kk = 0
                    for (s, k0, n) in sl:
                        for j in range(n):
                            last = e.matmul(ps[b][:, 0:N], lhsT=slots[:, s, j * 8 + m, :], rhs=act[:, k0 + j, t0:t1],
                                            start=(kk == 0), stop=(kk == nk - 1))
                            kk += 1
                    return last
                pg.op("pe", mm, reads=tuple(("slot", s) for (s, _, _) in sl) + tuple(("act", k_, ti) for k_ in range(nk)),
                      writes=(("ps", b),))
                pe_tick()
                pcs_ = pieces(t0, t1)
                if POOL_EVAC and after_tile is None and len(pcs_) == 1 and (m % POOL_EVAC_MOD) == POOL_EVAC_MOD - 1:
                    (sg, lo, hi) = pcs_[0]
                    tb = tmpbuf()

                    def fxa(e, m=m, b=b, sg=sg, tb=tb, N=N):
                        return e.activation(out=tmp[:, tb, 0:N], in_=ps[b][:, 0:N], func=AF.Identity,
                                            scale=ghb[:, lb, s_, m, sg:sg + 1])
                    pg.op("act", fxa, reads=(("ps", b), ("gh", lb, s_)), writes=(("tmp", tb),))

                    def fxp(e, m=m, tb=tb, lo=lo, hi=hi, N=N):
                        return e.tensor_tensor(out=x[:, m, lo:hi], in0=x[:, m, lo:hi], in1=tmp[:, tb, 0:N], op=ALU.add)
                    pg.op("pool", fxp, reads=(("tmp", tb),), writes=(("x", m, ti),))
                    continue
                for (sg, lo, hi) in pcs_:
                    def fx(e, m=m, b=b, sg=sg, lo=lo, hi=hi, t0=t0):
                        return e.scalar_tensor_tensor(out=x[:, m, lo:hi], in0=ps[b][:, lo - t0:hi - t0],
                                                      scalar=ghb[:, lb, s_, m, sg:sg + 1], in1=x[:, m, lo:hi],
                                                      op0=ALU.mult, op1=ALU.add)
                    pg.op("dve", fx, reads=(("ps", b), ("gh", lb, s_)), writes=(("x", m, ti),))
            if after_tile is not None:
                after_tile(ti)

    FFN_PARTS = [(0, 4), (4, 4), (8, 4), (12, 4), (16, 3), (19, 3)]

    def ffn_chunk_tile(s, jl, ti):
        t0, t1 = TILES[ti]
        N = t1 - t0
        b1, b2 = bank(), bank()

        def mm(e):
            last = None
            for k in range(KC):
                e.matmul(ps[b1][:, 0:N], lhsT=slots[:, s, k, :], rhs=h[:, k, t0:t1], start=(k == 0), stop=(k == KC - 1))
            for k in range(KC):
                last = e.matmul(ps[b2][:, 0:N], lhsT=slots[:, s, 8 + k, :], rhs=h[:, k, t0:t1],
                                start=(k == 0), stop=(k == KC - 1))
            return last
        pg.op("pe", mm, reads=(("slot", s),) + tuple(("h", k, ti) for k in range(KC)), writes=(("ps", b1), ("ps", b2)))
        pe_tick()
        tb = tmpbuf()

        def fs(e):
            return e.activation(out=tmp[:, tb, 0:N], in_=ps[b1][:, 0:N], func=AF.Silu)
        pg.op("act", fs, reads=(("ps", b1),), writes=(("tmp", tb),))

        def fm(e):
            return e.tensor_tensor(out=act[:, jl, t0:t1], in0=tmp[:, tb, 0:N], in1=ps[b2][:, 0:N], op=ALU.mult)
        pg.op("dve", fm, reads=(("tmp", tb), ("ps", b2)), writes=(("act", jl, ti),))

    def ffn_piece(S, j):
        wg = w_gu[S["l"], S["f"]]
        return load_piece([(0, 8, wcols(wg, j * 128)), (8, 8, wcols(wg, DFF + j * 128))])

    def ffn_preload(S):
        S["fs"] = [ffn_piece(S, j) for j in range(4)]
        held.update(S["fs"])

    def ffn_first_tile(S, ti):
        for jl in range(4):
            ffn_chunk_tile(S["fs"][jl], jl, ti)

    def ffn_rest(S, boundary):
        l, s_ = S["l"], S["s"]
        wd = w_dn[l, S["f"]]
        out_phase(l, s_, wd, 0, 4)
        for pi, (j0, nj) in enumerate(FFN_PARTS):
            if pi == 0:
                continue
            for jl in range(nj):
                s = ffn_piece(S, j0 + jl)
                for ti in range(NT):
                    ffn_chunk_tile(s, jl, ti)
                pump_mod(2)
            if pi == len(FFN_PARTS) - 1:
                out_phase(l, s_, wd, j0, nj, pre_loop=boundary[0], after_tile=boundary[1])
            else:
                out_phase(l, s_, wd, j0, nj)

    def c_chunk_tile(S, c, cl, zi, s1, s2, ti):
        o = S["l"] // 2
        t0, t1 = TILES[ti]
        N = t1 - t0
        pcs = pieces(t0, t1)
        bb, bc, bv = bank(), bank(), bank()
        zk = ("ztl", zi)

        def mm(e):
            last = None
            for (bk, s, o8) in ((bb, s1, 0), (bc, s1, 8), (bv, s2, 0)):
                for k in range(KC):
                    last = e.matmul(ps[bk][:, 0:N], lhsT=slots[:, s, o8 + k, :], rhs=h[:, k, t0:t1],
                                    start=(k == 0), stop=(k == KC - 1))
            return last
        pg.op("pe", mm, reads=(("slot", s1), ("slot", s2)) + tuple(("h", k, ti) for k in range(KC)),
              writes=(("ps", bb), ("ps", bc), ("ps", bv)))
        pe_tick()
        tv = tmpbuf()

        def fv(e):
            return e.activation(out=tmp[:, tv, 0:N], in_=ps[bv][:, 0:N], func=AF.Copy)
        pg.op("act", fv, reads=(("ps", bv),), writes=(("tmp", tv),))
        offs = []
        off = 0
        for (sg, lo, hi) in pcs:
            off += 2
            offs.append(off)
            if sg == 0:
                if ti == 0:
                    pg.op("dve", lambda e: e.memset(ztl[:, zi, 0:2], 0.0), writes=(zk,))
                else:
                    pN = TILES[ti - 1][1] - TILES[ti - 1][0]
                    pg.op("dve", lambda e, pN=pN: e.tensor_copy(out=ztl[:, zi, 0:2], in_=ztl[:, zi, pN:pN + 2]),
                          reads=(zk,), writes=(zk,))
            else:
                pg.op("dve", lambda e, off=off, sg=sg: e.tensor_copy(out=ztl[:, zi, off - 2:off], in_=scs[:, o, sg - 1, c, :]),
                      reads=("st_c",), writes=(zk,))
            off += hi - lo
        tot = off
        for (sg, lo, hi), of in zip(pcs, offs):
            def fzz(e, lo=lo, hi=hi, of=of):
                return e.tensor_tensor(out=ztl[:, zi, of:of + hi - lo], in0=ps[bc][:, lo - t0:hi - t0],
                                       in1=tmp[:, tv, lo - t0:hi - t0], op=ALU.mult)
            pg.op("dve", fzz, reads=(("ps", bc), ("tmp", tv)), writes=(zk,))
        if ti == 0:
            pg.op("dve", lambda e: e.tensor_tensor(out=ztl[:, zi, 2:2 + HALO], in0=ztl[:, zi, 2:2 + HALO],
                                                   in1=P("hmask", HALO), op=ALU.mult), reads=("prm",), writes=(zk,))
        Wb = tot - 2
        ta, tb_ = tmpbuf(), tmpbuf()
        cw = _off["cconvw"] + (o * 8 + c) * 3

        def f0(e):
            return e.activation(out=tmp[:, ta, 0:Wb], in_=ztl[:, zi, 0:Wb], func=AF.Identity, scale=prm[:, cw:cw + 1])
        pg.op("act", f0, reads=(zk, "prm"), writes=(("tmp", ta),))

        def f1(e):
            return e.scalar_tensor_tensor(out=tmp[:, tb_, 0:Wb], in0=ztl[:, zi, 1:1 + Wb], scalar=prm[:, cw + 1:cw + 2],
                                          in1=tmp[:, ta, 0:Wb], op0=ALU.mult, op1=ALU.add)
        pg.op("dve", f1, reads=(zk, ("tmp", ta)), writes=(("tmp", tb_),))

        def f2(e):
            return e.scalar_tensor_tensor(out=tmp[:, ta, 0:Wb], in0=ztl[:, zi, 2:2 + Wb], scalar=prm[:, cw + 2:cw + 3],
                                          in1=tmp[:, tb_, 0:Wb], op0=ALU.mult, op1=ALU.add)
        pg.op("dve", f2, reads=(zk, ("tmp", tb_)), writes=(("tmp", ta),))
        tbg = tmpbuf()
        pg.op("act", lambda e: e.activation(out=tmp[:, tbg, 0:N], in_=ps[bb][:, 0:N], func=AF.Copy),
              reads=(("ps", bb),), writes=(("tmp", tbg),))
        for (sg, lo, hi), of in zip(pcs, offs):
            def fo(e, lo=lo, hi=hi, of=of):
                return e.tensor_tensor(out=act[:, cl, lo:hi], in0=tmp[:, ta, of - 2:of - 2 + hi - lo],
                                       in1=tmp[:, tbg, lo - t0:hi - t0], op=ALU.mult)
            pg.op("dve", fo, reads=(("tmp", ta), ("tmp", tbg)), writes=(("act", cl, ti),))
        if ti == NT - 1:
            for (sg, lo, hi), of in zip(pcs, offs):
                e0 = of + (hi - lo) - 2

                def fd(e, sg=sg, e0=e0):
                    return e.dma_start(out=oc_d[:, o, sg, c, :], in_=ztl[:, zi, e0:e0 + 2])
                pg.dma("sp", "outs", fd, reads=(zk,))

    def c_pieces(S, c):
        win = c_in[S["l"] // 2]
        s1 = load_piece([(0, 8, wcols(win, c * 128)), (8, 8, wcols(win, D + c * 128))])
        s2 = load_piece([(0, 8, wcols(win, 2 * D + c * 128))])
        return s1, s2

    def c_preload(S):
        S["fs"] = [c_pieces(S, 0), c_pieces(S, 1)]
        held.update(S["fs"][0] + S["fs"][1])

    def c_first_tile(S, ti):
        for c in range(2):
            c_chunk_tile(S, c, c, c, S["fs"][c][0], S["fs"][c][1], ti)

    def c_rest(S, boundary):
        l = S["l"]
        o = l // 2
        for hf in range(2):
            for cl in range(4):
                c = hf * 4 + cl
                if hf == 0 and cl < 2:
                    continue
                s1, s2 = c_pieces(S, c)
                for ti in range(NT):
                    c_chunk_tile(S, c, cl, 0, s1, s2, ti)
                pump_mod(2)
            if hf == 1:
                out_phase(l, 1, c_out[o], hf * 4, 4, pre_loop=boundary[0], after_tile=boundary[1])
            else:
                out_phase(l, 1, c_out[o], hf * 4, 4)

    def ab_preload(S):
        win = ab_in[S["l"] // 2]
        S["fs"] = [load_piece([(0, 8, wcols(win, c * 128)), (8, 8, wcols(win, 512 + c * 128))]) for c in range(4)]
        held.update(S["fs"])

    def ab_first_tile(S, ti):
        e_ = S["l"] // 2
        sA = S["fs"]
        t0, t1 = TILES[ti]
        N = t1 - t0
        pcs = pieces(t0, t1)
        offs = []
        off = 0
        for (sg, lo, hi) in pcs:
            off += 30
            offs.append(off)
            off += hi - lo
        tot = off
        Wb = tot - 30

        def inproj_glu(c):
            bu_, bg_ = bank(), bank()

            def mm(e, s=sA[c]):
                last = None
                for (bk, o8) in ((bu_, 0), (bg_, 8)):
                    for k in range(KC):
                        last = e.matmul(ps[bk][:, 0:N], lhsT=slots[:, s, o8 + k, :], rhs=h[:, k, t0:t1],
                                        start=(k == 0), stop=(k == KC - 1))
                return last
            pg.op("pe", mm, reads=(("slot", sA[c]),) + tuple(("h", k, ti) for k in range(KC)),
                  writes=(("ps", bu_), ("ps", bg_)))
            pe_tick()
            tsg = tmpbuf()
            pg.op("act", lambda e: e.activation(out=tmp[:, tsg, 0:N], in_=ps[bg_][:, 0:N], func=AF.Tanh, scale=0.5),
                  reads=(("ps", bg_),), writes=(("tmp", tsg),))
            ak = ("abf", c)
            for (sg, lo, hi), of in zip(pcs, offs):
                if sg == 0:
                    if ti == 0:
                        pg.op("dve", lambda e: e.memset(abf[:, c, 0:30], 0.0), writes=(ak,))
                    else:
                        pN = TILES[ti - 1][1] - TILES[ti - 1][0]
                        pg.op("dve", lambda e, pN=pN: e.tensor_copy(out=abf[:, c, 0:30], in_=abf[:, c, pN:pN + 30]),
                              reads=(ak,), writes=(ak,))
                else:
                    pg.op("dve", lambda e, of=of, sg=sg: e.tensor_scalar(out=abf[:, c, of - 30:of], in0=sa[:, e_, sg - 1, c, :],
                                                                         scalar1=2.0, scalar2=None, op0=ALU.mult),
                          reads=("st_a",), writes=(ak,))

                def fa(e, lo=lo, hi=hi, of=of):
                    return e.scalar_tensor_tensor(out=abf[:, c, of:of + hi - lo], in0=tmp[:, tsg, lo - t0:hi - t0], scalar=1.0,
                                                  in1=ps[bu_][:, lo - t0:hi - t0], op0=ALU.add, op1=ALU.mult)
                pg.op("dve", fa, reads=(("ps", bu_), ("tmp", tsg)), writes=(ak,))
                if ti == NT - 1:
                    def fst(e, hi=hi, sg=sg):
                        return e.scalar_tensor_tensor(out=astg[:, sg, c, :], in0=tmp[:, tsg, hi - 30 - t0:hi - t0], scalar=1.0,
                                                      in1=ps[bu_][:, hi - 30 - t0:hi - t0], op0=ALU.add, op1=ALU.mult)
                    pg.op("dve", fst, reads=(("ps", bu_), ("tmp", tsg)), writes=("astg",))
                    pg.op("dve", lambda e, sg=sg: e.tensor_scalar(out=astg[:, sg, c, :], in0=astg[:, sg, c, :], scalar1=0.5,
                                                                  scalar2=None, op0=ALU.mult), reads=("astg",), writes=("astg",))
            if ti == 0:
                pg.op("dve", lambda e: e.tensor_tensor(out=abf[:, c, 30:30 + HALO], in0=abf[:, c, 30:30 + HALO],
                                                       in1=P("hmask", HALO), op=ALU.mult), reads=("prm",), writes=(ak,))

        def gen(c):
            for hf_, (k0, k1) in enumerate(((0, 16), (16, 31))):
                def fld(e, k0=k0, k1=k1, hf_=hf_):
                    n = k1 - k0
                    return e.dma_start(out=dg[:, k0:k1, :].rearrange("p k j -> p (k j)"), in_=dgd[e_, c, hf_, :, 0:n * 128])
                pg.dma("sp", "dgl%d" % hf_, fld, reads=(("dgd", e_, hf_),), writes=(("dg", hf_),))

        def conv(c):
            ak = ("abf", c)
            bcv = bank()

            def mc0(e):
                last = None
                for k in range(16):
                    last = e.matmul(ps[bcv][:, 0:Wb], lhsT=dg[:, k, :], rhs=abf[:, c, k:k + Wb], start=(k == 0), stop=False)
                return last
            pg.op("pe", mc0, reads=(("dg", 0), ak), writes=(("ps", bcv),))

            def mc1(e):
                last = None
                for k in range(16, 31):
                    last = e.matmul(ps[bcv][:, 0:Wb], lhsT=dg[:, k, :], rhs=abf[:, c, k:k + Wb], start=False, stop=(k == 30))
                return last
            pg.op("pe", mc1, reads=(("dg", 1), ak), writes=(("ps", bcv),))
            pe_tick()
            cbo = _off["aconvb"] + e_ * 4 + c
            pg.op("act", lambda e: e.activation(out=cob[:, c, 0:Wb], in_=ps[bcv][:, 0:Wb], func=AF.Identity,
                                                bias=prm[:, cbo:cbo + 1], scale=1.0),
                  reads=(("ps", bcv), "prm"), writes=(("cob", c),))

        inproj_glu(0)
        gen(0)
        for c in range(4):
            if c + 1 < 4:
                inproj_glu(c + 1)
            conv(c)
            if c + 1 < 4:
                gen(c + 1)
        if ti == NT - 1:
            for (sg, lo, hi) in pcs:
                def fd(e, sg=sg):
                    return e.dma_start(out=oa_d[:, e_, sg, :, :], in_=astg[:, sg, :, :])
                pg.dma("sp", "outs", fd, reads=("astg",))

        def ln_tail():
            b1, b2 = bank(), bank()

            def mm1(e):
                last = None
                for c in range(4):
                    last = e.matmul(ps[b1][:, 0:Wb], lhsT=ones_f[:, :], rhs=cob[:, c, 0:Wb], start=(c == 0), stop=(c == 3))
                return last
            pg.op("pe", mm1, reads=tuple(("cob", c) for c in range(4)) + ("ones",), writes=(("ps", b1),))
            sqt = []
            for c in range(4):
                tq = tmpbuf()
                sqt.append(tq)
                pg.op("act", lambda e, c=c, tq=tq: e.activation(out=tmp[:, tq, 0:Wb], in_=cob[:, c, 0:Wb], func=AF.Square),
                      reads=(("cob", c),), writes=(("tmp", tq),))

            def mm2(e):
                last = None
                for c in range(4):
                    last = e.matmul(ps[b2][:, 0:Wb], lhsT=ones_f[:, :], rhs=tmp[:, sqt[c], 0:Wb], start=(c == 0), stop=(c == 3))
                return last
            pg.op("pe", mm2, reads=tuple(("tmp", tq) for tq in sqt) + ("ones",), writes=(("ps", b2),))
            pg.op("dve", lambda e: e.tensor_scalar(out=lnm[:, 0, 0:Wb], in0=ps[b1][:, 0:Wb], scalar1=1.0 / 512, scalar2=None, op0=ALU.mult),
                  reads=(("ps", b1),), writes=("lnmean",))
            tm = tmpbuf()
            pg.op("dve", lambda e: e.tensor_tensor(out=tmp[:, tm, 0:Wb], in0=lnm[:, 0, 0:Wb], in1=lnm[:, 0, 0:Wb], op=ALU.mult),
                  reads=("lnmean",), writes=(("tmp", tm),))
            pg.op("dve", lambda e: e.scalar_tensor_tensor(out=lnm[:, 1, 0:Wb], in0=ps[b2][:, 0:Wb], scalar=1.0 / 512, in1=tmp[:, tm, 0:Wb],
                                                          op0=ALU.mult, op1=ALU.subtract),
                  reads=(("ps", b2), ("tmp", tm)), writes=("lnrstd",))
            pg.op("act", lambda e: e.activation(out=lnm[:, 1, 0:Wb], in_=lnm[:, 1, 0:Wb], func=AF.Sqrt, bias=P("epsc", 1), scale=1.0),
                  reads=("lnrstd", "prm"), writes=("lnrstd",))
            pg.op("dve", lambda e: e.reciprocal(out=lnm[:, 1, 0:Wb], in_=lnm[:, 1, 0:Wb]), reads=("lnrstd",), writes=("lnrstd",))
            for c in range(4):
                t1_, t2_ = tmpbuf(), tmpbuf()
                pg.op("dve", lambda e, c=c, t1_=t1_: e.tensor_tensor(out=tmp[:, t1_, 0:Wb], in0=cob[:, c, 0:Wb], in1=lnm[:, 0, 0:Wb], op=ALU.subtract),
                      reads=(("cob", c), "lnmean"), writes=(("tmp", t1_),))
                pg.op("dve", lambda e, t1_=t1_, t2_=t2_: e.tensor_tensor(out=tmp[:, t2_, 0:Wb], in0=tmp[:, t1_, 0:Wb], in1=lnm[:, 1, 0:Wb], op=ALU.mult),
                      reads=(("tmp", t1_), "lnrstd"), writes=(("tmp", t2_),))
                go = _off["alng"] + e_ * 4 + c
                bo = _off["alnb"] + e_ * 4 + c
                for (sg, lo, hi), of in zip(pcs, offs):
                    def fsl(e, c=c, t2_=t2_, lo=lo, hi=hi, of=of, go=go, bo=bo):
                        return e.activation(out=act[:, c, lo:hi], in_=tmp[:, t2_, of - 30:of - 30 + hi - lo], func=AF.Silu,
                                            bias=prm[:, bo:bo + 1], scale=prm[:, go:go + 1])
                    pg.op("act", fsl, reads=(("tmp", t2_), "prm"), writes=(("act", c, ti),))
        pe_defer.append([2, ln_tail])

    def ab_rest(S, boundary):
        l = S["l"]
        e_ = l // 2
        win = ab_in[e_]
        out_phase(l, 1, ab_out[e_], 0, 4)
        for g in range(4):
            wnd = 2 << g
            s = load_piece([(0, 8, wcols(win, 1024 + g * 128)), (8, 1, b_wg[e_, g].rearrange("p (o c) -> p o c", o=1))])
            for ti, (t0, t1) in enumerate(TILES):
                N = t1 - t0
                pcs = pieces(t0, t1)
                offs = []
                off = 0
                for (sg, lo, hi) in pcs:
                    off += 15
                    offs.append(off)
                    off += hi - lo
                tot = off
                Wb = tot - 15
                bb = bank()
                bi = st["bub"]
                st["bub"] = 1 - bi
                bk_ = ("bub", bi)

                def mm(e, s=s, bb=bb, t0=t0, t1=t1, N=N):
                    last = None
                    for k in range(KC):
                        last = e.matmul(ps[bb][:, 0:N], lhsT=slots[:, s, k, :], rhs=h[:, k, t0:t1], start=(k == 0), stop=(k == KC - 1))
                    return last
                pg.op("pe", mm, reads=(("slot", s),) + tuple(("h", k, ti) for k in range(KC)), writes=(("ps", bb),))
                pe_tick()
                for (sg, lo, hi), of in zip(pcs, offs):
                    if sg == 0:
                        if ti == 0:
                            pg.op("dve", lambda e, bi=bi: e.memset(bub[:, bi, 0:15], 0.0), writes=(bk_,))
                        else:
                            pN = TILES[ti - 1][1] - TILES[ti - 1][0]
                            pg.op("dve", lambda e, bi=bi, pN=pN: e.tensor_copy(out=bub[:, bi, 0:15], in_=bub[:, 1 - bi, pN:pN + 15]),
                                  reads=(("bub", 1 - bi),), writes=(bk_,))
                    else:
                        pg.op("dve", lambda e, bi=bi, of=of, sg=sg, g=g: e.tensor_copy(out=bub[:, bi, of - 15:of], in_=sbs[:, e_, sg - 1, g, :]),
                              reads=("st_b",), writes=(bk_,))
                    pg.op("act", lambda e, bi=bi, bb=bb, lo=lo, hi=hi, of=of, t0=t0: e.activation(
                        out=bub[:, bi, of:of + hi - lo], in_=ps[bb][:, lo - t0:hi - t0], func=AF.Copy),
                        reads=(("ps", bb),), writes=(bk_,))
                if ti == 0:
                    pg.op("dve", lambda e, bi=bi: e.tensor_tensor(out=bub[:, bi, 15:15 + HALO], in0=bub[:, bi, 15:15 + HALO],
                                                                  in1=P("hmask", HALO), op=ALU.mult), reads=("prm",), writes=(bk_,))
                cur = None
                sh = 1
                for lev in range(g + 1):
                    tn = tmpbuf()
                    if cur is None:
                        pg.op("dve", lambda e, bi=bi, tn=tn, sh=sh, tot=tot: e.tensor_tensor(
                            out=tmp[:, tn, sh:tot], in0=bub[:, bi, sh:tot], in1=bub[:, bi, 0:tot - sh], op=ALU.add),
                            reads=(bk_,), writes=(("tmp", tn),))
                    else:
                        pg.op("dve", lambda e, tn=tn, cur=cur, sh=sh, tot=tot: e.tensor_tensor(
                            out=tmp[:, tn, sh:tot], in0=tmp[:, cur, sh:tot], in1=tmp[:, cur, 0:tot - sh], op=ALU.add),
                            reads=(("tmp", cur),), writes=(("tmp", tn),))
                    cur = tn
                    sh *= 2
                di = st["dbf"]
                st["dbf"] = 1 - di
                pg.op("dve", lambda e, bi=bi, cur=cur, di=di, Wb=Wb, wnd=wnd: e.scalar_tensor_tensor(
                    out=dbf[:, di, 0:Wb], in0=tmp[:, cur, 15:15 + Wb], scalar=1.0 / wnd, in1=bub[:, bi, 15:15 + Wb],
                    op0=ALU.mult, op1=ALU.subtract), reads=(("tmp", cur), bk_), writes=(("dbf", di),))
                if ti == 0:
                    tf = tmpbuf()
                    io = _off["invcnt"] + g * 16
                    pg.op("dve", lambda e, cur=cur, tf=tf, io=io: e.tensor_tensor(
                        out=tmp[:, tf, 0:16], in0=tmp[:, cur, 15 + HALO:15 + HALO + 16], in1=prm[:, io:io + 16], op=ALU.mult),
                        reads=(("tmp", cur), "prm"), writes=(("tmp", tf),))
                    pg.op("dve", lambda e, bi=bi, tf=tf, di=di: e.tensor_tensor(
                        out=dbf[:, di, HALO:HALO + 16], in0=tmp[:, tf, 0:16], in1=bub[:, bi, 15 + HALO:15 + HALO + 16], op=ALU.subtract),
                        reads=(("tmp", tf), bk_), writes=(("dbf", di),))
                if ti == NT - 1:
                    for (sg, lo, hi), of in zip(pcs, offs):
                        e0 = of + (hi - lo) - 15

                        def fd(e, bi=bi, sg=sg, e0=e0, g=g):
                            return e.dma_start(out=ob_d[:, e_, sg, g, :], in_=bub[:, bi, e0:e0 + 15])
                        pg.dma("sp", "outs", fd, reads=(bk_,))

                def grp(s=s, di=di, Wb=Wb, g=g, ti=ti, pcs=pcs, offs=offs):
                    bd = bank()
                    pg.op("pe", lambda e: e.matmul(ps[bd][:, 0:Wb], lhsT=slots[:, s, 8, :], rhs=dbf[:, di, 0:Wb], start=True, stop=True),
                          reads=(("slot", s), ("dbf", di)), writes=(("ps", bd),))
                    so = _off["bscale"] + e_ * 4 + g
                    for (sg, lo, hi), of in zip(pcs, offs):
                        pg.op("act", lambda e, lo=lo, hi=hi, of=of: e.activation(
                            out=act[:, g, lo:hi], in_=ps[bd][:, of - 15:of - 15 + hi - lo], func=AF.Identity, scale=prm[:, so:so + 1]),
                            reads=(("ps", bd), "prm"), writes=(("act", g, ti),))
                pe_defer.append([1, grp])
            pump_mod(1)
        pe_flush()
        out_phase(l, 1, ab_out[e_], 4, 4, pre_loop=boundary[0], after_tile=boundary[1])

    def f_prm(e):
        return e.dma_start(out=prm[:, :], in_=prm_d)
    pg.dma("sp", "ld_prm", f_prm, writes=("prm",))
    pg.dma("sp", "ld_cv", lambda e: e.dma_start(out=cv[:, :, :], in_=cv_d), writes=("cv",))
    pg.dma("sp", "ld_id", lambda e: e.dma_start(out=tmp[:, 0, 0:128], in_=id_d), writes=(("tmp", 0),))
    x_loads = []
    for ti, (t0, t1) in enumerate(TILES):
        def fxl(e, t0=t0, t1=t1):
            return e.dma_start(out=x[:, :, t0:t1], in_=xT_d[:, :, t0:t1])
        x_loads.append((ti, fxl))
    ti0, f0_ = x_loads[0]
    pg.dma("sp", "ld_x0", f0_, writes=tuple(("x", m, 0) for m in range(KC)))
    pg.dma("sp", "ld_sa", lambda e: e.dma_start(out=sa[:, :, :, :, :].rearrange("p a b c d -> p (a b c d)"), in_=sa_d), writes=("st_a",))
    pg.dma("sp", "ld_sb", lambda e: e.dma_start(out=sbs[:, :, :, :, :].rearrange("p a b c d -> p (a b c d)"), in_=sb_d), writes=("st_b",))
    pg.dma("sp", "ld_sc", lambda e: e.dma_start(out=scs[:, :, :, :, :].rearrange("p a b c d -> p (a b c d)"), in_=sc_d), writes=("st_c",))

    pg.op("dve", lambda e: e.memset(ones_bf[:, :], 1.0), writes=("ones",))
    pg.op("dve", lambda e: e.memset(ones_f[:, :], 1.0), writes=("ones",))
    pg.op("dve", lambda e: e.tensor_copy(out=ident_bf[:, :], in_=tmp[:, 0, 0:128]), reads=(("tmp", 0),), writes=("ident",))
    st["tmp"] = 1
    pg.op("act", lambda e: e.activation(out=cact[:, :, :], in_=cv[:, :, :], func=AF.Silu), reads=("cv",), writes=("cact",))

    def gen_diag(e2, anchor):
        o_ = _off["aconvw"] + e2 * 124
        pg.op("dve", lambda e, o_=o_: e.tensor_scalar(out=awh[:, :], in0=prm[:, o_:o_ + 124], scalar1=0.5, scalar2=None, op0=ALU.mult),
              reads=("prm", anchor), writes=("awh",))
        for c in range(4):
            for hf_, (k0, k1) in enumerate(((0, 16), (16, 31))):
                def fdg(e, c=c, k0=k0, k1=k1):
                    n = k1 - k0
                    return e.tensor_tensor(out=dg[:, k0:k1, :],
                                           in0=ident_bf[:, :].unsqueeze(1).to_broadcast([128, n, 128]),
                                           in1=awh[:, c * 31 + k0:c * 31 + k1].unsqueeze(2).to_broadcast([128, n, 128]),
                                           op=ALU.mult)
                pg.op("dve", fdg, reads=("ident", "awh"), writes=(("dg", hf_),))

                def fst_(e, e2=e2, c=c, k0=k0, k1=k1, hf_=hf_):
                    n = k1 - k0
                    return e.dma_start(out=dgd[e2, c, hf_, :, 0:n * 128], in_=dg[:, k0:k1, :].rearrange("p k j -> p (k j)"))
                pg.dma("sp", "dgw%d" % hf_, fst_, reads=(("dg", hf_),), writes=(("dgd", e2, hf_),))

    subs = []
    for l in range(DEPTH):
        subs.append({"kind": "ffn", "l": l, "s": 0, "f": 0})
        subs.append({"kind": "ab" if l % 2 == 0 else "c", "l": l, "s": 1})
        subs.append({"kind": "ffn", "l": l, "s": 2, "f": 1})
    PRE = {"ffn": ffn_preload, "ab": ab_preload, "c": c_preload}
    FIRST = {"ffn": ffn_first_tile, "ab": ab_first_tile, "c": c_first_tile}
    REST = {"ffn": ffn_rest, "ab": ab_rest, "c": c_rest}
    LAG = 2
    NDEF = 5

    for i in range(36):
        mod_pending.append((0, i))
    S0 = subs[0]
    for i in range(12):
        if i == 8:
            PRE["ffn"](S0)
        s_before = st["slot"]
        pump_mod(1)
        if 1 <= i <= 5:
            ti_, fx_ = x_loads[i]
            pg.dma("sp", "ld_x%d" % ti_, fx_, reads=(("slot", s_before),), writes=tuple(("x", m, ti_) for m in range(KC)))
    for i in range(36):
        mod_pending.append((1, i))

    for ti in range(NT):
        norm_tile(0, 0, ti)
        if ti >= 1:
            FIRST["ffn"](S0, ti - 1)
    FIRST["ffn"](S0, NT - 1)
    held.clear()

    for i, S in enumerate(subs):
        nxt = subs[i + 1] if i + 1 < len(subs) else None
        if i == 0:
            gen_diag(0, ("act", 0, NT - 1))
        if S["kind"] == "ffn" and S["s"] == 0 and S["l"] == 1:
            gen_diag(1, ("act", 0, NT - 1))
        if S["kind"] == "ffn" and S["s"] == 0 and S["l"] >= 1 and S["l"] + 1 < DEPTH:
            for i_ in range(36):
                mod_pending.append((S["l"] + 1, i_))

        def pre(nxt=nxt):
            if nxt is not None:
                PRE[nxt["kind"]](nxt)

        def cb(ti, nxt=nxt):
            if nxt is None:
                norm_tile(DEPTH - 1, 0, ti, final=True, defer=3)
            else:
                norm_tile(nxt["l"], nxt["s"], ti, defer=NDEF)
                if ti - LAG >= 0:
                    FIRST[nxt["kind"]](nxt, ti - LAG)
        REST[S["kind"]](S, (pre, cb))
        pe_flush()
        if nxt is not None:
            for ti in range(NT - LAG, NT):
                FIRST[nxt["kind"]](nxt, ti)
        held.clear()
        pe_flush()

    pg.final_wait("sp", ["outs"])
    sim = pg.finalize()
    _SIM["time_us"] = sim
    _SIM["pg"] = pg
    _SIM["busy"] = {e: sum(pg.ops[i]["dur"] for i in pg.order[e]) for e in pg.ENGS}

    with nc.Block() as block:
        @block.tensor
        def _(e):
            pg.run("pe", e)

        @block.scalar
        def _(e):
            pg.run("act", e)

        @block.vector
        def _(e):
            pg.run("dve", e)

        @block.gpsimd
        def _(e):
            pg.run("pool", e)

        @block.sync
        def _(e):
            pg.run("sp", e)


def _fm(v):
    v = np.asarray(v, dtype=np.float32)
    n = v.shape[-1] // 128
    v = v.reshape(v.shape[:-1] + (n, 128))
    return np.ascontiguousarray(np.moveaxis(v, -1, 0))


_NC_CACHE = {}


def kernel(x_prompt, x_sample, state_conv_a, state_pool_b, state_conv_c, c_prompt, c_sample,
           ada_w, ada_b, norm_g, ffn_w_gu, ffn_w_down, ab_w_in, a_conv_w, a_conv_b, a_ln_g,
           a_ln_b, b_w_group, b_scale, ab_w_out, c_w_in, c_conv_w, c_w_out, final_g):
    f32 = np.float32
    xp = np.asarray(x_prompt, f32)[0]
    xs = np.asarray(x_sample, f32)
    prm_base = np.zeros((128, NPRM), f32)

    def put(name, arr):
        arr = np.ascontiguousarray(arr, dtype=f32).reshape(128, -1)
        prm_base[:, _off[name]:_off[name] + arr.shape[1]] = arr
    put("normg", _fm(norm_g))
    put("finalg", _fm(final_g))
    put("adab", _fm(ada_b))
    put("aconvw", np.transpose(_fm(a_conv_w), (0, 1, 3, 2)))
    put("aconvb", _fm(a_conv_b))
    put("alng", _fm(a_ln_g))
    put("alnb", _fm(a_ln_b))
    put("bscale", _fm(b_scale))
    put("cconvw", np.transpose(_fm(c_conv_w), (0, 1, 3, 2)))
    sca = np.asarray(state_conv_a, f32)
    spb = np.asarray(state_pool_b, f32)
    scc = np.asarray(state_conv_c, f32)
    cpr = np.asarray(c_prompt, f32)
    csa = np.asarray(c_sample, f32)
    shared = {
        "ada_w": np.ascontiguousarray(ada_w, dtype=f32), "ffn_w_gu": np.ascontiguousarray(ffn_w_gu, dtype=f32),
        "ffn_w_down": np.ascontiguousarray(ffn_w_down, dtype=f32), "ab_w_in": np.ascontiguousarray(ab_w_in, dtype=f32),
        "b_w_group": np.ascontiguousarray(b_w_group, dtype=f32), "ab_w_out": np.ascontiguousarray(ab_w_out, dtype=f32),
        "c_w_in": np.ascontiguousarray(c_w_in, dtype=f32), "c_w_out": np.ascontiguousarray(c_w_out, dtype=f32),
    }
    in_maps = []
    for i in range(NCORES):
        toks = np.zeros((T, D), f32)
        if i == 0:
            toks[HALO:HALO + MAIN] = xp[0:MAIN]
        else:
            toks[0:HALO + MAIN] = xp[i * MAIN - HALO:(i + 1) * MAIN]
        toks[HALO + MAIN:HALO + MAIN + LS] = xs[2 * i]
        toks[HALO + MAIN + LS:] = xs[2 * i + 1]
        xT = np.ascontiguousarray(toks.reshape(T, KC, 128).transpose(2, 1, 0))
        cvec = np.stack([cpr[0], csa[2 * i], csa[2 * i + 1]], 0)
        cvf = np.ascontiguousarray(cvec.reshape(3, KC, 128).transpose(2, 1, 0))
        prm = prm_base.copy()
        prm[:, _off["hmask"]:_off["hmask"] + HALO] = 0.0 if i == 0 else 1.0
        ic = np.zeros((4, 16), f32)
        for g in range(4):
            w = 2 << g
            for p_ in range(16):
                ic[g, p_] = (1.0 / min(w, p_ + 1)) if i == 0 else (1.0 / w)
        prm[:, _off["invcnt"]:_off["invcnt"] + 64] = ic.reshape(1, 64)
        prm[:, _off["epsc"]] = EPS
        sa_i = np.ascontiguousarray(np.transpose(_fm(sca[:, 2 * i:2 * i + 2]), (0, 1, 2, 4, 3))).reshape(128, -1)
        sb_i = np.ascontiguousarray(np.transpose(_fm(spb[:, 2 * i:2 * i + 2]), (0, 1, 2, 4, 3))).reshape(128, -1)
        sc_i = np.ascontiguousarray(np.transpose(_fm(scc[:, 2 * i:2 * i + 2]), (0, 1, 2, 4, 3))).reshape(128, -1)
        m = {"xT": xT, "cvec": cvf, "prm": prm, "sa": sa_i, "sb": sb_i, "scn": sc_i, "ident": np.eye(128, dtype=f32)}
        m.update(shared)
        in_maps.append(m)

    if "nc" not in _NC_CACHE:
        _NC_CACHE["nc"] = build_nc()
    nc = _NC_CACHE["nc"]
    res = run_bass_kernel_spmd(nc, in_maps, core_ids=list(range(NCORES)))
    R = res.results

    y_prompt = np.zeros((1, NCORES * MAIN, D), f32)
    y_sample = np.zeros((2 * NCORES, LS, D), f32)
    na_s = np.zeros((2, 2 * NCORES, 30, 512), f32)
    nb_s = np.zeros((2, 2 * NCORES, 15, 512), f32)
    nc_s = np.zeros((2, 2 * NCORES, 2, 1024), f32)
    for i in range(NCORES):
        yT = np.asarray(R[i]["yT"], f32)
        rows = yT.transpose(2, 1, 0).reshape(MAIN + 2 * LS, D)
        y_prompt[0, i * MAIN:(i + 1) * MAIN] = rows[:MAIN]
        y_sample[2 * i] = rows[MAIN:MAIN + LS]
        y_sample[2 * i + 1] = rows[MAIN + LS:]
        oa = np.asarray(R[i]["oa"], f32).transpose(1, 2, 4, 3, 0).reshape(2, 3, 30, 512)
        ob = np.asarray(R[i]["ob"], f32).transpose(1, 2, 4, 3, 0).reshape(2, 3, 15, 512)
        oc = np.asarray(R[i]["oc"], f32).transpose(1, 2, 4, 3, 0).reshape(2, 3, 2, 1024)
        na_s[:, 2 * i:2 * i + 2] = oa[:, 1:3]
        nb_s[:, 2 * i:2 * i + 2] = ob[:, 1:3]
        nc_s[:, 2 * i:2 * i + 2] = oc[:, 1:3]
        if i == NCORES - 1:
            na_p = oa[:, 0:1].copy()
            nb_p = ob[:, 0:1].copy()
            nc_p = oc[:, 0:1].copy()
    return (y_prompt, y_sample, na_p, nb_p, nc_p, na_s, nb_s, nc_s)
```

```python
import numpy as np
from contextlib import ExitStack
import concourse.bass as bass
import concourse.mybir as mybir
from concourse.bass_utils import run_bass_kernel_spmd

F32 = mybir.dt.float32
BF16 = mybir.dt.bfloat16
AF = mybir.ActivationFunctionType
ALU = mybir.AluOpType

NCORES = 8
D = 1024
KC = 8
DFF = 2816
NJ = 22
DEPTH = 4
HALO = 64
MAIN = 2048
LS = 32
T = HALO + MAIN + 2 * LS
SEGS = [(0, HALO + MAIN), (HALO + MAIN, HALO + MAIN + LS), (HALO + MAIN + LS, T)]
TILES = [(0, 384), (384, 768), (768, 1152), (1152, 1536), (1536, 1856), (1856, 2176)]
NMAX = 384
EPS = 1e-6
NSLOT = 7
SLOTW = 2048
ACTK = 4

_off = {}
_o = 0
for _n, _w in [("normg", 96), ("finalg", 8), ("adab", 288), ("aconvw", 248), ("aconvb", 8), ("alng", 8),
               ("alnb", 8), ("bscale", 8), ("cconvw", 48), ("hmask", 64), ("invcnt", 64), ("epsc", 1)]:
    _off[_n] = _o
    _o += _w
NPRM = _o


def pieces(t0, t1):
    out = []
    for s, (a, b) in enumerate(SEGS):
        lo, hi = max(a, t0), min(b, t1)
        if lo < hi:
            out.append((s, lo, hi))
    return out


class _FakeIns:
    def then_inc(self, *a, **k):
        return self


class _FakeEng:
    def __init__(self):
        self.cost = 0.0
        self.tag = None
        self.nbytes = 0

    @staticmethod
    def _free(ap):
        n = 1
        for d in ap.shape[1:]:
            n *= int(d)
        return n

    def matmul(self, out, lhsT=None, rhs=None, **k):
        c = max(self._free(out) / 2400.0 + 0.004, 0.035)
        if lhsT is not None and lhsT.dtype == F32:
            c *= 2.2
        self.cost += c
        return _FakeIns()

    def activation(self, out=None, in_=None, func=None, **k):
        self.cost += 0.22 + self._free(out) * 0.001
        if func in (AF.Silu, AF.Tanh):
            self.tag = "A"
        elif func == AF.Sqrt:
            self.tag = "B"
        return _FakeIns()

    def reciprocal(self, out=None, in_=None, **k):
        self.cost += 0.06 + self._free(out) * 0.0064
        return _FakeIns()

    def scalar_tensor_tensor(self, out=None, **k):
        self.cost += 0.06 + self._free(out) * 0.00146
        return _FakeIns()

    def dma_start(self, out=None, in_=None, **k):
        n = 1
        for d in out.shape:
            n *= int(d)
        self.nbytes += n * (4 if in_.dtype == F32 else 2)
        return _FakeIns()

    def __getattr__(self, name):
        def generic(*a, **k):
            out = k.get("out", a[0] if a else None)
            self.cost += 0.06 + (self._free(out) * 0.0012 if out is not None else 0.0)
            return _FakeIns()
        return generic


class Prog:
    ENGS = ("pe", "act", "dve", "pool", "sp")

    def __init__(self, nc, ctx):
        self.nc = nc
        self.ctx = ctx
        self.semh = {}
        for n in ("pe", "act", "dve"):
            self.semh[n] = ctx.enter_context(nc.semaphore("s_" + n))
        self.ops = []
        self.lastw = {}
        self.readers = {}
        self.final_sems = []
        self.order = None

    def dma_sem(self, name):
        if name not in self.semh:
            self.semh[name] = self.ctx.enter_context(self.nc.semaphore("d_" + name))
        return name

    def _add(self, eng, fn, sem, inc, reads, writes, dur, xfer, tag):
        idx = len(self.ops)
        deps = set()
        for k in reads:
            w = self.lastw.get(k)
            if w is not None:
                deps.add(w)
        for k in writes:
            w = self.lastw.get(k)
            if w is not None:
                deps.add(w)
            deps.update(self.readers.get(k, ()))
        deps.discard(idx)
        self.ops.append({"eng": eng, "fn": fn, "sem": sem, "inc": inc, "deps": deps, "dur": dur, "xfer": xfer, "tag": tag})
        for k in reads:
            self.readers.setdefault(k, []).append(idx)
        for k in writes:
            self.lastw[k] = idx
            self.readers[k] = []
        return idx

    def op(self, eng, fn, reads=(), writes=()):
        fe = _FakeEng()
        fn(fe)
        return self._add(eng, fn, eng, 1, reads, writes, fe.cost, None, fe.tag)

    def dma(self, queue, semname, fn, reads=(), writes=()):
        self.dma_sem(semname)
        fe = _FakeEng()
        fn(fe)
        issue = 1.0 if queue == "pool" else 0.15
        return self._add(queue, fn, semname, 16, reads, writes, issue, fe.nbytes, None)

    def final_wait(self, queue, semnames):
        self.final_sems = [(queue, s) for s in semnames]

    def finalize(self):
        import heapq
        ops = self.ops
        n = len(ops)
        nd = [len(o["deps"]) for o in ops]
        users = [[] for _ in range(n)]
        for i, o in enumerate(ops):
            for d in o["deps"]:
                users[d].append(i)
        LAT = 0.2
        done_t = [0.0] * n
        start_t = [0.0] * n
        free = {e: 0.0 for e in self.ENGS}
        pend = {e: [] for e in self.ENGS}
        avail = {e: [] for e in self.ENGS}
        act_set = [None]
        dma_pipe = [0.0]
        order = {e: [] for e in self.ENGS}
        for i, o in enumerate(ops):
            if nd[i] == 0:
                heapq.heappush(pend[o["eng"]], (0.0, i))
        left = n
        while left:
            best = None
            for e in self.ENGS:
                t = free[e]
                while pend[e] and pend[e][0][0] <= t:
                    heapq.heappush(avail[e], heapq.heappop(pend[e])[1])
                if avail[e]:
                    cand = (t, avail[e][0], e, True)
                elif pend[e]:
                    cand = (pend[e][0][0], pend[e][0][1], e, False)
                else:
                    continue
                if best is None or cand[:2] < best[:2]:
                    best = cand
            st_, i, e, from_avail = best
            if from_avail:
                heapq.heappop(avail[e])
            else:
                heapq.heappop(pend[e])
            o = ops[i]
            start_t[i] = st_
            dur = o["dur"]
            if e == "act" and o["tag"] is not None:
                if act_set[0] is not None and act_set[0] != o["tag"]:
                    dur += 1.8
                act_set[0] = o["tag"]
            if o["xfer"] is not None:
                free[e] = st_ + dur
                xs = max(st_ + dur, dma_pipe[0])
                dma_pipe[0] = xs + o["xfer"] / 200e3
                done_t[i] = dma_pipe[0] + 2.0
            else:
                free[e] = st_ + dur
                done_t[i] = st_ + dur
            order[e].append(i)
            left -= 1
            for u in users[i]:
                nd[u] -= 1
                if nd[u] == 0:
                    rt = max(done_t[d] for d in ops[u]["deps"]) + LAT
                    heapq.heappush(pend[ops[u]["eng"]], (rt, u))
        self.order = order
        self.start_t = start_t
        self.done_t = done_t
        self.sim_time = max(done_t) if n else 0.0
        cnt = {}
        self.val = [0] * n
        for e in self.ENGS:
            for i in order[e]:
                sname = ops[i]["sem"]
                cnt[sname] = cnt.get(sname, 0) + ops[i]["inc"]
                self.val[i] = cnt[sname]
        self.cnt = cnt
        return self.sim_time

    def run(self, eng_name, eng):
        ops = self.ops
        waited = {}
        for i in self.order[eng_name]:
            o = ops[i]
            need = {}
            for d in o["deps"]:
                sd = ops[d]["sem"]
                v = self.val[d]
                if need.get(sd, 0) < v:
                    need[sd] = v
            for sd, v in need.items():
                if waited.get(sd, 0) < v:
                    waited[sd] = v
                    eng.wait_ge(self.semh[sd], v)
            ins = o["fn"](eng)
            ins.then_inc(self.semh[o["sem"]], o["inc"])
        for (q, sname) in self.final_sems:
            if q == eng_name and self.cnt.get(sname, 0) > 0:
                eng.wait_ge(self.semh[sname], self.cnt[sname])


_SIM = {}


def build_nc():
    nc = bass.Bass("TRN2", target_bir_lowering=False)
    ctx = ExitStack()
    with ctx:
        _build(nc, ctx)
    return nc


def _build(nc, ctx):
    def din(name, shape):
        return nc.dram_tensor(name, list(shape), F32, kind="ExternalInput").ap()

    def dout(name, shape):
        return nc.dram_tensor(name, list(shape), F32, kind="ExternalOutput").ap()

    xT_d = din("xT", [128, KC, T])
    cv_d = din("cvec", [128, KC, 3])
    prm_d = din("prm", [128, NPRM])
    sa_d = din("sa", [128, 2 * 2 * 4 * 30])
    sb_d = din("sb", [128, 2 * 2 * 4 * 15])
    sc_d = din("scn", [128, 2 * 2 * 8 * 2])
    id_d = din("ident", [128, 128])
    ada_w = din("ada_w", [DEPTH, D, 9 * D])
    w_gu = din("ffn_w_gu", [DEPTH, 2, D, 2 * DFF])
    w_dn = din("ffn_w_down", [DEPTH, 2, DFF, D])
    ab_in = din("ab_w_in", [2, D, 1536])
    b_wg = din("b_w_group", [2, 4, 128, 128])
    ab_out = din("ab_w_out", [2, D, D])
    c_in = din("c_w_in", [2, D, 3 * D])
    c_out = din("c_w_out", [2, D, D])
    yT_d = dout("yT", [128, KC, MAIN + 2 * LS])
    oa_d = dout("oa", [128, 2, 3, 4, 30])
    ob_d = dout("ob", [128, 2, 3, 4, 15])
    oc_d = dout("oc", [128, 2, 3, 8, 2])
    dgd = nc.dram_tensor("dgd", [2, 4, 2, 128, 16 * 128], BF16, kind="Internal").ap()

    def sb_(name, shape, dt=F32):
        return ctx.enter_context(nc.sbuf_tensor(name, list(shape), dt))

    NT = len(TILES)
    x = sb_("x", [128, KC, T])
    h = sb_("h", [128, KC, T], BF16)
    act = sb_("act", [128, ACTK, T], BF16)
    slots = sb_("slots", [128, NSLOT, 16, 128], BF16)
    sqb = sb_("sqb", [128, 2, KC, NMAX], BF16)
    rbuf = sb_("rbuf", [128, 1, NMAX])
    NTMP = 6
    TMPW = 400
    tmp = sb_("tmp", [128, NTMP, TMPW])
    abf = sb_("abf", [128, 4, 30 + 384], BF16)
    astg = sb_("astg", [128, 3, 4, 30])
    dg = sb_("dg", [128, 31, 128], BF16)
    cob = sb_("cob", [128, 4, 384])
    lnm = sb_("lnm", [128, 2, 384])
    bub = sb_("bub", [128, 2, 45 + 384])
    ztl = bub
    dbf = sb_("dbf", [128, 2, 384], BF16)
    prm = sb_("prm_s", [128, NPRM])
    cv = sb_("cv_s", [128, KC, 3])
    cact = sb_("cact", [128, KC, 3], BF16)
    sa = sb_("sa_s", [128, 2, 2, 4, 30])
    sbs = sb_("sb_s", [128, 2, 2, 4, 15])
    scs = sb_("sc_s", [128, 2, 2, 8, 2])
    modfm = sb_("modfm", [128, 2, 72, 3])
    gsb = sb_("gsb", [128, 2, 3, KC, 3])
    ghb = sb_("ghb", [128, 2, 3, KC, 3])
    ones_bf = sb_("ones_bf", [128, 128], BF16)
    ones_f = sb_("ones_f", [128, 128])
    ident_bf = sb_("ident_bf", [128, 128], BF16)
    awh = sb_("awh", [128, 124])
    ps = [ctx.enter_context(nc.psum_tensor("ps%d" % i, [128, 512], F32)) for i in range(8)]

    pg = Prog(nc, ctx)
    st = {"bank": 0, "slot": 0, "tmp": 0, "sq": 0, "r": 0, "dbf": 0, "bub": 0}

    def bank():
        b = st["bank"]
        st["bank"] = (b + 1) % 8
        return b

    def tmpbuf():
        i = st["tmp"]
        st["tmp"] = (i + 1) % NTMP
        return i

    def P(name, w=1, i=0):
        o = _off[name] + i
        return prm[:, o:o + w]

    held = set()

    def alloc_slot():
        for _ in range(NSLOT):
            s = st["slot"]
            st["slot"] = (s + 1) % NSLOT
            if s not in held:
                return s
        raise RuntimeError("no free weight slot")

    def load_piece(parts):
        s = alloc_slot()
        for (b0, nb, src) in parts:
            def f(e, s=s, b0=b0, nb=nb, src=src):
                dst = slots[:, s, b0:b0 + nb, :]
                if len(src.shape) == 3 and src.shape[2] == 1024:
                    dst = dst.rearrange("p (j m) c -> p j (m c)", m=8)
                elif len(src.shape) == 3 and src.shape[2] == 512:
                    dst = dst.rearrange("p (j m) c -> p j (m c)", m=4)
                return e.dma_start(out=dst, in_=src)
            pg.dma("pool", "slot%d" % s, f, reads=(), writes=(("slot", s),))
        return s

    def wcols(w2d, c0, n=128):
        return w2d.rearrange("(kc p) n -> p kc n", p=128)[:, :, c0:c0 + n]

    def wrows(w2d, r0, nr):
        return w2d.rearrange("(j p) n -> p j n", p=128)[:, r0:r0 + nr, :]

    pe_defer = []

    def pe_tick():
        for d in list(pe_defer):
            d[0] -= 1
            if d[0] <= 0:
                pe_defer.remove(d)
                d[1]()

    def pe_flush():
        while pe_defer:
            d = pe_defer.pop(0)
            d[1]()

    mod_pending = []

    def mod_piece(l, i):
        src = ada_w[l].rearrange("(kc p) n -> p kc n", p=128)[:, :, i * 256:(i + 1) * 256]
        s = alloc_slot()

        def f(e, s=s, src=src):
            return e.dma_start(out=slots[:, s, :, :].rearrange("p (k a) c -> p k (a c)", a=2), in_=src)
        pg.dma("pool", "slot%d" % s, f, writes=(("slot", s),))
        b = bank()

        def mm(e, s=s, b=b):
            last = None
            for ql in range(2):
                for k in range(KC):
                    last = e.matmul(ps[b][:, ql * 3:ql * 3 + 3], lhsT=slots[:, s, k * 2 + ql, :],
                                    rhs=cact[:, k, :], start=(k == 0), stop=(k == KC - 1))
            return last
        pg.op("pe", mm, reads=(("slot", s), "cact"), writes=(("ps", b),))
        for ql in range(2):
            q = 2 * i + ql

            def ev(e, b=b, ql=ql, q=q, l=l):
                return e.activation(out=modfm[:, l % 2, q, :], in_=ps[b][:, ql * 3:ql * 3 + 3],
                                    func=AF.Identity, bias=P("adab", 1, l * 72 + q), scale=1.0)
            pg.op("act", ev, reads=(("ps", b), "prm"), writes=(("mod", l % 2, q // 8),))

    def mod_derive(l, s_):
        lb = l % 2
        for m in range(KC):
            def f(e, m=m):
                o = _off["normg"] + (l * 3 + s_) * 8 + m
                return e.tensor_scalar(out=gsb[:, lb, s_, m, :], in0=modfm[:, lb, (3 * s_ + 1) * 8 + m, :],
                                       scalar1=1.0, scalar2=prm[:, o:o + 1], op0=ALU.add, op1=ALU.mult)
            pg.op("dve", f, reads=(("mod", lb, 3 * s_ + 1), "prm"), writes=(("gs", lb, s_),))

        def f2(e):
            return e.tensor_scalar(out=ghb[:, lb, s_, :, :], in0=modfm[:, lb, (3 * s_ + 2) * 8:(3 * s_ + 3) * 8, :],
                                   scalar1=(1.0 if s_ == 1 else 0.5), scalar2=None, op0=ALU.mult)
        pg.op("dve", f2, reads=(("mod", lb, 3 * s_ + 2),), writes=(("gh", lb, s_),))

    def pump_mod(n=1):
        for _ in range(n):
            if mod_pending:
                l, i = mod_pending.pop(0)
                mod_piece(l, i)
                if i % 12 == 11:
                    mod_derive(l, i // 12)

    def norm_tile(l, s_, ti, final=False, defer=0):
        t0, t1 = TILES[ti]
        N = t1 - t0
        lb = l % 2
        sq = st["sq"]
        st["sq"] = 1 - sq
        for m in range(KC):
            def f(e, m=m):
                return e.activation(out=sqb[:, sq, m, 0:N], in_=x[:, m, t0:t1], func=AF.Square)
            pg.op("act", f, reads=(("x", m, ti),), writes=(("sqb", sq),))

        def rest():
            b = bank()
            ri = 0

            def mm(e):
                last = None
                for m in range(KC):
                    last = e.matmul(ps[b][:, 0:N], lhsT=ones_bf[:, :], rhs=sqb[:, sq, m, 0:N],
                                    start=(m == 0), stop=(m == KC - 1))
                return last
            pg.op("pe", mm, reads=(("sqb", sq), "ones"), writes=(("ps", b),))

            def fr0(e):
                return e.activation(out=rbuf[:, ri, 0:N], in_=ps[b][:, 0:N], func=AF.Sqrt, bias=P("epsc", 1), scale=1.0 / D)
            pg.op("act", fr0, reads=(("ps", b), "prm"), writes=(("r", ri),))

            def fr(e):
                return e.reciprocal(out=rbuf[:, ri, 0:N], in_=rbuf[:, ri, 0:N])
            pg.op("dve", fr, reads=(("r", ri),), writes=(("r", ri),))
            for m in range(KC):
                if final:
                    def fy(e, m=m):
                        o = _off["finalg"] + m
                        return e.scalar_tensor_tensor(out=x[:, m, t0:t1], in0=x[:, m, t0:t1], scalar=prm[:, o:o + 1],
                                                      in1=rbuf[:, ri, 0:N], op0=ALU.mult, op1=ALU.mult)
                    pg.op("dve", fy, reads=(("r", ri), "prm"), writes=(("x", m, ti),))
                    continue
                tb = tmpbuf()

                def ft(e, m=m, tb=tb):
                    return e.tensor_tensor(out=tmp[:, tb, 0:N], in0=x[:, m, t0:t1], in1=rbuf[:, ri, 0:N], op=ALU.mult)
                pg.op("dve", ft, reads=(("x", m, ti), ("r", ri)), writes=(("tmp", tb),))
                for (sg, lo, hi) in pieces(t0, t1):
                    def fh(e, m=m, tb=tb, sg=sg, lo=lo, hi=hi):
                        return e.activation(out=h[:, m, lo:hi], in_=tmp[:, tb, lo - t0:hi - t0], func=AF.Identity,
                                            bias=modfm[:, lb, (3 * s_) * 8 + m, sg:sg + 1],
                                            scale=gsb[:, lb, s_, m, sg:sg + 1])
                    pg.op("act", fh, reads=(("tmp", tb), ("gs", lb, s_), ("mod", lb, 3 * s_)), writes=(("h", m, ti),))
            if final:
                lo_ = max(t0, HALO)

                def fyo(e):
                    return e.dma_start(out=yT_d[:, :, lo_ - HALO:t1 - HALO], in_=x[:, :, lo_:t1])
                pg.dma("sp", "outs", fyo, reads=tuple(("x", m, ti) for m in range(KC)))
        if defer > 0:
            pe_defer.append([defer, rest])
        else:
            rest()

    def wcolhalf(w2d, r0, nk, hc):
        return w2d.rearrange("(j p) n -> p j n", p=128)[:, r0:r0 + nk, hc * 512:(hc + 1) * 512]

    def out_phase(l, s_, wsrc2d, r0, nk, pre_loop=None, after_tile=None, pre_a=None, tile_order=None):
        lb = l % 2
        sl = []
        for hc in range(2):
            if hc == 0 and pre_a is not None:
                sl.append(pre_a)
            else:
                sl.append(load_piece([(0, nk * 4, wcolhalf(wsrc2d, r0, nk, hc))]))
        if pre_loop is not None:
            pre_loop()
        for ti in (tile_order if tile_order is not None else range(NT)):
            t0, t1 = TILES[ti]
            N = t1 - t0
            for m in range(KC):
                b = bank()

                def mm(e, m=m, b=b, t0=t0, t1=t1, N=N):
                    last = None
                    s = sl[m // 4]
                    for j in range(nk):
                        last = e.matmul(ps[b][:, 0:N], lhsT=slots[:, s, j * 4 + (m % 4), :], rhs=act[:, j, t0:t1],
                                        start=(j == 0), stop=(j == nk - 1))
                    return last
                pg.op("pe", mm, reads=(("slot", sl[m // 4]),) + tuple(("act", k_, ti) for k_ in range(nk)),
                      writes=(("ps", b),))
                pe_tick()
                for (sg, lo, hi) in pieces(t0, t1):
                    def fx(e, m=m, b=b, sg=sg, lo=lo, hi=hi, t0=t0):
                        return e.scalar_tensor_tensor(out=x[:, m, lo:hi], in0=ps[b][:, lo - t0:hi - t0],
                                                      scalar=ghb[:, lb, s_, m, sg:sg + 1], in1=x[:, m, lo:hi],
                                                      op0=ALU.mult, op1=ALU.add)
                    pg.op("dve", fx, reads=(("ps", b), ("gh", lb, s_)), writes=(("x", m, ti),))
            if after_tile is not None:
                after_tile(ti)

    FFN_PARTS = [(0, 4), (4, 4), (8, 4), (12, 4), (16, 3), (19, 3)]
    BORD = [NT - 1] + list(range(NT - 1))

    def ffn_chunk_tile(s, jl, ti):
        t0, t1 = TILES[ti]
        N = t1 - t0
        b1, b2 = bank(), bank()

        def mm(e):
            last = None
            for k in range(KC):
                e.matmul(ps[b1][:, 0:N], lhsT=slots[:, s, k, :], rhs=h[:, k, t0:t1], start=(k == 0), stop=(k == KC - 1))
            for k in range(KC):
                last = e.matmul(ps[b2][:, 0:N], lhsT=slots[:, s, 8 + k, :], rhs=h[:, k, t0:t1],
                                start=(k == 0), stop=(k == KC - 1))
            return last
        pg.op("pe", mm, reads=(("slot", s),) + tuple(("h", k, ti) for k in range(KC)), writes=(("ps", b1), ("ps", b2)))
        pe_tick()
        tb = tmpbuf()

        def fs(e):
            return e.activation(out=tmp[:, tb, 0:N], in_=ps[b1][:, 0:N], func=AF.Silu)
        pg.op("act", fs, reads=(("ps", b1),), writes=(("tmp", tb),))

        def fm(e):
            return e.tensor_tensor(out=act[:, jl, t0:t1], in0=tmp[:, tb, 0:N], in1=ps[b2][:, 0:N], op=ALU.mult)
        pg.op("dve", fm, reads=(("tmp", tb), ("ps", b2)), writes=(("act", jl, ti),))

    def ffn_piece(S, j):
        wg = w_gu[S["l"], S["f"]]
        return load_piece([(0, 8, wcols(wg, j * 128)), (8, 8, wcols(wg, DFF + j * 128))])

    def ffn_preload(S):
        S["fs"] = [ffn_piece(S, j) for j in range(4)]
        held.update(S["fs"])

    def ffn_first_tile(S, ti):
        for jl in range(4):
            ffn_chunk_tile(S["fs"][jl], jl, ti)

    def ffn_rest(S, boundary):
        l, s_ = S["l"], S["s"]
        wd = w_dn[l, S["f"]]
        out_phase(l, s_, wd, 0, 4, pre_a=S.get("opA"))
        for pi, (j0, nj) in enumerate(FFN_PARTS):
            if pi == 0:
                continue
            for jl in range(nj):
                s = ffn_piece(S, j0 + jl)
                for ti in range(NT):
                    ffn_chunk_tile(s, jl, ti)
                pump_mod(2)
            if pi == len(FFN_PARTS) - 1:
                out_phase(l, s_, wd, j0, nj, pre_loop=boundary[0], after_tile=boundary[1], tile_order=BORD)
            else:
                out_phase(l, s_, wd, j0, nj)

    def c_chunk_tile(S, c, cl, zi, s1, s2, ti):
        o = S["l"] // 2
        t0, t1 = TILES[ti]
        N = t1 - t0
        pcs = pieces(t0, t1)
        bb, bc, bv = bank(), bank(), bank()
        zk = ("ztl", zi)

        def mm(e):
            last = None
            for (bk, s, o8) in ((bb, s1, 0), (bc, s1, 8), (bv, s2, 0)):
                for k in range(KC):
                    last = e.matmul(ps[bk][:, 0:N], lhsT=slots[:, s, o8 + k, :], rhs=h[:, k, t0:t1],
                                    start=(k == 0), stop=(k == KC - 1))
            return last
        pg.op("pe", mm, reads=(("slot", s1), ("slot", s2)) + tuple(("h", k, ti) for k in range(KC)),
              writes=(("ps", bb), ("ps", bc), ("ps", bv)))
        pe_tick()
        tv = tmpbuf()

        def fv(e):
            return e.activation(out=tmp[:, tv, 0:N], in_=ps[bv][:, 0:N], func=AF.Copy)
        pg.op("act", fv, reads=(("ps", bv),), writes=(("tmp", tv),))
        offs = []
        off = 0
        for (sg, lo, hi) in pcs:
            off += 2
            offs.append(off)
            if sg == 0:
                if ti == 0:
                    pg.op("dve", lambda e: e.memset(ztl[:, zi, 0:2], 0.0), writes=(zk,))
                else:
                    pN = TILES[ti - 1][1] - TILES[ti - 1][0]
                    pg.op("dve", lambda e, pN=pN: e.tensor_copy(out=ztl[:, zi, 0:2], in_=ztl[:, zi, pN:pN + 2]),
                          reads=(zk,), writes=(zk,))
            else:
                pg.op("dve", lambda e, off=off, sg=sg: e.tensor_copy(out=ztl[:, zi, off - 2:off], in_=scs[:, o, sg - 1, c, :]),
                      reads=("st_c",), writes=(zk,))
            off += hi - lo
        tot = off
        for (sg, lo, hi), of in zip(pcs, offs):
            def fzz(e, lo=lo, hi=hi, of=of):
                return e.tensor_tensor(out=ztl[:, zi, of:of + hi - lo], in0=ps[bc][:, lo - t0:hi - t0],
                                       in1=tmp[:, tv, lo - t0:hi - t0], op=ALU.mult)
            pg.op("dve", fzz, reads=(("ps", bc), ("tmp", tv)), writes=(zk,))
        if ti == 0:
            pg.op("dve", lambda e: e.tensor_tensor(out=ztl[:, zi, 2:2 + HALO], in0=ztl[:, zi, 2:2 + HALO],
                                                   in1=P("hmask", HALO), op=ALU.mult), reads=("prm",), writes=(zk,))
        Wb = tot - 2
        ta, tb_ = tmpbuf(), tmpbuf()
        cw = _off["cconvw"] + (o * 8 + c) * 3

        def f0(e):
            return e.activation(out=tmp[:, ta, 0:Wb], in_=ztl[:, zi, 0:Wb], func=AF.Identity, scale=prm[:, cw:cw + 1])
        pg.op("act", f0, reads=(zk, "prm"), writes=(("tmp", ta),))

        def f1(e):
            return e.scalar_tensor_tensor(out=tmp[:, tb_, 0:Wb], in0=ztl[:, zi, 1:1 + Wb], scalar=prm[:, cw + 1:cw + 2],
                                          in1=tmp[:, ta, 0:Wb], op0=ALU.mult, op1=ALU.add)
        pg.op("dve", f1, reads=(zk, ("tmp", ta)), writes=(("tmp", tb_),))

        def f2(e):
            return e.scalar_tensor_tensor(out=tmp[:, ta, 0:Wb], in0=ztl[:, zi, 2:2 + Wb], scalar=prm[:, cw + 2:cw + 3],
                                          in1=tmp[:, tb_, 0:Wb], op0=ALU.mult, op1=ALU.add)
        pg.op("dve", f2, reads=(zk, ("tmp", tb_)), writes=(("tmp", ta),))
        tbg = tmpbuf()
        pg.op("act", lambda e: e.activation(out=tmp[:, tbg, 0:N], in_=ps[bb][:, 0:N], func=AF.Copy),
              reads=(("ps", bb),), writes=(("tmp", tbg),))
        for (sg, lo, hi), of in zip(pcs, offs):
            def fo(e, lo=lo, hi=hi, of=of):
                return e.tensor_tensor(out=act[:, cl, lo:hi], in0=tmp[:, ta, of - 2:of - 2 + hi - lo],
                                       in1=tmp[:, tbg, lo - t0:hi - t0], op=ALU.mult)
            pg.op("dve", fo, reads=(("tmp", ta), ("tmp", tbg)), writes=(("act", cl, ti),))
        if ti == NT - 1:
            for (sg, lo, hi), of in zip(pcs, offs):
                e0 = of + (hi - lo) - 2

                def fd(e, sg=sg, e0=e0):
                    return e.dma_start(out=oc_d[:, o, sg, c, :], in_=ztl[:, zi, e0:e0 + 2])
                pg.dma("sp", "outs", fd, reads=(zk,))

    def c_pieces(S, c):
        win = c_in[S["l"] // 2]
        s1 = load_piece([(0, 8, wcols(win, c * 128)), (8, 8, wcols(win, D + c * 128))])
        s2 = load_piece([(0, 8, wcols(win, 2 * D + c * 128))])
        return s1, s2

    def c_preload(S):
        S["fs"] = [c_pieces(S, 0), c_pieces(S, 1)]
        held.update(S["fs"][0] + S["fs"][1])

    def c_first_tile(S, ti):
        for c in range(2):
            c_chunk_tile(S, c, c, c, S["fs"][c][0], S["fs"][c][1], ti)

    def c_rest(S, boundary):
        l = S["l"]
        o = l // 2
        for hf in range(2):
            for cl in range(4):
                c = hf * 4 + cl
                if hf == 0 and cl < 2:
                    continue
                s1, s2 = c_pieces(S, c)
                for ti in range(NT):
                    c_chunk_tile(S, c, cl, 0, s1, s2, ti)
                pump_mod(2)
            if hf == 1:
                out_phase(l, 1, c_out[o], hf * 4, 4, pre_loop=boundary[0], after_tile=boundary[1], tile_order=BORD)
            else:
                out_phase(l, 1, c_out[o], hf * 4, 4, pre_a=S.get("opA"))

    def ab_preload(S):
        win = ab_in[S["l"] // 2]
        S["fs"] = [load_piece([(0, 8, wcols(win, c * 128)), (8, 8, wcols(win, 512 + c * 128))]) for c in range(4)]
        held.update(S["fs"])

    def ab_first_tile(S, ti):
        e_ = S["l"] // 2
        sA = S["fs"]
        t0, t1 = TILES[ti]
        N = t1 - t0
        pcs = pieces(t0, t1)
        offs = []
        off = 0
        for (sg, lo, hi) in pcs:
            off += 30
            offs.append(off)
            off += hi - lo
        tot = off
        Wb = tot - 30

        def inproj_glu(c):
            bu_, bg_ = bank(), bank()

            def mm(e, s=sA[c]):
                last = None
                for (bk, o8) in ((bu_, 0), (bg_, 8)):
                    for k in range(KC):
                        last = e.matmul(ps[bk][:, 0:N], lhsT=slots[:, s, o8 + k, :], rhs=h[:, k, t0:t1],
                                        start=(k == 0), stop=(k == KC - 1))
                return last
            pg.op("pe", mm, reads=(("slot", sA[c]),) + tuple(("h", k, ti) for k in range(KC)),
                  writes=(("ps", bu_), ("ps", bg_)))
            pe_tick()
            tsg = tmpbuf()
            pg.op("act", lambda e: e.activation(out=tmp[:, tsg, 0:N], in_=ps[bg_][:, 0:N], func=AF.Tanh, scale=0.5),
                  reads=(("ps", bg_),), writes=(("tmp", tsg),))
            ak = ("abf", c)
            for (sg, lo, hi), of in zip(pcs, offs):
                if sg == 0:
                    if ti == 0:
                        pg.op("dve", lambda e: e.memset(abf[:, c, 0:30], 0.0), writes=(ak,))
                    else:
                        pN = TILES[ti - 1][1] - TILES[ti - 1][0]
                        pg.op("dve", lambda e, pN=pN: e.tensor_copy(out=abf[:, c, 0:30], in_=abf[:, c, pN:pN + 30]),
                              reads=(ak,), writes=(ak,))
                else:
                    pg.op("dve", lambda e, of=of, sg=sg: e.tensor_scalar(out=abf[:, c, of - 30:of], in0=sa[:, e_, sg - 1, c, :],
                                                                         scalar1=2.0, scalar2=None, op0=ALU.mult),
                          reads=("st_a",), writes=(ak,))

                def fa(e, lo=lo, hi=hi, of=of):
                    return e.scalar_tensor_tensor(out=abf[:, c, of:of + hi - lo], in0=tmp[:, tsg, lo - t0:hi - t0], scalar=1.0,
                                                  in1=ps[bu_][:, lo - t0:hi - t0], op0=ALU.add, op1=ALU.mult)
                pg.op("dve", fa, reads=(("ps", bu_), ("tmp", tsg)), writes=(ak,))
                if ti == NT - 1:
                    def fst(e, hi=hi, sg=sg):
                        return e.scalar_tensor_tensor(out=astg[:, sg, c, :], in0=tmp[:, tsg, hi - 30 - t0:hi - t0], scalar=1.0,
                                                      in1=ps[bu_][:, hi - 30 - t0:hi - t0], op0=ALU.add, op1=ALU.mult)
                    pg.op("dve", fst, reads=(("ps", bu_), ("tmp", tsg)), writes=("astg",))
                    pg.op("dve", lambda e, sg=sg: e.tensor_scalar(out=astg[:, sg, c, :], in0=astg[:, sg, c, :], scalar1=0.5,
                                                                  scalar2=None, op0=ALU.mult), reads=("astg",), writes=("astg",))
            if ti == 0:
                pg.op("dve", lambda e: e.tensor_tensor(out=abf[:, c, 30:30 + HALO], in0=abf[:, c, 30:30 + HALO],
                                                       in1=P("hmask", HALO), op=ALU.mult), reads=("prm",), writes=(ak,))

        def gen(c):
            for hf_, (k0, k1) in enumerate(((0, 16), (16, 31))):
                def fld(e, k0=k0, k1=k1, hf_=hf_):
                    n = k1 - k0
                    return e.dma_start(out=dg[:, k0:k1, :].rearrange("p k j -> p (k j)"), in_=dgd[e_, c, hf_, :, 0:n * 128])
                pg.dma("sp", "dgl%d" % hf_, fld, reads=(("dgd", e_, hf_),), writes=(("dg", hf_),))

        def conv(c):
            ak = ("abf", c)
            bcv = bank()

            def mc0(e):
                last = None
                for k in range(16):
                    last = e.matmul(ps[bcv][:, 0:Wb], lhsT=dg[:, k, :], rhs=abf[:, c, k:k + Wb], start=(k == 0), stop=False)
                return last
            pg.op("pe", mc0, reads=(("dg", 0), ak), writes=(("ps", bcv),))

            def mc1(e):
                last = None
                for k in range(16, 31):
                    last = e.matmul(ps[bcv][:, 0:Wb], lhsT=dg[:, k, :], rhs=abf[:, c, k:k + Wb], start=False, stop=(k == 30))
                return last
            pg.op("pe", mc1, reads=(("dg", 1), ak), writes=(("ps", bcv),))
            pe_tick()
            cbo = _off["aconvb"] + e_ * 4 + c
            pg.op("act", lambda e: e.activation(out=cob[:, c, 0:Wb], in_=ps[bcv][:, 0:Wb], func=AF.Identity,
                                                bias=prm[:, cbo:cbo + 1], scale=1.0),
                  reads=(("ps", bcv), "prm"), writes=(("cob", c),))

        inproj_glu(0)
        gen(0)
        for c in range(4):
            if c + 1 < 4:
                inproj_glu(c + 1)
            conv(c)
            if c + 1 < 4:
                gen(c + 1)
        if ti == NT - 1:
            for (sg, lo, hi) in pcs:
                def fd(e, sg=sg):
                    return e.dma_start(out=oa_d[:, e_, sg, :, :], in_=astg[:, sg, :, :])
                pg.dma("sp", "outs", fd, reads=("astg",))

        def ln_tail():
            b1, b2 = bank(), bank()

            def mm1(e):
                last = None
                for c in range(4):
                    last = e.matmul(ps[b1][:, 0:Wb], lhsT=ones_f[:, :], rhs=cob[:, c, 0:Wb], start=(c == 0), stop=(c == 3))
                return last
            pg.op("pe", mm1, reads=tuple(("cob", c) for c in range(4)) + ("ones",), writes=(("ps", b1),))
            sqt = []
            for c in range(4):
                tq = tmpbuf()
                sqt.append(tq)
                pg.op("act", lambda e, c=c, tq=tq: e.activation(out=tmp[:, tq, 0:Wb], in_=cob[:, c, 0:Wb], func=AF.Square),
                      reads=(("cob", c),), writes=(("tmp", tq),))

            def mm2(e):
                last = None
                for c in range(4):
                    last = e.matmul(ps[b2][:, 0:Wb], lhsT=ones_f[:, :], rhs=tmp[:, sqt[c], 0:Wb], start=(c == 0), stop=(c == 3))
                return last
            pg.op("pe", mm2, reads=tuple(("tmp", tq) for tq in sqt) + ("ones",), writes=(("ps", b2),))
            pg.op("dve", lambda e: e.tensor_scalar(out=lnm[:, 0, 0:Wb], in0=ps[b1][:, 0:Wb], scalar1=1.0 / 512, scalar2=None, op0=ALU.mult),
                  reads=(("ps", b1),), writes=("lnmean",))
            tm = tmpbuf()
            pg.op("dve", lambda e: e.tensor_tensor(out=tmp[:, tm, 0:Wb], in0=lnm[:, 0, 0:Wb], in1=lnm[:, 0, 0:Wb], op=ALU.mult),
                  reads=("lnmean",), writes=(("tmp", tm),))
            pg.op("dve", lambda e: e.scalar_tensor_tensor(out=lnm[:, 1, 0:Wb], in0=ps[b2][:, 0:Wb], scalar=1.0 / 512, in1=tmp[:, tm, 0:Wb],
                                                          op0=ALU.mult, op1=ALU.subtract),
                  reads=(("ps", b2), ("tmp", tm)), writes=("lnrstd",))
            pg.op("act", lambda e: e.activation(out=lnm[:, 1, 0:Wb], in_=lnm[:, 1, 0:Wb], func=AF.Sqrt, bias=P("epsc", 1), scale=1.0),
                  reads=("lnrstd", "prm"), writes=("lnrstd",))
            pg.op("dve", lambda e: e.reciprocal(out=lnm[:, 1, 0:Wb], in_=lnm[:, 1, 0:Wb]), reads=("lnrstd",), writes=("lnrstd",))
            for c in range(4):
                t1_, t2_ = tmpbuf(), tmpbuf()
                pg.op("dve", lambda e, c=c, t1_=t1_: e.tensor_tensor(out=tmp[:, t1_, 0:Wb], in0=cob[:, c, 0:Wb], in1=lnm[:, 0, 0:Wb], op=ALU.subtract),
                      reads=(("cob", c), "lnmean"), writes=(("tmp", t1_),))
                pg.op("dve", lambda e, t1_=t1_, t2_=t2_: e.tensor_tensor(out=tmp[:, t2_, 0:Wb], in0=tmp[:, t1_, 0:Wb], in1=lnm[:, 1, 0:Wb], op=ALU.mult),
                      reads=(("tmp", t1_), "lnrstd"), writes=(("tmp", t2_),))
                go = _off["alng"] + e_ * 4 + c
                bo = _off["alnb"] + e_ * 4 + c
                for (sg, lo, hi), of in zip(pcs, offs):
                    def fsl(e, c=c, t2_=t2_, lo=lo, hi=hi, of=of, go=go, bo=bo):
                        return e.activation(out=act[:, c, lo:hi], in_=tmp[:, t2_, of - 30:of - 30 + hi - lo], func=AF.Silu,
                                            bias=prm[:, bo:bo + 1], scale=prm[:, go:go + 1])
                    pg.op("act", fsl, reads=(("tmp", t2_), "prm"), writes=(("act", c, ti),))
        pe_defer.append([2, ln_tail])

    def ab_rest(S, boundary):
        l = S["l"]
        e_ = l // 2
        win = ab_in[e_]
        out_phase(l, 1, ab_out[e_], 0, 4, pre_a=S.get("opA"))
        for g in range(4):
            wnd = 2 << g
            s = load_piece([(0, 8, wcols(win, 1024 + g * 128)), (8, 1, b_wg[e_, g].rearrange("p (o c) -> p o c", o=1))])
            for ti, (t0, t1) in enumerate(TILES):
                N = t1 - t0
                pcs = pieces(t0, t1)
                offs = []
                off = 0
                for (sg, lo, hi) in pcs:
                    off += 15
                    offs.append(off)
                    off += hi - lo
                tot = off
                Wb = tot - 15
                bb = bank()
                bi = st["bub"]
                st["bub"] = 1 - bi
                bk_ = ("bub", bi)

                def mm(e, s=s, bb=bb, t0=t0, t1=t1, N=N):
                    last = None
                    for k in range(KC):
                        last = e.matmul(ps[bb][:, 0:N], lhsT=slots[:, s, k, :], rhs=h[:, k, t0:t1], start=(k == 0), stop=(k == KC - 1))
                    return last
                pg.op("pe", mm, reads=(("slot", s),) + tuple(("h", k, ti) for k in range(KC)), writes=(("ps", bb),))
                pe_tick()
                for (sg, lo, hi), of in zip(pcs, offs):
                    if sg == 0:
                        if ti == 0:
                            pg.op("dve", lambda e, bi=bi: e.memset(bub[:, bi, 0:15], 0.0), writes=(bk_,))
                        else:
                            pN = TILES[ti - 1][1] - TILES[ti - 1][0]
                            pg.op("dve", lambda e, bi=bi, pN=pN: e.tensor_copy(out=bub[:, bi, 0:15], in_=bub[:, 1 - bi, pN:pN + 15]),
                                  reads=(("bub", 1 - bi),), writes=(bk_,))
                    else:
                        pg.op("dve", lambda e, bi=bi, of=of, sg=sg, g=g: e.tensor_copy(out=bub[:, bi, of - 15:of], in_=sbs[:, e_, sg - 1, g, :]),
                              reads=("st_b",), writes=(bk_,))
                    pg.op("act", lambda e, bi=bi, bb=bb, lo=lo, hi=hi, of=of, t0=t0: e.activation(
                        out=bub[:, bi, of:of + hi - lo], in_=ps[bb][:, lo - t0:hi - t0], func=AF.Copy),
                        reads=(("ps", bb),), writes=(bk_,))
                if ti == 0:
                    pg.op("dve", lambda e, bi=bi: e.tensor_tensor(out=bub[:, bi, 15:15 + HALO], in0=bub[:, bi, 15:15 + HALO],
                                                                  in1=P("hmask", HALO), op=ALU.mult), reads=("prm",), writes=(bk_,))
                cur = None
                sh = 1
                for lev in range(g + 1):
                    tn = tmpbuf()
                    if cur is None:
                        pg.op("dve", lambda e, bi=bi, tn=tn, sh=sh, tot=tot: e.tensor_tensor(
                            out=tmp[:, tn, sh:tot], in0=bub[:, bi, sh:tot], in1=bub[:, bi, 0:tot - sh], op=ALU.add),
                            reads=(bk_,), writes=(("tmp", tn),))
                    else:
                        pg.op("dve", lambda e, tn=tn, cur=cur, sh=sh, tot=tot: e.tensor_tensor(
                            out=tmp[:, tn, sh:tot], in0=tmp[:, cur, sh:tot], in1=tmp[:, cur, 0:tot - sh], op=ALU.add),
                            reads=(("tmp", cur),), writes=(("tmp", tn),))
                    cur = tn
                    sh *= 2
                di = st["dbf"]
                st["dbf"] = 1 - di
                pg.op("dve", lambda e, bi=bi, cur=cur, di=di, Wb=Wb, wnd=wnd: e.scalar_tensor_tensor(
                    out=dbf[:, di, 0:Wb], in0=tmp[:, cur, 15:15 + Wb], scalar=1.0 / wnd, in1=bub[:, bi, 15:15 + Wb],
                    op0=ALU.mult, op1=ALU.subtract), reads=(("tmp", cur), bk_), writes=(("dbf", di),))
                if ti == 0:
                    tf = tmpbuf()
                    io = _off["invcnt"] + g * 16
                    pg.op("dve", lambda e, cur=cur, tf=tf, io=io: e.tensor_tensor(
                        out=tmp[:, tf, 0:16], in0=tmp[:, cur, 15 + HALO:15 + HALO + 16], in1=prm[:, io:io + 16], op=ALU.mult),
                        reads=(("tmp", cur), "prm"), writes=(("tmp", tf),))
                    pg.op("dve", lambda e, bi=bi, tf=tf, di=di: e.tensor_tensor(
                        out=dbf[:, di, HALO:HALO + 16], in0=tmp[:, tf, 0:16], in1=bub[:, bi, 15 + HALO:15 + HALO + 16], op=ALU.subtract),
                        reads=(("tmp", tf), bk_), writes=(("dbf", di),))
                if ti == NT - 1:
                    for (sg, lo, hi), of in zip(pcs, offs):
                        e0 = of + (hi - lo) - 15

                        def fd(e, bi=bi, sg=sg, e0=e0, g=g):
                            return e.dma_start(out=ob_d[:, e_, sg, g, :], in_=bub[:, bi, e0:e0 + 15])
                        pg.dma("sp", "outs", fd, reads=(bk_,))

                def grp(s=s, di=di, Wb=Wb, g=g, ti=ti, pcs=pcs, offs=offs):
                    bd = bank()
                    pg.op("pe", lambda e: e.matmul(ps[bd][:, 0:Wb], lhsT=slots[:, s, 8, :], rhs=dbf[:, di, 0:Wb], start=True, stop=True),
                          reads=(("slot", s), ("dbf", di)), writes=(("ps", bd),))
                    so = _off["bscale"] + e_ * 4 + g
                    for (sg, lo, hi), of in zip(pcs, offs):
                        pg.op("act", lambda e, lo=lo, hi=hi, of=of: e.activation(
                            out=act[:, g, lo:hi], in_=ps[bd][:, of - 15:of - 15 + hi - lo], func=AF.Identity, scale=prm[:, so:so + 1]),
                            reads=(("ps", bd), "prm"), writes=(("act", g, ti),))
                pe_defer.append([1, grp])
            pump_mod(1)
        pe_flush()
        out_phase(l, 1, ab_out[e_], 4, 4, pre_loop=boundary[0], after_tile=boundary[1], tile_order=BORD)

    def f_prm(e):
        return e.dma_start(out=prm[:, :], in_=prm_d)
    pg.dma("sp", "ld_prm", f_prm, writes=("prm",))
    pg.dma("sp", "ld_cv", lambda e: e.dma_start(out=cv[:, :, :], in_=cv_d), writes=("cv",))
    pg.dma("sp", "ld_id", lambda e: e.dma_start(out=tmp[:, 0, 0:128], in_=id_d), writes=(("tmp", 0),))
    x_loads = []
    for ti, (t0, t1) in enumerate(TILES):
        def fxl(e, t0=t0, t1=t1):
            return e.dma_start(out=x[:, :, t0:t1], in_=xT_d[:, :, t0:t1])
        x_loads.append((ti, fxl))
    ti0, f0_ = x_loads[0]
    pg.dma("sp", "ld_x0", f0_, writes=tuple(("x", m, 0) for m in range(KC)))
    pg.dma("sp", "ld_sa", lambda e: e.dma_start(out=sa[:, :, :, :, :].rearrange("p a b c d -> p (a b c d)"), in_=sa_d), writes=("st_a",))
    pg.dma("sp", "ld_sb", lambda e: e.dma_start(out=sbs[:, :, :, :, :].rearrange("p a b c d -> p (a b c d)"), in_=sb_d), writes=("st_b",))
    pg.dma("sp", "ld_sc", lambda e: e.dma_start(out=scs[:, :, :, :, :].rearrange("p a b c d -> p (a b c d)"), in_=sc_d), writes=("st_c",))

    pg.op("dve", lambda e: e.memset(ones_bf[:, :], 1.0), writes=("ones",))
    pg.op("dve", lambda e: e.memset(ones_f[:, :], 1.0), writes=("ones",))
    pg.op("dve", lambda e: e.tensor_copy(out=ident_bf[:, :], in_=tmp[:, 0, 0:128]), reads=(("tmp", 0),), writes=("ident",))
    st["tmp"] = 1
    pg.op("act", lambda e: e.activation(out=cact[:, :, :], in_=cv[:, :, :], func=AF.Silu), reads=("cv",), writes=("cact",))

    def gen_diag(e2, anchor):
        o_ = _off["aconvw"] + e2 * 124
        pg.op("dve", lambda e, o_=o_: e.tensor_scalar(out=awh[:, :], in0=prm[:, o_:o_ + 124], scalar1=0.5, scalar2=None, op0=ALU.mult),
              reads=("prm", anchor), writes=("awh",))
        for c in range(4):
            for hf_, (k0, k1) in enumerate(((0, 16), (16, 31))):
                def fdg(e, c=c, k0=k0, k1=k1):
                    n = k1 - k0
                    return e.tensor_tensor(out=dg[:, k0:k1, :],
                                           in0=ident_bf[:, :].unsqueeze(1).to_broadcast([128, n, 128]),
                                           in1=awh[:, c * 31 + k0:c * 31 + k1].unsqueeze(2).to_broadcast([128, n, 128]),
                                           op=ALU.mult)
                pg.op("dve", fdg, reads=("ident", "awh"), writes=(("dg", hf_),))

                def fst_(e, e2=e2, c=c, k0=k0, k1=k1, hf_=hf_):
                    n = k1 - k0
                    return e.dma_start(out=dgd[e2, c, hf_, :, 0:n * 128], in_=dg[:, k0:k1, :].rearrange("p k j -> p (k j)"))
                pg.dma("sp", "dgw%d" % hf_, fst_, reads=(("dg", hf_),), writes=(("dgd", e2, hf_),))

    subs = []
    for l in range(DEPTH):
        subs.append({"kind": "ffn", "l": l, "s": 0, "f": 0})
        subs.append({"kind": "ab" if l % 2 == 0 else "c", "l": l, "s": 1})
        subs.append({"kind": "ffn", "l": l, "s": 2, "f": 1})
    PRE = {"ffn": ffn_preload, "ab": ab_preload, "c": c_preload}
    FIRST = {"ffn": ffn_first_tile, "ab": ab_first_tile, "c": c_first_tile}
    REST = {"ffn": ffn_rest, "ab": ab_rest, "c": c_rest}
    LAG = 2
    NDEF = 5

    for i in range(36):
        mod_pending.append((0, i))
    S0 = subs[0]
    for i in range(12):
        if i == 8:
            PRE["ffn"](S0)
        s_before = st["slot"]
        pump_mod(1)
        if 1 <= i <= 5:
            ti_, fx_ = x_loads[i]
            pg.dma("sp", "ld_x%d" % ti_, fx_, reads=(("slot", s_before),), writes=tuple(("x", m, ti_) for m in range(KC)))
    for i in range(36):
        mod_pending.append((1, i))

    for ti in range(NT):
        norm_tile(0, 0, ti)
        if ti >= 1:
            FIRST["ffn"](S0, ti - 1)
    FIRST["ffn"](S0, NT - 1)
    held.clear()

    for i, S in enumerate(subs):
        nxt = subs[i + 1] if i + 1 < len(subs) else None
        if i == 0:
            gen_diag(0, ("act", 0, NT - 1))
        if S["kind"] == "ffn" and S["s"] == 0 and S["l"] == 1:
            gen_diag(1, ("act", 0, NT - 1))
        if S["kind"] == "ffn" and S["s"] == 0 and S["l"] >= 1 and S["l"] + 1 < DEPTH:
            for i_ in range(36):
                mod_pending.append((S["l"] + 1, i_))

        def pre(nxt=nxt):
            if nxt is not None:
                PRE[nxt["kind"]](nxt)
                if nxt["kind"] == "ffn":
                    w0 = w_dn[nxt["l"], nxt["f"]]
                elif nxt["kind"] == "ab":
                    w0 = ab_out[nxt["l"] // 2]
                else:
                    w0 = c_out[nxt["l"] // 2]
                nxt["opA"] = load_piece([(0, 16, wcolhalf(w0, 0, 4, 0))])
                held.add(nxt["opA"])

        pos = [0]

        def cb(ti, nxt=nxt, pos=pos):
            p = pos[0]
            pos[0] += 1
            if nxt is None:
                norm_tile(DEPTH - 1, 0, ti, final=True, defer=3)
            else:
                norm_tile(nxt["l"], nxt["s"], ti, defer=NDEF)
                if p - 1 - LAG >= 0:
                    FIRST[nxt["kind"]](nxt, p - 1 - LAG)
        REST[S["kind"]](S, (pre, cb))
        pe_flush()
        if nxt is not None:
            for ti in range(NT - 1 - LAG, NT):
                FIRST[nxt["kind"]](nxt, ti)
        held.clear()
        pe_flush()

    pg.final_wait("sp", ["outs"])
    sim = pg.finalize()
    _SIM["time_us"] = sim
    _SIM["pg"] = pg
    _SIM["busy"] = {e: sum(pg.ops[i]["dur"] for i in pg.order[e]) for e in pg.ENGS}

    with nc.Block() as block:
        @block.tensor
        def _(e):
            pg.run("pe", e)

        @block.scalar
        def _(e):
            pg.run("act", e)

        @block.vector
        def _(e):
            pg.run("dve", e)

        @block.gpsimd
        def _(e):
            pg.run("pool", e)

        @block.sync
        def _(e):
            pg.run("sp", e)


def _fm(v):
    v = np.asarray(v, dtype=np.float32)
    n = v.shape[-1] // 128
    v = v.reshape(v.shape[:-1] + (n, 128))
    return np.ascontiguousarray(np.moveaxis(v, -1, 0))


_NC_CACHE = {}


def kernel(x_prompt, x_sample, state_conv_a, state_pool_b, state_conv_c, c_prompt, c_sample,
           ada_w, ada_b, norm_g, ffn_w_gu, ffn_w_down, ab_w_in, a_conv_w, a_conv_b, a_ln_g,
           a_ln_b, b_w_group, b_scale, ab_w_out, c_w_in, c_conv_w, c_w_out, final_g):
    f32 = np.float32
    xp = np.asarray(x_prompt, f32)[0]
    xs = np.asarray(x_sample, f32)
    prm_base = np.zeros((128, NPRM), f32)

    def put(name, arr):
        arr = np.ascontiguousarray(arr, dtype=f32).reshape(128, -1)
        prm_base[:, _off[name]:_off[name] + arr.shape[1]] = arr
    put("normg", _fm(norm_g))
    put("finalg", _fm(final_g))
    put("adab", _fm(ada_b))
    put("aconvw", np.transpose(_fm(a_conv_w), (0, 1, 3, 2)))
    put("aconvb", _fm(a_conv_b))
    put("alng", _fm(a_ln_g))
    put("alnb", _fm(a_ln_b))
    put("bscale", _fm(b_scale))
    put("cconvw", np.transpose(_fm(c_conv_w), (0, 1, 3, 2)))
    sca = np.asarray(state_conv_a, f32)
    spb = np.asarray(state_pool_b, f32)
    scc = np.asarray(state_conv_c, f32)
    cpr = np.asarray(c_prompt, f32)
    csa = np.asarray(c_sample, f32)
    shared = {
        "ada_w": np.ascontiguousarray(ada_w, dtype=f32), "ffn_w_gu": np.ascontiguousarray(ffn_w_gu, dtype=f32),
        "ffn_w_down": np.ascontiguousarray(ffn_w_down, dtype=f32), "ab_w_in": np.ascontiguousarray(ab_w_in, dtype=f32),
        "b_w_group": np.ascontiguousarray(b_w_group, dtype=f32), "ab_w_out": np.ascontiguousarray(ab_w_out, dtype=f32),
        "c_w_in": np.ascontiguousarray(c_w_in, dtype=f32), "c_w_out": np.ascontiguousarray(c_w_out, dtype=f32),
    }
    in_maps = []
    for i in range(NCORES):
        toks = np.zeros((T, D), f32)
        if i == 0:
            toks[HALO:HALO + MAIN] = xp[0:MAIN]
        else:
            toks[0:HALO + MAIN] = xp[i * MAIN - HALO:(i + 1) * MAIN]
        toks[HALO + MAIN:HALO + MAIN + LS] = xs[2 * i]
        toks[HALO + MAIN + LS:] = xs[2 * i + 1]
        xT = np.ascontiguousarray(toks.reshape(T, KC, 128).transpose(2, 1, 0))
        cvec = np.stack([cpr[0], csa[2 * i], csa[2 * i + 1]], 0)
        cvf = np.ascontiguousarray(cvec.reshape(3, KC, 128).transpose(2, 1, 0))
        prm = prm_base.copy()
        prm[:, _off["hmask"]:_off["hmask"] + HALO] = 0.0 if i == 0 else 1.0
        ic = np.zeros((4, 16), f32)
        for g in range(4):
            w = 2 << g
            for p_ in range(16):
                ic[g, p_] = (1.0 / min(w, p_ + 1)) if i == 0 else (1.0 / w)
        prm[:, _off["invcnt"]:_off["invcnt"] + 64] = ic.reshape(1, 64)
        prm[:, _off["epsc"]] = EPS
        sa_i = np.ascontiguousarray(np.transpose(_fm(sca[:, 2 * i:2 * i + 2]), (0, 1, 2, 4, 3))).reshape(128, -1)
        sb_i = np.ascontiguousarray(np.transpose(_fm(spb[:, 2 * i:2 * i + 2]), (0, 1, 2, 4, 3))).reshape(128, -1)
        sc_i = np.ascontiguousarray(np.transpose(_fm(scc[:, 2 * i:2 * i + 2]), (0, 1, 2, 4, 3))).reshape(128, -1)
        m = {"xT": xT, "cvec": cvf, "prm": prm, "sa": sa_i, "sb": sb_i, "scn": sc_i, "ident": np.eye(128, dtype=f32)}
        m.update(shared)
        in_maps.append(m)

    if "nc" not in _NC_CACHE:
        _NC_CACHE["nc"] = build_nc()
    nc = _NC_CACHE["nc"]
    res = run_bass_kernel_spmd(nc, in_maps, core_ids=list(range(NCORES)))
    R = res.results

    y_prompt = np.zeros((1, NCORES * MAIN, D), f32)
    y_sample = np.zeros((2 * NCORES, LS, D), f32)
    na_s = np.zeros((2, 2 * NCORES, 30, 512), f32)
    nb_s = np.zeros((2, 2 * NCORES, 15, 512), f32)
    nc_s = np.zeros((2, 2 * NCORES, 2, 1024), f32)
    for i in range(NCORES):
        yT = np.asarray(R[i]["yT"], f32)
        rows = yT.transpose(2, 1, 0).reshape(MAIN + 2 * LS, D)
        y_prompt[0, i * MAIN:(i + 1) * MAIN] = rows[:MAIN]
        y_sample[2 * i] = rows[MAIN:MAIN + LS]
        y_sample[2 * i + 1] = rows[MAIN + LS:]
        oa = np.asarray(R[i]["oa"], f32).transpose(1, 2, 4, 3, 0).reshape(2, 3, 30, 512)
        ob = np.asarray(R[i]["ob"], f32).transpose(1, 2, 4, 3, 0).reshape(2, 3, 15, 512)
        oc = np.asarray(R[i]["oc"], f32).transpose(1, 2, 4, 3, 0).reshape(2, 3, 2, 1024)
        na_s[:, 2 * i:2 * i + 2] = oa[:, 1:3]
        nb_s[:, 2 * i:2 * i + 2] = ob[:, 1:3]
        nc_s[:, 2 * i:2 * i + 2] = oc[:, 1:3]
        if i == NCORES - 1:
            na_p = oa[:, 0:1].copy()
            nb_p = ob[:, 0:1].copy()
            nc_p = oc[:, 0:1].copy()
    return (y_prompt, y_sample, na_p, nb_p, nc_p, na_s, nb_s, nc_s)
```

```python
import numpy as np
from contextlib import ExitStack
import concourse.bass as bass
import concourse.mybir as mybir
from concourse.bass_utils import run_bass_kernel_spmd

F32 = mybir.dt.float32
BF16 = mybir.dt.bfloat16
AF = mybir.ActivationFunctionType
ALU = mybir.AluOpType

NCORES = 8
D = 1024
KC = 8
DFF = 2816
NJ = 22
DEPTH = 4
HALO = 64
MAIN = 2048
LS = 32
T = HALO + MAIN + 2 * LS
SEGS = [(0, HALO + MAIN), (HALO + MAIN, HALO + MAIN + LS), (HALO + MAIN + LS, T)]
TILES = [(0, 384), (384, 768), (768, 1152), (1152, 1536), (1536, 1856), (1856, 2176)]
NMAX = 384
EPS = 1e-6
NSLOT = 6
SLOTW = 2048
ACTK = 4

_off = {}
_o = 0
for _n, _w in [("normg", 96), ("finalg", 8), ("adab", 288), ("aconvw", 248), ("aconvb", 8), ("alng", 8),
               ("alnb", 8), ("bscale", 8), ("cconvw", 48), ("hmask", 64), ("invcnt", 64), ("epsc", 1)]:
    _off[_n] = _o
    _o += _w
NPRM = _o


def pieces(t0, t1):
    out = []
    for s, (a, b) in enumerate(SEGS):
        lo, hi = max(a, t0), min(b, t1)
        if lo < hi:
            out.append((s, lo, hi))
    return out


class _FakeIns:
    def then_inc(self, *a, **k):
        return self


class _FakeEng:
    def __init__(self):
        self.cost = 0.0
        self.tag = None
        self.nbytes = 0

    @staticmethod
    def _free(ap):
        n = 1
        for d in ap.shape[1:]:
            n *= int(d)
        return n

    def matmul(self, out, lhsT=None, rhs=None, **k):
        c = max(self._free(out) / 2400.0 + 0.004, 0.035)
        if lhsT is not None and lhsT.dtype == F32:
            c *= 2.2
        self.cost += c
        return _FakeIns()

    def activation(self, out=None, in_=None, func=None, **k):
        self.cost += 0.22 + self._free(out) * 0.001
        if func in (AF.Silu, AF.Tanh):
            self.tag = "A"
        elif func == AF.Sqrt:
            self.tag = "B"
        return _FakeIns()

    def reciprocal(self, out=None, in_=None, **k):
        self.cost += 0.06 + self._free(out) * 0.0064
        return _FakeIns()

    def scalar_tensor_tensor(self, out=None, **k):
        self.cost += 0.06 + self._free(out) * 0.00146
        return _FakeIns()

    def dma_start(self, out=None, in_=None, **k):
        n = 1
        for d in out.shape:
            n *= int(d)
        self.nbytes += n * (4 if in_.dtype == F32 else 2)
        return _FakeIns()

    def __getattr__(self, name):
        def generic(*a, **k):
            out = k.get("out", a[0] if a else None)
            self.cost += 0.06 + (self._free(out) * 0.0012 if out is not None else 0.0)
            return _FakeIns()
        return generic


class Prog:
    ENGS = ("pe", "act", "dve", "pool", "sp")

    def __init__(self, nc, ctx):
        self.nc = nc
        self.ctx = ctx
        self.semh = {}
        for n in ("pe", "act", "dve"):
            self.semh[n] = ctx.enter_context(nc.semaphore("s_" + n))
        self.ops = []
        self.lastw = {}
        self.readers = {}
        self.final_sems = []
        self.order = None

    def dma_sem(self, name):
        if name not in self.semh:
            self.semh[name] = self.ctx.enter_context(self.nc.semaphore("d_" + name))
        return name

    def _add(self, eng, fn, sem, inc, reads, writes, dur, xfer, tag):
        idx = len(self.ops)
        deps = set()
        for k in reads:
            w = self.lastw.get(k)
            if w is not None:
                deps.add(w)
        for k in writes:
            w = self.lastw.get(k)
            if w is not None:
                deps.add(w)
            deps.update(self.readers.get(k, ()))
        deps.discard(idx)
        self.ops.append({"eng": eng, "fn": fn, "sem": sem, "inc": inc, "deps": deps, "dur": dur, "xfer": xfer, "tag": tag})
        for k in reads:
            self.readers.setdefault(k, []).append(idx)
        for k in writes:
            self.lastw[k] = idx
            self.readers[k] = []
        return idx

    def op(self, eng, fn, reads=(), writes=()):
        fe = _FakeEng()
        fn(fe)
        return self._add(eng, fn, eng, 1, reads, writes, fe.cost, None, fe.tag)

    def dma(self, queue, semname, fn, reads=(), writes=()):
        self.dma_sem(semname)
        fe = _FakeEng()
        fn(fe)
        issue = 1.0 if queue == "pool" else 0.15
        return self._add(queue, fn, semname, 16, reads, writes, issue, fe.nbytes, None)

    def final_wait(self, queue, semnames):
        self.final_sems = [(queue, s) for s in semnames]

    def finalize(self):
        import heapq
        ops = self.ops
        n = len(ops)
        nd = [len(o["deps"]) for o in ops]
        users = [[] for _ in range(n)]
        for i, o in enumerate(ops):
            for d in o["deps"]:
                users[d].append(i)
        LAT = 0.2
        done_t = [0.0] * n
        start_t = [0.0] * n
        free = {e: 0.0 for e in self.ENGS}
        pend = {e: [] for e in self.ENGS}
        avail = {e: [] for e in self.ENGS}
        act_set = [None]
        dma_pipe = [0.0]
        order = {e: [] for e in self.ENGS}
        for i, o in enumerate(ops):
            if nd[i] == 0:
                heapq.heappush(pend[o["eng"]], (0.0, i))
        left = n
        while left:
            best = None
            for e in self.ENGS:
                t = free[e]
                while pend[e] and pend[e][0][0] <= t:
                    heapq.heappush(avail[e], heapq.heappop(pend[e])[1])
                if avail[e]:
                    cand = (t, avail[e][0], e, True)
                elif pend[e]:
                    cand = (pend[e][0][0], pend[e][0][1], e, False)
                else:
                    continue
                if best is None or cand[:2] < best[:2]:
                    best = cand
            st_, i, e, from_avail = best
            if from_avail:
                heapq.heappop(avail[e])
            else:
                heapq.heappop(pend[e])
            o = ops[i]
            start_t[i] = st_
            dur = o["dur"]
            if e == "act" and o["tag"] is not None:
                if act_set[0] is not None and act_set[0] != o["tag"]:
                    dur += 1.8
                act_set[0] = o["tag"]
            if o["xfer"] is not None:
                free[e] = st_ + dur
                xs = max(st_ + dur, dma_pipe[0])
                dma_pipe[0] = xs + o["xfer"] / 200e3
                done_t[i] = dma_pipe[0] + 2.0
            else:
                free[e] = st_ + dur
                done_t[i] = st_ + dur
            order[e].append(i)
            left -= 1
            for u in users[i]:
                nd[u] -= 1
                if nd[u] == 0:
                    rt = max(done_t[d] for d in ops[u]["deps"]) + LAT
                    heapq.heappush(pend[ops[u]["eng"]], (rt, u))
        self.order = order
        self.start_t = start_t
        self.done_t = done_t
        self.sim_time = max(done_t) if n else 0.0
        cnt = {}
        self.val = [0] * n
        for e in self.ENGS:
            for i in order[e]:
                sname = ops[i]["sem"]
                cnt[sname] = cnt.get(sname, 0) + ops[i]["inc"]
                self.val[i] = cnt[sname]
        self.cnt = cnt
        return self.sim_time

    def run(self, eng_name, eng):
        ops = self.ops
        waited = {}
        for i in self.order[eng_name]:
            o = ops[i]
            need = {}
            for d in o["deps"]:
                sd = ops[d]["sem"]
                v = self.val[d]
                if need.get(sd, 0) < v:
                    need[sd] = v
            for sd, v in need.items():
                if waited.get(sd, 0) < v:
                    waited[sd] = v
                    eng.wait_ge(self.semh[sd], v)
            ins = o["fn"](eng)
            ins.then_inc(self.semh[o["sem"]], o["inc"])
        for (q, sname) in self.final_sems:
            if q == eng_name and self.cnt.get(sname, 0) > 0:
                eng.wait_ge(self.semh[sname], self.cnt[sname])


_SIM = {}


def build_nc():
    nc = bass.Bass("TRN2", target_bir_lowering=False)
    ctx = ExitStack()
    with ctx:
        _build(nc, ctx)
    return nc


def _build(nc, ctx):
    def din(name, shape):
        return nc.dram_tensor(name, list(shape), F32, kind="ExternalInput").ap()

    def dout(name, shape):
        return nc.dram_tensor(name, list(shape), F32, kind="ExternalOutput").ap()

    xT_d = din("xT", [128, KC, T])
    cv_d = din("cvec", [128, KC, 3])
    prm_d = din("prm", [128, NPRM])
    sa_d = din("sa", [128, 2 * 2 * 4 * 30])
    sb_d = din("sb", [128, 2 * 2 * 4 * 15])
    sc_d = din("scn", [128, 2 * 2 * 8 * 2])
    id_d = din("ident", [128, 128])
    ada_w = din("ada_w", [DEPTH, D, 9 * D])
    w_gu = din("ffn_w_gu", [DEPTH, 2, D, 2 * DFF])
    w_dn = din("ffn_w_down", [DEPTH, 2, DFF, D])
    ab_in = din("ab_w_in", [2, D, 1536])
    b_wg = din("b_w_group", [2, 4, 128, 128])
    ab_out = din("ab_w_out", [2, D, D])
    c_in = din("c_w_in", [2, D, 3 * D])
    c_out = din("c_w_out", [2, D, D])
    yT_d = dout("yT", [128, KC, MAIN + 2 * LS])
    oa_d = dout("oa", [128, 2, 3, 4, 30])
    ob_d = dout("ob", [128, 2, 3, 4, 15])
    oc_d = dout("oc", [128, 2, 3, 8, 2])
    dgd = nc.dram_tensor("dgd", [2, 4, 2, 128, 16 * 128], BF16, kind="Internal").ap()

    def sb_(name, shape, dt=F32):
        return ctx.enter_context(nc.sbuf_tensor(name, list(shape), dt))

    NT = len(TILES)
    x = sb_("x", [128, KC, T])
    h = sb_("h", [128, KC, T], BF16)
    act = sb_("act", [128, ACTK, T], BF16)
    slots = sb_("slots", [128, NSLOT, 16, 128], BF16)
    sqb = sb_("sqb", [128, 2, KC, NMAX], BF16)
    rbuf = sb_("rbuf", [128, 2, NMAX])
    NTMP = 6
    TMPW = 400
    tmp = sb_("tmp", [128, NTMP, TMPW])
    abf = sb_("abf", [128, 4, 30 + 384], BF16)
    astg = sb_("astg", [128, 3, 4, 30])
    dg = sb_("dg", [128, 31, 128], BF16)
    cob = sb_("cob", [128, 4, 384])
    lnm = sb_("lnm", [128, 2, 384])
    bub = sb_("bub", [128, 2, 45 + 384])
    dbf = sb_("dbf", [128, 2, 384], BF16)
    ztl = sb_("ztl", [128, 2, 6 + 384])
    prm = sb_("prm_s", [128, NPRM])
    cv = sb_("cv_s", [128, KC, 3])
    cact = sb_("cact", [128, KC, 3], BF16)
    sa = sb_("sa_s", [128, 2, 2, 4, 30])
    sbs = sb_("sb_s", [128, 2, 2, 4, 15])
    scs = sb_("sc_s", [128, 2, 2, 8, 2])
    modfm = sb_("modfm", [128, 2, 72, 3])
    gsb = sb_("gsb", [128, 2, 3, KC, 3])
    ghb = sb_("ghb", [128, 2, 3, KC, 3])
    ones_bf = sb_("ones_bf", [128, 128], BF16)
    ones_f = sb_("ones_f", [128, 128])
    ident_bf = sb_("ident_bf", [128, 128], BF16)
    awh = sb_("awh", [128, 124])
    ps = [ctx.enter_context(nc.psum_tensor("ps%d" % i, [128, 512], F32)) for i in range(8)]

    pg = Prog(nc, ctx)
    st = {"bank": 0, "slot": 0, "tmp": 0, "sq": 0, "r": 0, "dbf": 0, "bub": 0}

    def bank():
        b = st["bank"]
        st["bank"] = (b + 1) % 8
        return b

    def tmpbuf():
        i = st["tmp"]
        st["tmp"] = (i + 1) % NTMP
        return i

    def P(name, w=1, i=0):
        o = _off[name] + i
        return prm[:, o:o + w]

    held = set()

    def alloc_slot():
        for _ in range(NSLOT):
            s = st["slot"]
            st["slot"] = (s + 1) % NSLOT
            if s not in held:
                return s
        raise RuntimeError("no free weight slot")

    def load_piece(parts):
        s = alloc_slot()
        for (b0, nb, src) in parts:
            def f(e, s=s, b0=b0, nb=nb, src=src):
                dst = slots[:, s, b0:b0 + nb, :]
                if len(src.shape) == 3 and src.shape[2] == 1024:
                    dst = dst.rearrange("p (j m) c -> p j (m c)", m=8)
                return e.dma_start(out=dst, in_=src)
            pg.dma("pool", "slot%d" % s, f, reads=(), writes=(("slot", s),))
        return s

    def wcols(w2d, c0, n=128):
        return w2d.rearrange("(kc p) n -> p kc n", p=128)[:, :, c0:c0 + n]

    def wrows(w2d, r0, nr):
        return w2d.rearrange("(j p) n -> p j n", p=128)[:, r0:r0 + nr, :]

    pe_defer = []

    def pe_tick():
        for d in list(pe_defer):
            d[0] -= 1
            if d[0] <= 0:
                pe_defer.remove(d)
                d[1]()

    def pe_flush():
        while pe_defer:
            d = pe_defer.pop(0)
            d[1]()

    mod_pending = []

    def mod_piece(l, i):
        src = ada_w[l].rearrange("(kc p) n -> p kc n", p=128)[:, :, i * 256:(i + 1) * 256]
        s = alloc_slot()

        def f(e, s=s, src=src):
            return e.dma_start(out=slots[:, s, :, :].rearrange("p (k a) c -> p k (a c)", a=2), in_=src)
        pg.dma("pool", "slot%d" % s, f, writes=(("slot", s),))
        b = bank()

        def mm(e, s=s, b=b):
            last = None
            for ql in range(2):
                for k in range(KC):
                    last = e.matmul(ps[b][:, ql * 3:ql * 3 + 3], lhsT=slots[:, s, k * 2 + ql, :],
                                    rhs=cact[:, k, :], start=(k == 0), stop=(k == KC - 1))
            return last
        pg.op("pe", mm, reads=(("slot", s), "cact"), writes=(("ps", b),))
        for ql in range(2):
            q = 2 * i + ql

            def ev(e, b=b, ql=ql, q=q, l=l):
                return e.activation(out=modfm[:, l % 2, q, :], in_=ps[b][:, ql * 3:ql * 3 + 3],
                                    func=AF.Identity, bias=P("adab", 1, l * 72 + q), scale=1.0)
            pg.op("act", ev, reads=(("ps", b), "prm"), writes=(("mod", l % 2, q // 8),))

    def mod_derive(l, s_):
        lb = l % 2
        for m in range(KC):
            def f(e, m=m):
                o = _off["normg"] + (l * 3 + s_) * 8 + m
                return e.tensor_scalar(out=gsb[:, lb, s_, m, :], in0=modfm[:, lb, (3 * s_ + 1) * 8 + m, :],
                                       scalar1=1.0, scalar2=prm[:, o:o + 1], op0=ALU.add, op1=ALU.mult)
            pg.op("dve", f, reads=(("mod", lb, 3 * s_ + 1), "prm"), writes=(("gs", lb, s_),))

        def f2(e):
            return e.tensor_scalar(out=ghb[:, lb, s_, :, :], in0=modfm[:, lb, (3 * s_ + 2) * 8:(3 * s_ + 3) * 8, :],
                                   scalar1=(1.0 if s_ == 1 else 0.5), scalar2=None, op0=ALU.mult)
        pg.op("dve", f2, reads=(("mod", lb, 3 * s_ + 2),), writes=(("gh", lb, s_),))

    def pump_mod(n=1):
        for _ in range(n):
            if mod_pending:
                l, i = mod_pending.pop(0)
                mod_piece(l, i)
                if i % 12 == 11:
                    mod_derive(l, i // 12)

    def norm_tile(l, s_, ti, final=False, defer=0):
        t0, t1 = TILES[ti]
        N = t1 - t0
        lb = l % 2
        sq = st["sq"]
        st["sq"] = 1 - sq
        for m in range(KC):
            def f(e, m=m):
                return e.activation(out=sqb[:, sq, m, 0:N], in_=x[:, m, t0:t1], func=AF.Square)
            pg.op("act", f, reads=(("x", m, ti),), writes=(("sqb", sq),))

        def rest():
            b = bank()
            ri = st["r"]
            st["r"] = 1 - ri

            def mm(e):
                last = None
                for m in range(KC):
                    last = e.matmul(ps[b][:, 0:N], lhsT=ones_bf[:, :], rhs=sqb[:, sq, m, 0:N],
                                    start=(m == 0), stop=(m == KC - 1))
                return last
            pg.op("pe", mm, reads=(("sqb", sq), "ones"), writes=(("ps", b),))

            def fr0(e):
                return e.activation(out=rbuf[:, ri, 0:N], in_=ps[b][:, 0:N], func=AF.Sqrt, bias=P("epsc", 1), scale=1.0 / D)
            pg.op("act", fr0, reads=(("ps", b), "prm"), writes=(("r", ri),))

            def fr(e):
                return e.reciprocal(out=rbuf[:, ri, 0:N], in_=rbuf[:, ri, 0:N])
            pg.op("dve", fr, reads=(("r", ri),), writes=(("r", ri),))
            for m in range(KC):
                if final:
                    def fy(e, m=m):
                        o = _off["finalg"] + m
                        return e.scalar_tensor_tensor(out=x[:, m, t0:t1], in0=x[:, m, t0:t1], scalar=prm[:, o:o + 1],
                                                      in1=rbuf[:, ri, 0:N], op0=ALU.mult, op1=ALU.mult)
                    pg.op("dve", fy, reads=(("r", ri), "prm"), writes=(("x", m, ti),))
                    continue
                tb = tmpbuf()

                def ft(e, m=m, tb=tb):
                    return e.tensor_tensor(out=tmp[:, tb, 0:N], in0=x[:, m, t0:t1], in1=rbuf[:, ri, 0:N], op=ALU.mult)
                pg.op("dve", ft, reads=(("x", m, ti), ("r", ri)), writes=(("tmp", tb),))
                pcs_ = pieces(t0, t1)
                if len(pcs_) == 3:
                    a0 = pcs_[1][1] - t0

                    def fs1(e, m=m, tb=tb, a0=a0):
                        v = tmp[:, tb, a0:a0 + 2 * LS].rearrange("p (s t) -> p s t", t=LS)
                        return e.tensor_tensor(out=v, in0=v, in1=gsb[:, lb, s_, m, 1:3].unsqueeze(2).to_broadcast([128, 2, LS]),
                                               op=ALU.mult)
                    pg.op("dve", fs1, reads=(("tmp", tb), ("gs", lb, s_)), writes=(("tmp", tb),))

                    def fs2(e, m=m, tb=tb, a0=a0):
                        v = tmp[:, tb, a0:a0 + 2 * LS].rearrange("p (s t) -> p s t", t=LS)
                        o_ = h[:, m, t0 + a0:t0 + a0 + 2 * LS].rearrange("p (s t) -> p s t", t=LS)
                        return e.tensor_tensor(out=o_, in0=v,
                                               in1=modfm[:, lb, (3 * s_) * 8 + m, 1:3].unsqueeze(2).to_broadcast([128, 2, LS]),
                                               op=ALU.add)
                    pg.op("dve", fs2, reads=(("tmp", tb), ("mod", lb, 3 * s_)), writes=(("h", m, ti),))
                    pcs_ = pcs_[:1]
                for (sg, lo, hi) in pcs_:
                    def fh(e, m=m, tb=tb, sg=sg, lo=lo, hi=hi):
                        return e.activation(out=h[:, m, lo:hi], in_=tmp[:, tb, lo - t0:hi - t0], func=AF.Identity,
                                            bias=modfm[:, lb, (3 * s_) * 8 + m, sg:sg + 1],
                                            scale=gsb[:, lb, s_, m, sg:sg + 1])
                    pg.op("act", fh, reads=(("tmp", tb), ("gs", lb, s_), ("mod", lb, 3 * s_)), writes=(("h", m, ti),))
            if final:
                lo_ = max(t0, HALO)

                def fyo(e):
                    return e.dma_start(out=yT_d[:, :, lo_ - HALO:t1 - HALO], in_=x[:, :, lo_:t1])
                pg.dma("sp", "outs", fyo, reads=tuple(("x", m, ti) for m in range(KC)))
        if defer > 0:
            pe_defer.append([defer, rest])
        else:
            rest()

    def out_phase(l, s_, wsrc2d, r0, nk, pre_loop=None, after_tile=None):
        lb = l % 2
        sl = []
        k = 0
        while k < nk:
            n = min(2, nk - k)
            sl.append((load_piece([(0, n * 8, wrows(wsrc2d, r0 + k, n))]), k, n))
            k += n
        if pre_loop is not None:
            pre_loop()
        for ti, (t0, t1) in enumerate(TILES):
            N = t1 - t0
            for m in range(KC):
                b = bank()

                def mm(e, m=m, b=b, t0=t0, t1=t1, N=N):
                    last = None
                    kk = 0
                    for (s, k0, n) in sl:
                        for j in range(n):
                            last = e.matmul(ps[b][:, 0:N], lhsT=slots[:, s, j * 8 + m, :], rhs=act[:, k0 + j, t0:t1],
                                            start=(kk == 0), stop=(kk == nk - 1))
                            kk += 1
                    return last
                pg.op("pe", mm, reads=tuple(("slot", s) for (s, _, _) in sl) + tuple(("act", k_, ti) for k_ in range(nk)),
                      writes=(("ps", b),))
                pe_tick()
                for (sg, lo, hi) in pieces(t0, t1):
                    def fx(e, m=m, b=b, sg=sg, lo=lo, hi=hi, t0=t0):
                        return e.scalar_tensor_tensor(out=x[:, m, lo:hi], in0=ps[b][:, lo - t0:hi - t0],
                                                      scalar=ghb[:, lb, s_, m, sg:sg + 1], in1=x[:, m, lo:hi],
                                                      op0=ALU.mult, op1=ALU.add)
                    pg.op("dve", fx, reads=(("ps", b), ("gh", lb, s_)), writes=(("x", m, ti),))
            if after_tile is not None:
                after_tile(ti)

    FFN_PARTS = [(0, 4), (4, 4), (8, 4), (12, 4), (16, 3), (19, 3)]

    def ffn_chunk_tile(s, jl, ti):
        t0, t1 = TILES[ti]
        N = t1 - t0
        b1, b2 = bank(), bank()

        def mm(e):
            last = None
            for k in range(KC):
                e.matmul(ps[b1][:, 0:N], lhsT=slots[:, s, k, :], rhs=h[:, k, t0:t1], start=(k == 0), stop=(k == KC - 1))
            for k in range(KC):
                last = e.matmul(ps[b2][:, 0:N], lhsT=slots[:, s, 8 + k, :], rhs=h[:, k, t0:t1],
                                start=(k == 0), stop=(k == KC - 1))
            return last
        pg.op("pe", mm, reads=(("slot", s),) + tuple(("h", k, ti) for k in range(KC)), writes=(("ps", b1), ("ps", b2)))
        pe_tick()
        tb = tmpbuf()

        def fs(e):
            return e.activation(out=tmp[:, tb, 0:N], in_=ps[b1][:, 0:N], func=AF.Silu)
        pg.op("act", fs, reads=(("ps", b1),), writes=(("tmp", tb),))

        def fm(e):
            return e.tensor_tensor(out=act[:, jl, t0:t1], in0=tmp[:, tb, 0:N], in1=ps[b2][:, 0:N], op=ALU.mult)
        pg.op("dve", fm, reads=(("tmp", tb), ("ps", b2)), writes=(("act", jl, ti),))

    def ffn_piece(S, j):
        wg = w_gu[S["l"], S["f"]]
        return load_piece([(0, 8, wcols(wg, j * 128)), (8, 8, wcols(wg, DFF + j * 128))])

    def ffn_preload(S):
        S["fs"] = [ffn_piece(S, j) for j in range(4)]
        held.update(S["fs"])

    def ffn_first_tile(S, ti):
        for jl in range(4):
            ffn_chunk_tile(S["fs"][jl], jl, ti)

    def ffn_rest(S, boundary):
        l, s_ = S["l"], S["s"]
        wd = w_dn[l, S["f"]]
        out_phase(l, s_, wd, 0, 4)
        for pi, (j0, nj) in enumerate(FFN_PARTS):
            if pi == 0:
                continue
            for jl in range(nj):
                s = ffn_piece(S, j0 + jl)
                for ti in range(NT):
                    ffn_chunk_tile(s, jl, ti)
                pump_mod(2)
            if pi == len(FFN_PARTS) - 1:
                out_phase(l, s_, wd, j0, nj, pre_loop=boundary[0], after_tile=boundary[1])
            else:
                out_phase(l, s_, wd, j0, nj)

    def c_chunk_tile(S, c, cl, zi, s1, s2, ti):
        o = S["l"] // 2
        t0, t1 = TILES[ti]
        N = t1 - t0
        pcs = pieces(t0, t1)
        bb, bc, bv = bank(), bank(), bank()
        zk = ("ztl", zi)

        def mm(e):
            last = None
            for (bk, s, o8) in ((bb, s1, 0), (bc, s1, 8), (bv, s2, 0)):
                for k in range(KC):
                    last = e.matmul(ps[bk][:, 0:N], lhsT=slots[:, s, o8 + k, :], rhs=h[:, k, t0:t1],
                                    start=(k == 0), stop=(k == KC - 1))
            return last
        pg.op("pe", mm, reads=(("slot", s1), ("slot", s2)) + tuple(("h", k, ti) for k in range(KC)),
              writes=(("ps", bb), ("ps", bc), ("ps", bv)))
        pe_tick()
        tv = tmpbuf()

        def fv(e):
            return e.activation(out=tmp[:, tv, 0:N], in_=ps[bv][:, 0:N], func=AF.Copy)
        pg.op("act", fv, reads=(("ps", bv),), writes=(("tmp", tv),))
        offs = []
        off = 0
        for (sg, lo, hi) in pcs:
            off += 2
            offs.append(off)
            if sg == 0:
                if ti == 0:
                    pg.op("dve", lambda e: e.memset(ztl[:, zi, 0:2], 0.0), writes=(zk,))
                else:
                    pN = TILES[ti - 1][1] - TILES[ti - 1][0]
                    pg.op("dve", lambda e, pN=pN: e.tensor_copy(out=ztl[:, zi, 0:2], in_=ztl[:, zi, pN:pN + 2]),
                          reads=(zk,), writes=(zk,))
            else:
                pg.op("dve", lambda e, off=off, sg=sg: e.tensor_copy(out=ztl[:, zi, off - 2:off], in_=scs[:, o, sg - 1, c, :]),
                      reads=("st_c",), writes=(zk,))
            off += hi - lo
        tot = off
        for (sg, lo, hi), of in zip(pcs, offs):
            def fzz(e, lo=lo, hi=hi, of=of):
                return e.tensor_tensor(out=ztl[:, zi, of:of + hi - lo], in0=ps[bc][:, lo - t0:hi - t0],
                                       in1=tmp[:, tv, lo - t0:hi - t0], op=ALU.mult)
            pg.op("dve", fzz, reads=(("ps", bc), ("tmp", tv)), writes=(zk,))
        if ti == 0:
            pg.op("dve", lambda e: e.tensor_tensor(out=ztl[:, zi, 2:2 + HALO], in0=ztl[:, zi, 2:2 + HALO],
                                                   in1=P("hmask", HALO), op=ALU.mult), reads=("prm",), writes=(zk,))
        Wb = tot - 2
        ta, tb_ = tmpbuf(), tmpbuf()
        cw = _off["cconvw"] + (o * 8 + c) * 3

        def f0(e):
            return e.activation(out=tmp[:, ta, 0:Wb], in_=ztl[:, zi, 0:Wb], func=AF.Identity, scale=prm[:, cw:cw + 1])
        pg.op("act", f0, reads=(zk, "prm"), writes=(("tmp", ta),))

        def f1(e):
            return e.scalar_tensor_tensor(out=tmp[:, tb_, 0:Wb], in0=ztl[:, zi, 1:1 + Wb], scalar=prm[:, cw + 1:cw + 2],
                                          in1=tmp[:, ta, 0:Wb], op0=ALU.mult, op1=ALU.add)
        pg.op("dve", f1, reads=(zk, ("tmp", ta)), writes=(("tmp", tb_),))

        def f2(e):
            return e.scalar_tensor_tensor(out=tmp[:, ta, 0:Wb], in0=ztl[:, zi, 2:2 + Wb], scalar=prm[:, cw + 2:cw + 3],
                                          in1=tmp[:, tb_, 0:Wb], op0=ALU.mult, op1=ALU.add)
        pg.op("dve", f2, reads=(zk, ("tmp", tb_)), writes=(("tmp", ta),))
        tbg = tmpbuf()
        pg.op("act", lambda e: e.activation(out=tmp[:, tbg, 0:N], in_=ps[bb][:, 0:N], func=AF.Copy),
              reads=(("ps", bb),), writes=(("tmp", tbg),))
        for (sg, lo, hi), of in zip(pcs, offs):
            def fo(e, lo=lo, hi=hi, of=of):
                return e.tensor_tensor(out=act[:, cl, lo:hi], in0=tmp[:, ta, of - 2:of - 2 + hi - lo],
                                       in1=tmp[:, tbg, lo - t0:hi - t0], op=ALU.mult)
            pg.op("dve", fo, reads=(("tmp", ta), ("tmp", tbg)), writes=(("act", cl, ti),))
        if ti == NT - 1:
            for (sg, lo, hi), of in zip(pcs, offs):
                e0 = of + (hi - lo) - 2

                def fd(e, sg=sg, e0=e0):
                    return e.dma_start(out=oc_d[:, o, sg, c, :], in_=ztl[:, zi, e0:e0 + 2])
                pg.dma("sp", "outs", fd, reads=(zk,))

    def c_pieces(S, c):
        win = c_in[S["l"] // 2]
        s1 = load_piece([(0, 8, wcols(win, c * 128)), (8, 8, wcols(win, D + c * 128))])
        s2 = load_piece([(0, 8, wcols(win, 2 * D + c * 128))])
        return s1, s2

    def c_preload(S):
        S["fs"] = [c_pieces(S, 0), c_pieces(S, 1)]
        held.update(S["fs"][0] + S["fs"][1])

    def c_first_tile(S, ti):
        for c in range(2):
            c_chunk_tile(S, c, c, c, S["fs"][c][0], S["fs"][c][1], ti)

    def c_rest(S, boundary):
        l = S["l"]
        o = l // 2
        for hf in range(2):
            for cl in range(4):
                c = hf * 4 + cl
                if hf == 0 and cl < 2:
                    continue
                s1, s2 = c_pieces(S, c)
                for ti in range(NT):
                    c_chunk_tile(S, c, cl, 0, s1, s2, ti)
                pump_mod(2)
            if hf == 1:
                out_phase(l, 1, c_out[o], hf * 4, 4, pre_loop=boundary[0], after_tile=boundary[1])
            else:
                out_phase(l, 1, c_out[o], hf * 4, 4)

    def ab_preload(S):
        win = ab_in[S["l"] // 2]
        S["fs"] = [load_piece([(0, 8, wcols(win, c * 128)), (8, 8, wcols(win, 512 + c * 128))]) for c in range(4)]
        held.update(S["fs"])

    def ab_first_tile(S, ti):
        e_ = S["l"] // 2
        sA = S["fs"]
        t0, t1 = TILES[ti]
        N = t1 - t0
        pcs = pieces(t0, t1)
        offs = []
        off = 0
        for (sg, lo, hi) in pcs:
            off += 30
            offs.append(off)
            off += hi - lo
        tot = off
        Wb = tot - 30

        def inproj_glu(c):
            bu_, bg_ = bank(), bank()

            def mm(e, s=sA[c]):
                last = None
                for (bk, o8) in ((bu_, 0), (bg_, 8)):
                    for k in range(KC):
                        last = e.matmul(ps[bk][:, 0:N], lhsT=slots[:, s, o8 + k, :], rhs=h[:, k, t0:t1],
                                        start=(k == 0), stop=(k == KC - 1))
                return last
            pg.op("pe", mm, reads=(("slot", sA[c]),) + tuple(("h", k, ti) for k in range(KC)),
                  writes=(("ps", bu_), ("ps", bg_)))
            pe_tick()
            tsg = tmpbuf()
            pg.op("act", lambda e: e.activation(out=tmp[:, tsg, 0:N], in_=ps[bg_][:, 0:N], func=AF.Tanh, scale=0.5),
                  reads=(("ps", bg_),), writes=(("tmp", tsg),))
            ak = ("abf", c)
            for (sg, lo, hi), of in zip(pcs, offs):
                if sg == 0:
                    if ti == 0:
                        pg.op("dve", lambda e: e.memset(abf[:, c, 0:30], 0.0), writes=(ak,))
                    else:
                        pN = TILES[ti - 1][1] - TILES[ti - 1][0]
                        pg.op("dve", lambda e, pN=pN: e.tensor_copy(out=abf[:, c, 0:30], in_=abf[:, c, pN:pN + 30]),
                              reads=(ak,), writes=(ak,))
                else:
                    pg.op("dve", lambda e, of=of, sg=sg: e.tensor_scalar(out=abf[:, c, of - 30:of], in0=sa[:, e_, sg - 1, c, :],
                                                                         scalar1=2.0, scalar2=None, op0=ALU.mult),
                          reads=("st_a",), writes=(ak,))

                def fa(e, lo=lo, hi=hi, of=of):
                    return e.scalar_tensor_tensor(out=abf[:, c, of:of + hi - lo], in0=tmp[:, tsg, lo - t0:hi - t0], scalar=1.0,
                                                  in1=ps[bu_][:, lo - t0:hi - t0], op0=ALU.add, op1=ALU.mult)
                pg.op("dve", fa, reads=(("ps", bu_), ("tmp", tsg)), writes=(ak,))
                if ti == NT - 1:
                    def fst(e, hi=hi, sg=sg):
                        return e.scalar_tensor_tensor(out=astg[:, sg, c, :], in0=tmp[:, tsg, hi - 30 - t0:hi - t0], scalar=1.0,
                                                      in1=ps[bu_][:, hi - 30 - t0:hi - t0], op0=ALU.add, op1=ALU.mult)
                    pg.op("dve", fst, reads=(("ps", bu_), ("tmp", tsg)), writes=("astg",))
                    pg.op("dve", lambda e, sg=sg: e.tensor_scalar(out=astg[:, sg, c, :], in0=astg[:, sg, c, :], scalar1=0.5,
                                                                  scalar2=None, op0=ALU.mult), reads=("astg",), writes=("astg",))
            if ti == 0:
                pg.op("dve", lambda e: e.tensor_tensor(out=abf[:, c, 30:30 + HALO], in0=abf[:, c, 30:30 + HALO],
                                                       in1=P("hmask", HALO), op=ALU.mult), reads=("prm",), writes=(ak,))

        def gen(c):
            for hf_, (k0, k1) in enumerate(((0, 16), (16, 31))):
                def fld(e, k0=k0, k1=k1, hf_=hf_):
                    n = k1 - k0
                    return e.dma_start(out=dg[:, k0:k1, :].rearrange("p k j -> p (k j)"), in_=dgd[e_, c, hf_, :, 0:n * 128])
                pg.dma("sp", "dgl%d" % hf_, fld, reads=(("dgd", e_, hf_),), writes=(("dg", hf_),))

        def conv(c):
            ak = ("abf", c)
            bcv = bank()

            def mc0(e):
                last = None
                for k in range(16):
                    last = e.matmul(ps[bcv][:, 0:Wb], lhsT=dg[:, k, :], rhs=abf[:, c, k:k + Wb], start=(k == 0), stop=False)
                return last
            pg.op("pe", mc0, reads=(("dg", 0), ak), writes=(("ps", bcv),))

            def mc1(e):
                last = None
                for k in range(16, 31):
                    last = e.matmul(ps[bcv][:, 0:Wb], lhsT=dg[:, k, :], rhs=abf[:, c, k:k + Wb], start=False, stop=(k == 30))
                return last
            pg.op("pe", mc1, reads=(("dg", 1), ak), writes=(("ps", bcv),))
            pe_tick()
            cbo = _off["aconvb"] + e_ * 4 + c
            pg.op("act", lambda e: e.activation(out=cob[:, c, 0:Wb], in_=ps[bcv][:, 0:Wb], func=AF.Identity,
                                                bias=prm[:, cbo:cbo + 1], scale=1.0),
                  reads=(("ps", bcv), "prm"), writes=(("cob", c),))

        inproj_glu(0)
        gen(0)
        for c in range(4):
            if c + 1 < 4:
                inproj_glu(c + 1)
            conv(c)
            if c + 1 < 4:
                gen(c + 1)
        if ti == NT - 1:
            for (sg, lo, hi) in pcs:
                def fd(e, sg=sg):
                    return e.dma_start(out=oa_d[:, e_, sg, :, :], in_=astg[:, sg, :, :])
                pg.dma("sp", "outs", fd, reads=("astg",))

        def ln_tail():
            b1, b2 = bank(), bank()

            def mm1(e):
                last = None
                for c in range(4):
                    last = e.matmul(ps[b1][:, 0:Wb], lhsT=ones_f[:, :], rhs=cob[:, c, 0:Wb], start=(c == 0), stop=(c == 3))
                return last
            pg.op("pe", mm1, reads=tuple(("cob", c) for c in range(4)) + ("ones",), writes=(("ps", b1),))
            sqt = []
            for c in range(4):
                tq = tmpbuf()
                sqt.append(tq)
                pg.op("act", lambda e, c=c, tq=tq: e.activation(out=tmp[:, tq, 0:Wb], in_=cob[:, c, 0:Wb], func=AF.Square),
                      reads=(("cob", c),), writes=(("tmp", tq),))

            def mm2(e):
                last = None
                for c in range(4):
                    last = e.matmul(ps[b2][:, 0:Wb], lhsT=ones_f[:, :], rhs=tmp[:, sqt[c], 0:Wb], start=(c == 0), stop=(c == 3))
                return last
            pg.op("pe", mm2, reads=tuple(("tmp", tq) for tq in sqt) + ("ones",), writes=(("ps", b2),))
            pg.op("dve", lambda e: e.tensor_scalar(out=lnm[:, 0, 0:Wb], in0=ps[b1][:, 0:Wb], scalar1=1.0 / 512, scalar2=None, op0=ALU.mult),
                  reads=(("ps", b1),), writes=("lnmean",))
            tm = tmpbuf()
            pg.op("dve", lambda e: e.tensor_tensor(out=tmp[:, tm, 0:Wb], in0=lnm[:, 0, 0:Wb], in1=lnm[:, 0, 0:Wb], op=ALU.mult),
                  reads=("lnmean",), writes=(("tmp", tm),))
            pg.op("dve", lambda e: e.scalar_tensor_tensor(out=lnm[:, 1, 0:Wb], in0=ps[b2][:, 0:Wb], scalar=1.0 / 512, in1=tmp[:, tm, 0:Wb],
                                                          op0=ALU.mult, op1=ALU.subtract),
                  reads=(("ps", b2), ("tmp", tm)), writes=("lnrstd",))
            pg.op("act", lambda e: e.activation(out=lnm[:, 1, 0:Wb], in_=lnm[:, 1, 0:Wb], func=AF.Sqrt, bias=P("epsc", 1), scale=1.0),
                  reads=("lnrstd", "prm"), writes=("lnrstd",))
            pg.op("dve", lambda e: e.reciprocal(out=lnm[:, 1, 0:Wb], in_=lnm[:, 1, 0:Wb]), reads=("lnrstd",), writes=("lnrstd",))
            for c in range(4):
                t1_, t2_ = tmpbuf(), tmpbuf()
                pg.op("dve", lambda e, c=c, t1_=t1_: e.tensor_tensor(out=tmp[:, t1_, 0:Wb], in0=cob[:, c, 0:Wb], in1=lnm[:, 0, 0:Wb], op=ALU.subtract),
                      reads=(("cob", c), "lnmean"), writes=(("tmp", t1_),))
                pg.op("dve", lambda e, t1_=t1_, t2_=t2_: e.tensor_tensor(out=tmp[:, t2_, 0:Wb], in0=tmp[:, t1_, 0:Wb], in1=lnm[:, 1, 0:Wb], op=ALU.mult),
                      reads=(("tmp", t1_), "lnrstd"), writes=(("tmp", t2_),))
                go = _off["alng"] + e_ * 4 + c
                bo = _off["alnb"] + e_ * 4 + c
                for (sg, lo, hi), of in zip(pcs, offs):
                    def fsl(e, c=c, t2_=t2_, lo=lo, hi=hi, of=of, go=go, bo=bo):
                        return e.activation(out=act[:, c, lo:hi], in_=tmp[:, t2_, of - 30:of - 30 + hi - lo], func=AF.Silu,
                                            bias=prm[:, bo:bo + 1], scale=prm[:, go:go + 1])
                    pg.op("act", fsl, reads=(("tmp", t2_), "prm"), writes=(("act", c, ti),))
        pe_defer.append([2, ln_tail])

    def ab_rest(S, boundary):
        l = S["l"]
        e_ = l // 2
        win = ab_in[e_]
        out_phase(l, 1, ab_out[e_], 0, 4)
        for g in range(4):
            wnd = 2 << g
            s = load_piece([(0, 8, wcols(win, 1024 + g * 128)), (8, 1, b_wg[e_, g].rearrange("p (o c) -> p o c", o=1))])
            for ti, (t0, t1) in enumerate(TILES):
                N = t1 - t0
                pcs = pieces(t0, t1)
                offs = []
                off = 0
                for (sg, lo, hi) in pcs:
                    off += 15
                    offs.append(off)
                    off += hi - lo
                tot = off
                Wb = tot - 15
                bb = bank()
                bi = st["bub"]
                st["bub"] = 1 - bi
                bk_ = ("bub", bi)

                def mm(e, s=s, bb=bb, t0=t0, t1=t1, N=N):
                    last = None
                    for k in range(KC):
                        last = e.matmul(ps[bb][:, 0:N], lhsT=slots[:, s, k, :], rhs=h[:, k, t0:t1], start=(k == 0), stop=(k == KC - 1))
                    return last
                pg.op("pe", mm, reads=(("slot", s),) + tuple(("h", k, ti) for k in range(KC)), writes=(("ps", bb),))
                pe_tick()
                for (sg, lo, hi), of in zip(pcs, offs):
                    if sg == 0:
                        if ti == 0:
                            pg.op("dve", lambda e, bi=bi: e.memset(bub[:, bi, 0:15], 0.0), writes=(bk_,))
                        else:
                            pN = TILES[ti - 1][1] - TILES[ti - 1][0]
                            pg.op("dve", lambda e, bi=bi, pN=pN: e.tensor_copy(out=bub[:, bi, 0:15], in_=bub[:, 1 - bi, pN:pN + 15]),
                                  reads=(("bub", 1 - bi),), writes=(bk_,))
                    else:
                        pg.op("dve", lambda e, bi=bi, of=of, sg=sg, g=g: e.tensor_copy(out=bub[:, bi, of - 15:of], in_=sbs[:, e_, sg - 1, g, :]),
                              reads=("st_b",), writes=(bk_,))
                    pg.op("act", lambda e, bi=bi, bb=bb, lo=lo, hi=hi, of=of, t0=t0: e.activation(
                        out=bub[:, bi, of:of + hi - lo], in_=ps[bb][:, lo - t0:hi - t0], func=AF.Copy),
                        reads=(("ps", bb),), writes=(bk_,))
                if ti == 0:
                    pg.op("dve", lambda e, bi=bi: e.tensor_tensor(out=bub[:, bi, 15:15 + HALO], in0=bub[:, bi, 15:15 + HALO],
                                                                  in1=P("hmask", HALO), op=ALU.mult), reads=("prm",), writes=(bk_,))
                cur = None
                sh = 1
                for lev in range(g + 1):
                    tn = tmpbuf()
                    if cur is None:
                        pg.op("dve", lambda e, bi=bi, tn=tn, sh=sh, tot=tot: e.tensor_tensor(
                            out=tmp[:, tn, sh:tot], in0=bub[:, bi, sh:tot], in1=bub[:, bi, 0:tot - sh], op=ALU.add),
                            reads=(bk_,), writes=(("tmp", tn),))
                    else:
                        pg.op("dve", lambda e, tn=tn, cur=cur, sh=sh, tot=tot: e.tensor_tensor(
                            out=tmp[:, tn, sh:tot], in0=tmp[:, cur, sh:tot], in1=tmp[:, cur, 0:tot - sh], op=ALU.add),
                            reads=(("tmp", cur),), writes=(("tmp", tn),))
                    cur = tn
                    sh *= 2
                di = st["dbf"]
                st["dbf"] = 1 - di
                pg.op("dve", lambda e, bi=bi, cur=cur, di=di, Wb=Wb, wnd=wnd: e.scalar_tensor_tensor(
                    out=dbf[:, di, 0:Wb], in0=tmp[:, cur, 15:15 + Wb], scalar=1.0 / wnd, in1=bub[:, bi, 15:15 + Wb],
                    op0=ALU.mult, op1=ALU.subtract), reads=(("tmp", cur), bk_), writes=(("dbf", di),))
                if ti == 0:
                    tf = tmpbuf()
                    io = _off["invcnt"] + g * 16
                    pg.op("dve", lambda e, cur=cur, tf=tf, io=io: e.tensor_tensor(
                        out=tmp[:, tf, 0:16], in0=tmp[:, cur, 15 + HALO:15 + HALO + 16], in1=prm[:, io:io + 16], op=ALU.mult),
                        reads=(("tmp", cur), "prm"), writes=(("tmp", tf),))
                    pg.op("dve", lambda e, bi=bi, tf=tf, di=di: e.tensor_tensor(
                        out=dbf[:, di, HALO:HALO + 16], in0=tmp[:, tf, 0:16], in1=bub[:, bi, 15 + HALO:15 + HALO + 16], op=ALU.subtract),
                        reads=(("tmp", tf), bk_), writes=(("dbf", di),))
                if ti == NT - 1:
                    for (sg, lo, hi), of in zip(pcs, offs):
                        e0 = of + (hi - lo) - 15

                        def fd(e, bi=bi, sg=sg, e0=e0, g=g):
                            return e.dma_start(out=ob_d[:, e_, sg, g, :], in_=bub[:, bi, e0:e0 + 15])
                        pg.dma("sp", "outs", fd, reads=(bk_,))

                def grp(s=s, di=di, Wb=Wb, g=g, ti=ti, pcs=pcs, offs=offs):
                    bd = bank()
                    pg.op("pe", lambda e: e.matmul(ps[bd][:, 0:Wb], lhsT=slots[:, s, 8, :], rhs=dbf[:, di, 0:Wb], start=True, stop=True),
                          reads=(("slot", s), ("dbf", di)), writes=(("ps", bd),))
                    so = _off["bscale"] + e_ * 4 + g
                    for (sg, lo, hi), of in zip(pcs, offs):
                        pg.op("act", lambda e, lo=lo, hi=hi, of=of: e.activation(
                            out=act[:, g, lo:hi], in_=ps[bd][:, of - 15:of - 15 + hi - lo], func=AF.Identity, scale=prm[:, so:so + 1]),
                            reads=(("ps", bd), "prm"), writes=(("act", g, ti),))
                pe_defer.append([1, grp])
            pump_mod(1)
        pe_flush()
        out_phase(l, 1, ab_out[e_], 4, 4, pre_loop=boundary[0], after_tile=boundary[1])

    def f_prm(e):
        return e.dma_start(out=prm[:, :], in_=prm_d)
    pg.dma("sp", "ld_prm", f_prm, writes=("prm",))
    pg.dma("sp", "ld_cv", lambda e: e.dma_start(out=cv[:, :, :], in_=cv_d), writes=("cv",))
    pg.dma("sp", "ld_id", lambda e: e.dma_start(out=tmp[:, 0, 0:128], in_=id_d), writes=(("tmp", 0),))
    x_loads = []
    for ti, (t0, t1) in enumerate(TILES):
        def fxl(e, t0=t0, t1=t1):
            return e.dma_start(out=x[:, :, t0:t1], in_=xT_d[:, :, t0:t1])
        x_loads.append((ti, fxl))
    ti0, f0_ = x_loads[0]
    pg.dma("sp", "ld_x0", f0_, writes=tuple(("x", m, 0) for m in range(KC)))
    pg.dma("sp", "ld_sa", lambda e: e.dma_start(out=sa[:, :, :, :, :].rearrange("p a b c d -> p (a b c d)"), in_=sa_d), writes=("st_a",))
    pg.dma("sp", "ld_sb", lambda e: e.dma_start(out=sbs[:, :, :, :, :].rearrange("p a b c d -> p (a b c d)"), in_=sb_d), writes=("st_b",))
    pg.dma("sp", "ld_sc", lambda e: e.dma_start(out=scs[:, :, :, :, :].rearrange("p a b c d -> p (a b c d)"), in_=sc_d), writes=("st_c",))

    pg.op("dve", lambda e: e.memset(ones_bf[:, :], 1.0), writes=("ones",))
    pg.op("dve", lambda e: e.memset(ones_f[:, :], 1.0), writes=("ones",))
    pg.op("dve", lambda e: e.tensor_copy(out=ident_bf[:, :], in_=tmp[:, 0, 0:128]), reads=(("tmp", 0),), writes=("ident",))
    st["tmp"] = 1
    pg.op("act", lambda e: e.activation(out=cact[:, :, :], in_=cv[:, :, :], func=AF.Silu), reads=("cv",), writes=("cact",))

    def gen_diag(e2, anchor):
        o_ = _off["aconvw"] + e2 * 124
        pg.op("dve", lambda e, o_=o_: e.tensor_scalar(out=awh[:, :], in0=prm[:, o_:o_ + 124], scalar1=0.5, scalar2=None, op0=ALU.mult),
              reads=("prm", anchor), writes=("awh",))
        for c in range(4):
            for hf_, (k0, k1) in enumerate(((0, 16), (16, 31))):
                def fdg(e, c=c, k0=k0, k1=k1):
                    n = k1 - k0
                    return e.tensor_tensor(out=dg[:, k0:k1, :],
                                           in0=ident_bf[:, :].unsqueeze(1).to_broadcast([128, n, 128]),
                                           in1=awh[:, c * 31 + k0:c * 31 + k1].unsqueeze(2).to_broadcast([128, n, 128]),
                                           op=ALU.mult)
                pg.op("dve", fdg, reads=("ident", "awh"), writes=(("dg", hf_),))

                def fst_(e, e2=e2, c=c, k0=k0, k1=k1, hf_=hf_):
                    n = k1 - k0
                    return e.dma_start(out=dgd[e2, c, hf_, :, 0:n * 128], in_=dg[:, k0:k1, :].rearrange("p k j -> p (k j)"))
                pg.dma("sp", "dgw%d" % hf_, fst_, reads=(("dg", hf_),), writes=(("dgd", e2, hf_),))

    subs = []
    for l in range(DEPTH):
        subs.append({"kind": "ffn", "l": l, "s": 0, "f": 0})
        subs.append({"kind": "ab" if l % 2 == 0 else "c", "l": l, "s": 1})
        subs.append({"kind": "ffn", "l": l, "s": 2, "f": 1})
    PRE = {"ffn": ffn_preload, "ab": ab_preload, "c": c_preload}
    FIRST = {"ffn": ffn_first_tile, "ab": ab_first_tile, "c": c_first_tile}
    REST = {"ffn": ffn_rest, "ab": ab_rest, "c": c_rest}
    LAG = 2
    NDEF = 5

    for i in range(36):
        mod_pending.append((0, i))
    S0 = subs[0]
    for i in range(12):
        if i == 8:
            PRE["ffn"](S0)
        s_before = st["slot"]
        pump_mod(1)
        if 1 <= i <= 5:
            ti_, fx_ = x_loads[i]
            pg.dma("sp", "ld_x%d" % ti_, fx_, reads=(("slot", s_before),), writes=tuple(("x", m, ti_) for m in range(KC)))
    for i in range(36):
        mod_pending.append((1, i))

    for ti in range(NT):
        norm_tile(0, 0, ti)
        if ti >= 1:
            FIRST["ffn"](S0, ti - 1)
    FIRST["ffn"](S0, NT - 1)
    held.clear()

    for i, S in enumerate(subs):
        nxt = subs[i + 1] if i + 1 < len(subs) else None
        if i == 0:
            gen_diag(0, ("act", 0, NT - 1))
        if S["kind"] == "ffn" and S["s"] == 0 and S["l"] == 1:
            gen_diag(1, ("act", 0, NT - 1))
        if S["kind"] == "ffn" and S["s"] == 0 and S["l"] >= 1 and S["l"] + 1 < DEPTH:
            for i_ in range(36):
                mod_pending.append((S["l"] + 1, i_))

        def pre(nxt=nxt):
            if nxt is not None:
                PRE[nxt["kind"]](nxt)

        def cb(ti, nxt=nxt):
            if nxt is None:
                norm_tile(DEPTH - 1, 0, ti, final=True, defer=3)
            else:
                norm_tile(nxt["l"], nxt["s"], ti, defer=NDEF)
                if ti - LAG >= 0:
                    FIRST[nxt["kind"]](nxt, ti - LAG)
        REST[S["kind"]](S, (pre, cb))
        pe_flush()
        if nxt is not None:
            for ti in range(NT - LAG, NT):
                FIRST[nxt["kind"]](nxt, ti)
        held.clear()
        pe_flush()

    pg.final_wait("sp", ["outs"])
    sim = pg.finalize()
    _SIM["time_us"] = sim
    _SIM["pg"] = pg
    _SIM["busy"] = {e: sum(pg.ops[i]["dur"] for i in pg.order[e]) for e in pg.ENGS}

    with nc.Block() as block:
        @block.tensor
        def _(e):
            pg.run("pe", e)

        @block.scalar
        def _(e):
            pg.run("act", e)

        @block.vector
        def _(e):
            pg.run("dve", e)

        @block.gpsimd
        def _(e):
            pg.run("pool", e)

        @block.sync
        def _(e):
            pg.run("sp", e)


def _fm(v):
    v = np.asarray(v, dtype=np.float32)
    n = v.shape[-1] // 128
    v = v.reshape(v.shape[:-1] + (n, 128))
    return np.ascontiguousarray(np.moveaxis(v, -1, 0))


_NC_CACHE = {}


def kernel(x_prompt, x_sample, state_conv_a, state_pool_b, state_conv_c, c_prompt, c_sample,
           ada_w, ada_b, norm_g, ffn_w_gu, ffn_w_down, ab_w_in, a_conv_w, a_conv_b, a_ln_g,
           a_ln_b, b_w_group, b_scale, ab_w_out, c_w_in, c_conv_w, c_w_out, final_g):
    f32 = np.float32
    xp = np.asarray(x_prompt, f32)[0]
    xs = np.asarray(x_sample, f32)
    prm_base = np.zeros((128, NPRM), f32)

    def put(name, arr):
        arr = np.ascontiguousarray(arr, dtype=f32).reshape(128, -1)
        prm_base[:, _off[name]:_off[name] + arr.shape[1]] = arr
    put("normg", _fm(norm_g))
    put("finalg", _fm(final_g))
    put("adab", _fm(ada_b))
    put("aconvw", np.transpose(_fm(a_conv_w), (0, 1, 3, 2)))
    put("aconvb", _fm(a_conv_b))
    put("alng", _fm(a_ln_g))
    put("alnb", _fm(a_ln_b))
    put("bscale", _fm(b_scale))
    put("cconvw", np.transpose(_fm(c_conv_w), (0, 1, 3, 2)))
    sca = np.asarray(state_conv_a, f32)
    spb = np.asarray(state_pool_b, f32)
    scc = np.asarray(state_conv_c, f32)
    cpr = np.asarray(c_prompt, f32)
    csa = np.asarray(c_sample, f32)
    shared = {
        "ada_w": np.ascontiguousarray(ada_w, dtype=f32), "ffn_w_gu": np.ascontiguousarray(ffn_w_gu, dtype=f32),
        "ffn_w_down": np.ascontiguousarray(ffn_w_down, dtype=f32), "ab_w_in": np.ascontiguousarray(ab_w_in, dtype=f32),
        "b_w_group": np.ascontiguousarray(b_w_group, dtype=f32), "ab_w_out": np.ascontiguousarray(ab_w_out, dtype=f32),
        "c_w_in": np.ascontiguousarray(c_w_in, dtype=f32), "c_w_out": np.ascontiguousarray(c_w_out, dtype=f32),
    }
    in_maps = []
    for i in range(NCORES):
        toks = np.zeros((T, D), f32)
        if i == 0:
            toks[HALO:HALO + MAIN] = xp[0:MAIN]
        else:
            toks[0:HALO + MAIN] = xp[i * MAIN - HALO:(i + 1) * MAIN]
        toks[HALO + MAIN:HALO + MAIN + LS] = xs[2 * i]
        toks[HALO + MAIN + LS:] = xs[2 * i + 1]
        xT = np.ascontiguousarray(toks.reshape(T, KC, 128).transpose(2, 1, 0))
        cvec = np.stack([cpr[0], csa[2 * i], csa[2 * i + 1]], 0)
        cvf = np.ascontiguousarray(cvec.reshape(3, KC, 128).transpose(2, 1, 0))
        prm = prm_base.copy()
        prm[:, _off["hmask"]:_off["hmask"] + HALO] = 0.0 if i == 0 else 1.0
        ic = np.zeros((4, 16), f32)
        for g in range(4):
            w = 2 << g
            for p_ in range(16):
                ic[g, p_] = (1.0 / min(w, p_ + 1)) if i == 0 else (1.0 / w)
        prm[:, _off["invcnt"]:_off["invcnt"] + 64] = ic.reshape(1, 64)
        prm[:, _off["epsc"]] = EPS
        sa_i = np.ascontiguousarray(np.transpose(_fm(sca[:, 2 * i:2 * i + 2]), (0, 1, 2, 4, 3))).reshape(128, -1)
        sb_i = np.ascontiguousarray(np.transpose(_fm(spb[:, 2 * i:2 * i + 2]), (0, 1, 2, 4, 3))).reshape(128, -1)
        sc_i = np.ascontiguousarray(np.transpose(_fm(scc[:, 2 * i:2 * i + 2]), (0, 1, 2, 4, 3))).reshape(128, -1)
        m = {"xT": xT, "cvec": cvf, "prm": prm, "sa": sa_i, "sb": sb_i, "scn": sc_i, "ident": np.eye(128, dtype=f32)}
        m.update(shared)
        in_maps.append(m)

    if "nc" not in _NC_CACHE:
        _NC_CACHE["nc"] = build_nc()
    nc = _NC_CACHE["nc"]
    res = run_bass_kernel_spmd(nc, in_maps, core_ids=list(range(NCORES)))
    R = res.results

    y_prompt = np.zeros((1, NCORES * MAIN, D), f32)
    y_sample = np.zeros((2 * NCORES, LS, D), f32)
    na_s = np.zeros((2, 2 * NCORES, 30, 512), f32)
    nb_s = np.zeros((2, 2 * NCORES, 15, 512), f32)
    nc_s = np.zeros((2, 2 * NCORES, 2, 1024), f32)
    for i in range(NCORES):
        yT = np.asarray(R[i]["yT"], f32)
        rows = yT.transpose(2, 1, 0).reshape(MAIN + 2 * LS, D)
        y_prompt[0, i * MAIN:(i + 1) * MAIN] = rows[:MAIN]
        y_sample[2 * i] = rows[MAIN:MAIN + LS]
        y_sample[2 * i + 1] = rows[MAIN + LS:]
        oa = np.asarray(R[i]["oa"], f32).transpose(1, 2, 4, 3, 0).reshape(2, 3, 30, 512)
        ob = np.asarray(R[i]["ob"], f32).transpose(1, 2, 4, 3, 0).reshape(2, 3, 15, 512)
        oc = np.asarray(R[i]["oc"], f32).transpose(1, 2, 4, 3, 0).reshape(2, 3, 2, 1024)
        na_s[:, 2 * i:2 * i + 2] = oa[:, 1:3]
        nb_s[:, 2 * i:2 * i + 2] = ob[:, 1:3]
        nc_s[:, 2 * i:2 * i + 2] = oc[:, 1:3]
        if i == NCORES - 1:
            na_p = oa[:, 0:1].copy()
            nb_p = ob[:, 0:1].copy()
            nc_p = oc[:, 0:1].copy()
    return (y_prompt, y_sample, na_p, nb_p, nc_p, na_s, nb_s, nc_s)
```

```python
import numpy as np
from contextlib import ExitStack
import concourse.bass as bass
import concourse.mybir as mybir
from concourse.bass_utils import run_bass_kernel_spmd

F32 = mybir.dt.float32
BF16 = mybir.dt.bfloat16
AF = mybir.ActivationFunctionType
ALU = mybir.AluOpType

NCORES = 8
D = 1024
KC = 8
DFF = 2816
NJ = 22
DEPTH = 4
HALO = 64
MAIN = 2048
LS = 32
T = HALO + MAIN + 2 * LS
SEGS = [(0, HALO + MAIN), (HALO + MAIN, HALO + MAIN + LS), (HALO + MAIN + LS, T)]
TILES = [(0, 384), (384, 768), (768, 1152), (1152, 1536), (1536, 1856), (1856, 2176)]
NMAX = 384
EPS = 1e-6
NSLOT = 6
SLOTW = 2048
ACTK = 4

_off = {}
_o = 0
for _n, _w in [("normg", 96), ("finalg", 8), ("adab", 288), ("aconvw", 248), ("aconvb", 8), ("alng", 8),
               ("alnb", 8), ("bscale", 8), ("cconvw", 48), ("hmask", 64), ("invcnt", 64), ("epsc", 1)]:
    _off[_n] = _o
    _o += _w
NPRM = _o


def pieces(t0, t1):
    out = []
    for s, (a, b) in enumerate(SEGS):
        lo, hi = max(a, t0), min(b, t1)
        if lo < hi:
            out.append((s, lo, hi))
    return out


class _FakeIns:
    def then_inc(self, *a, **k):
        return self


class _FakeEng:
    def __init__(self):
        self.cost = 0.0
        self.tag = None
        self.nbytes = 0

    @staticmethod
    def _free(ap):
        n = 1
        for d in ap.shape[1:]:
            n *= int(d)
        return n

    def matmul(self, out, lhsT=None, rhs=None, **k):
        c = max(self._free(out) / 2400.0 + 0.004, 0.035)
        if lhsT is not None and lhsT.dtype == F32:
            c *= 2.2
        self.cost += c
        return _FakeIns()

    def activation(self, out=None, in_=None, func=None, **k):
        self.cost += 0.22 + self._free(out) * 0.001
        if func in (AF.Silu, AF.Tanh):
            self.tag = "A"
        elif func == AF.Sqrt:
            self.tag = "B"
        return _FakeIns()

    def reciprocal(self, out=None, in_=None, **k):
        self.cost += 0.06 + self._free(out) * 0.0064
        return _FakeIns()

    def scalar_tensor_tensor(self, out=None, **k):
        self.cost += 0.06 + self._free(out) * 0.00146
        return _FakeIns()

    def dma_start(self, out=None, in_=None, **k):
        n = 1
        for d in out.shape:
            n *= int(d)
        self.nbytes += n * (4 if in_.dtype == F32 else 2)
        return _FakeIns()

    def __getattr__(self, name):
        def generic(*a, **k):
            out = k.get("out", a[0] if a else None)
            self.cost += 0.06 + (self._free(out) * 0.0012 if out is not None else 0.0)
            return _FakeIns()
        return generic


class Prog:
    ENGS = ("pe", "act", "dve", "pool", "sp")

    def __init__(self, nc, ctx):
        self.nc = nc
        self.ctx = ctx
        self.semh = {}
        for n in ("pe", "act", "dve", "pool"):
            self.semh[n] = ctx.enter_context(nc.semaphore("s_" + n))
        self.ops = []
        self.lastw = {}
        self.readers = {}
        self.final_sems = []
        self.order = None

    def dma_sem(self, name):
        if name not in self.semh:
            self.semh[name] = self.ctx.enter_context(self.nc.semaphore("d_" + name))
        return name

    def _add(self, eng, fn, sem, inc, reads, writes, dur, xfer, tag):
        idx = len(self.ops)
        deps = set()
        for k in reads:
            w = self.lastw.get(k)
            if w is not None:
                deps.add(w)
        for k in writes:
            w = self.lastw.get(k)
            if w is not None:
                deps.add(w)
            deps.update(self.readers.get(k, ()))
        deps.discard(idx)
        self.ops.append({"eng": eng, "fn": fn, "sem": sem, "inc": inc, "deps": deps, "dur": dur, "xfer": xfer, "tag": tag})
        for k in reads:
            self.readers.setdefault(k, []).append(idx)
        for k in writes:
            self.lastw[k] = idx
            self.readers[k] = []
        return idx

    def op(self, eng, fn, reads=(), writes=()):
        fe = _FakeEng()
        fn(fe)
        cost = fe.cost * (1.6 if eng == "pool" else 1.0)
        return self._add(eng, fn, eng, 1, reads, writes, cost, None, fe.tag)

    def dma(self, queue, semname, fn, reads=(), writes=()):
        self.dma_sem(semname)
        fe = _FakeEng()
        fn(fe)
        issue = 1.0 if queue == "pool" else 0.15
        return self._add(queue, fn, semname, 16, reads, writes, issue, fe.nbytes, None)

    def final_wait(self, queue, semnames):
        self.final_sems = [(queue, s) for s in semnames]

    def finalize(self):
        import heapq
        ops = self.ops
        n = len(ops)
        nd = [len(o["deps"]) for o in ops]
        users = [[] for _ in range(n)]
        for i, o in enumerate(ops):
            for d in o["deps"]:
                users[d].append(i)
        LAT = 0.2
        done_t = [0.0] * n
        start_t = [0.0] * n
        free = {e: 0.0 for e in self.ENGS}
        pend = {e: [] for e in self.ENGS}
        avail = {e: [] for e in self.ENGS}
        act_set = [None]
        dma_pipe = [0.0]
        order = {e: [] for e in self.ENGS}
        for i, o in enumerate(ops):
            if nd[i] == 0:
                heapq.heappush(pend[o["eng"]], (0.0, i))
        left = n
        while left:
            best = None
            for e in self.ENGS:
                t = free[e]
                while pend[e] and pend[e][0][0] <= t:
                    heapq.heappush(avail[e], heapq.heappop(pend[e])[1])
                if avail[e]:
                    cand = (t, avail[e][0], e, True)
                elif pend[e]:
                    cand = (pend[e][0][0], pend[e][0][1], e, False)
                else:
                    continue
                if best is None or cand[:2] < best[:2]:
                    best = cand
            st_, i, e, from_avail = best
            if from_avail:
                heapq.heappop(avail[e])
            else:
                heapq.heappop(pend[e])
            o = ops[i]
            start_t[i] = st_
            dur = o["dur"]
            if e == "act" and o["tag"] is not None:
                if act_set[0] is not None and act_set[0] != o["tag"]:
                    dur += 1.8
                act_set[0] = o["tag"]
            if o["xfer"] is not None:
                free[e] = st_ + dur
                xs = max(st_ + dur, dma_pipe[0])
                dma_pipe[0] = xs + o["xfer"] / 200e3
                done_t[i] = dma_pipe[0] + 2.0
            else:
                free[e] = st_ + dur
                done_t[i] = st_ + dur
            order[e].append(i)
            left -= 1
            for u in users[i]:
                nd[u] -= 1
                if nd[u] == 0:
                    rt = max(done_t[d] for d in ops[u]["deps"]) + LAT
                    heapq.heappush(pend[ops[u]["eng"]], (rt, u))
        self.order = order
        self.start_t = start_t
        self.done_t = done_t
        self.sim_time = max(done_t) if n else 0.0
        cnt = {}
        self.val = [0] * n
        for e in self.ENGS:
            for i in order[e]:
                sname = ops[i]["sem"]
                cnt[sname] = cnt.get(sname, 0) + ops[i]["inc"]
                self.val[i] = cnt[sname]
        self.cnt = cnt
        return self.sim_time

    def run(self, eng_name, eng):
        ops = self.ops
        waited = {}
        for i in self.order[eng_name]:
            o = ops[i]
            need = {}
            for d in o["deps"]:
                sd = ops[d]["sem"]
                v = self.val[d]
                if need.get(sd, 0) < v:
                    need[sd] = v
            for sd, v in need.items():
                if waited.get(sd, 0) < v:
                    waited[sd] = v
                    eng.wait_ge(self.semh[sd], v)
            ins = o["fn"](eng)
            ins.then_inc(self.semh[o["sem"]], o["inc"])
        for (q, sname) in self.final_sems:
            if q == eng_name and self.cnt.get(sname, 0) > 0:
                eng.wait_ge(self.semh[sname], self.cnt[sname])


_SIM = {}


def build_nc():
    nc = bass.Bass("TRN2", target_bir_lowering=False)
    ctx = ExitStack()
    with ctx:
        _build(nc, ctx)
    return nc


def _build(nc, ctx):
    def din(name, shape):
        return nc.dram_tensor(name, list(shape), F32, kind="ExternalInput").ap()

    def dout(name, shape):
        return nc.dram_tensor(name, list(shape), F32, kind="ExternalOutput").ap()

    xT_d = din("xT", [128, KC, T])
    cv_d = din("cvec", [128, KC, 3])
    prm_d = din("prm", [128, NPRM])
    sa_d = din("sa", [128, 2 * 2 * 4 * 30])
    sb_d = din("sb", [128, 2 * 2 * 4 * 15])
    sc_d = din("scn", [128, 2 * 2 * 8 * 2])
    id_d = din("ident", [128, 128])
    ada_w = din("ada_w", [DEPTH, D, 9 * D])
    w_gu = din("ffn_w_gu", [DEPTH, 2, D, 2 * DFF])
    w_dn = din("ffn_w_down", [DEPTH, 2, DFF, D])
    ab_in = din("ab_w_in", [2, D, 1536])
    b_wg = din("b_w_group", [2, 4, 128, 128])
    ab_out = din("ab_w_out", [2, D, D])
    c_in = din("c_w_in", [2, D, 3 * D])
    c_out = din("c_w_out", [2, D, D])
    yT_d = dout("yT", [128, KC, MAIN + 2 * LS])
    oa_d = dout("oa", [128, 2, 3, 4, 30])
    ob_d = dout("ob", [128, 2, 3, 4, 15])
    oc_d = dout("oc", [128, 2, 3, 8, 2])
    dgd = nc.dram_tensor("dgd", [2, 4, 2, 128, 16 * 128], BF16, kind="Internal").ap()

    def sb_(name, shape, dt=F32):
        return ctx.enter_context(nc.sbuf_tensor(name, list(shape), dt))

    NT = len(TILES)
    x = sb_("x", [128, KC, T])
    h = sb_("h", [128, KC, T], BF16)
    act = sb_("act", [128, ACTK, T], BF16)
    slots = sb_("slots", [128, NSLOT, 16, 128], BF16)
    sqb = sb_("sqb", [128, 2, KC, NMAX], BF16)
    rbuf = sb_("rbuf", [128, 2, NMAX])
    NTMP = 6
    TMPW = 400
    tmp = sb_("tmp", [128, NTMP, TMPW])
    abf = sb_("abf", [128, 4, 30 + 384], BF16)
    astg = sb_("astg", [128, 3, 4, 30])
    dg = sb_("dg", [128, 31, 128], BF16)
    cob = sb_("cob", [128, 4, 384])
    lnm = sb_("lnm", [128, 2, 384])
    bub = sb_("bub", [128, 2, 45 + 384])
    dbf = sb_("dbf", [128, 2, 384], BF16)
    ztl = sb_("ztl", [128, 2, 6 + 384])
    prm = sb_("prm_s", [128, NPRM])
    cv = sb_("cv_s", [128, KC, 3])
    cact = sb_("cact", [128, KC, 3], BF16)
    sa = sb_("sa_s", [128, 2, 2, 4, 30])
    sbs = sb_("sb_s", [128, 2, 2, 4, 15])
    scs = sb_("sc_s", [128, 2, 2, 8, 2])
    modfm = sb_("modfm", [128, 2, 72, 3])
    gsb = sb_("gsb", [128, 2, 3, KC, 3])
    ghb = sb_("ghb", [128, 2, 3, KC, 3])
    ones_bf = sb_("ones_bf", [128, 128], BF16)
    ones_f = sb_("ones_f", [128, 128])
    ident_bf = sb_("ident_bf", [128, 128], BF16)
    awh = sb_("awh", [128, 124])
    ps = [ctx.enter_context(nc.psum_tensor("ps%d" % i, [128, 512], F32)) for i in range(8)]

    pg = Prog(nc, ctx)
    st = {"bank": 0, "slot": 0, "tmp": 0, "sq": 0, "r": 0, "dbf": 0, "bub": 0}

    def bank():
        b = st["bank"]
        st["bank"] = (b + 1) % 8
        return b

    def tmpbuf():
        i = st["tmp"]
        st["tmp"] = (i + 1) % NTMP
        return i

    def P(name, w=1, i=0):
        o = _off[name] + i
        return prm[:, o:o + w]

    held = set()

    def alloc_slot():
        for _ in range(NSLOT):
            s = st["slot"]
            st["slot"] = (s + 1) % NSLOT
            if s not in held:
                return s
        raise RuntimeError("no free weight slot")

    def load_piece(parts):
        s = alloc_slot()
        for (b0, nb, src) in parts:
            def f(e, s=s, b0=b0, nb=nb, src=src):
                dst = slots[:, s, b0:b0 + nb, :]
                if len(src.shape) == 3 and src.shape[2] == 1024:
                    dst = dst.rearrange("p (j m) c -> p j (m c)", m=8)
                return e.dma_start(out=dst, in_=src)
            pg.dma("pool", "slot%d" % s, f, reads=(), writes=(("slot", s),))
        return s

    def wcols(w2d, c0, n=128):
        return w2d.rearrange("(kc p) n -> p kc n", p=128)[:, :, c0:c0 + n]

    def wrows(w2d, r0, nr):
        return w2d.rearrange("(j p) n -> p j n", p=128)[:, r0:r0 + nr, :]

    pe_defer = []

    def pe_tick():
        for d in list(pe_defer):
            d[0] -= 1
            if d[0] <= 0:
                pe_defer.remove(d)
                d[1]()

    def pe_flush():
        while pe_defer:
            d = pe_defer.pop(0)
            d[1]()

    mod_pending = []

    def mod_piece(l, i):
        src = ada_w[l].rearrange("(kc p) n -> p kc n", p=128)[:, :, i * 256:(i + 1) * 256]
        s = alloc_slot()

        def f(e, s=s, src=src):
            return e.dma_start(out=slots[:, s, :, :].rearrange("p (k a) c -> p k (a c)", a=2), in_=src)
        pg.dma("pool", "slot%d" % s, f, writes=(("slot", s),))
        b = bank()

        def mm(e, s=s, b=b):
            last = None
            for ql in range(2):
                for k in range(KC):
                    last = e.matmul(ps[b][:, ql * 3:ql * 3 + 3], lhsT=slots[:, s, k * 2 + ql, :],
                                    rhs=cact[:, k, :], start=(k == 0), stop=(k == KC - 1))
            return last
        pg.op("pe", mm, reads=(("slot", s), "cact"), writes=(("ps", b),))
        for ql in range(2):
            q = 2 * i + ql

            def ev(e, b=b, ql=ql, q=q, l=l):
                return e.activation(out=modfm[:, l % 2, q, :], in_=ps[b][:, ql * 3:ql * 3 + 3],
                                    func=AF.Identity, bias=P("adab", 1, l * 72 + q), scale=1.0)
            pg.op("act", ev, reads=(("ps", b), "prm"), writes=(("mod", l % 2, q // 8),))

    def mod_derive(l, s_):
        lb = l % 2
        for m in range(KC):
            def f(e, m=m):
                o = _off["normg"] + (l * 3 + s_) * 8 + m
                return e.tensor_scalar(out=gsb[:, lb, s_, m, :], in0=modfm[:, lb, (3 * s_ + 1) * 8 + m, :],
                                       scalar1=1.0, scalar2=prm[:, o:o + 1], op0=ALU.add, op1=ALU.mult)
            pg.op("dve", f, reads=(("mod", lb, 3 * s_ + 1), "prm"), writes=(("gs", lb, s_),))

        def f2(e):
            return e.tensor_scalar(out=ghb[:, lb, s_, :, :], in0=modfm[:, lb, (3 * s_ + 2) * 8:(3 * s_ + 3) * 8, :],
                                   scalar1=(1.0 if s_ == 1 else 0.5), scalar2=None, op0=ALU.mult)
        pg.op("dve", f2, reads=(("mod", lb, 3 * s_ + 2),), writes=(("gh", lb, s_),))

    def pump_mod(n=1):
        for _ in range(n):
            if mod_pending:
                l, i = mod_pending.pop(0)
                mod_piece(l, i)
                if i % 12 == 11:
                    mod_derive(l, i // 12)

    def norm_tile(l, s_, ti, final=False, defer=0):
        t0, t1 = TILES[ti]
        N = t1 - t0
        lb = l % 2
        sq = st["sq"]
        st["sq"] = 1 - sq
        for m in range(KC):
            if m in (3, 7):
                def fp(e, m=m):
                    return e.tensor_tensor(out=sqb[:, sq, m, 0:N], in0=x[:, m, t0:t1], in1=x[:, m, t0:t1], op=ALU.mult)
                pg.op("pool", fp, reads=(("x", m, ti),), writes=(("sqb", sq),))
                continue

            def f(e, m=m):
                return e.activation(out=sqb[:, sq, m, 0:N], in_=x[:, m, t0:t1], func=AF.Square)
            pg.op("act", f, reads=(("x", m, ti),), writes=(("sqb", sq),))

        def rest():
            b = bank()
            ri = st["r"]
            st["r"] = 1 - ri

            def mm(e):
                last = None
                for m in range(KC):
                    last = e.matmul(ps[b][:, 0:N], lhsT=ones_bf[:, :], rhs=sqb[:, sq, m, 0:N],
                                    start=(m == 0), stop=(m == KC - 1))
                return last
            pg.op("pe", mm, reads=(("sqb", sq), "ones"), writes=(("ps", b),))

            def fr0(e):
                return e.activation(out=rbuf[:, ri, 0:N], in_=ps[b][:, 0:N], func=AF.Sqrt, bias=P("epsc", 1), scale=1.0 / D)
            pg.op("act", fr0, reads=(("ps", b), "prm"), writes=(("r", ri),))

            def fr(e):
                return e.reciprocal(out=rbuf[:, ri, 0:N], in_=rbuf[:, ri, 0:N])
            pg.op("dve", fr, reads=(("r", ri),), writes=(("r", ri),))
            for m in range(KC):
                if final:
                    def fy(e, m=m):
                        o = _off["finalg"] + m
                        return e.scalar_tensor_tensor(out=x[:, m, t0:t1], in0=x[:, m, t0:t1], scalar=prm[:, o:o + 1],
                                                      in1=rbuf[:, ri, 0:N], op0=ALU.mult, op1=ALU.mult)
                    pg.op("dve", fy, reads=(("r", ri), "prm"), writes=(("x", m, ti),))
                    continue
                tb = tmpbuf()

                def ft(e, m=m, tb=tb):
                    return e.tensor_tensor(out=tmp[:, tb, 0:N], in0=x[:, m, t0:t1], in1=rbuf[:, ri, 0:N], op=ALU.mult)
                pg.op("dve", ft, reads=(("x", m, ti), ("r", ri)), writes=(("tmp", tb),))
                pcs_ = pieces(t0, t1)
                if len(pcs_) == 3:
                    a0 = pcs_[1][1] - t0

                    def fs1(e, m=m, tb=tb, a0=a0):
                        v = tmp[:, tb, a0:a0 + 2 * LS].rearrange("p (s t) -> p s t", t=LS)
                        return e.tensor_tensor(out=v, in0=v, in1=gsb[:, lb, s_, m, 1:3].unsqueeze(2).to_broadcast([128, 2, LS]),
                                               op=ALU.mult)
                    pg.op("dve", fs1, reads=(("tmp", tb), ("gs", lb, s_)), writes=(("tmp", tb),))

                    def fs2(e, m=m, tb=tb, a0=a0):
                        v = tmp[:, tb, a0:a0 + 2 * LS].rearrange("p (s t) -> p s t", t=LS)
                        o_ = h[:, m, t0 + a0:t0 + a0 + 2 * LS].rearrange("p (s t) -> p s t", t=LS)
                        return e.tensor_tensor(out=o_, in0=v,
                                               in1=modfm[:, lb, (3 * s_) * 8 + m, 1:3].unsqueeze(2).to_broadcast([128, 2, LS]),
                                               op=ALU.add)
                    pg.op("dve", fs2, reads=(("tmp", tb), ("mod", lb, 3 * s_)), writes=(("h", m, ti),))
                    pcs_ = pcs_[:1]
                for (sg, lo, hi) in pcs_:
                    def fh(e, m=m, tb=tb, sg=sg, lo=lo, hi=hi):
                        return e.activation(out=h[:, m, lo:hi], in_=tmp[:, tb, lo - t0:hi - t0], func=AF.Identity,
                                            bias=modfm[:, lb, (3 * s_) * 8 + m, sg:sg + 1],
                                            scale=gsb[:, lb, s_, m, sg:sg + 1])
                    pg.op("act", fh, reads=(("tmp", tb), ("gs", lb, s_), ("mod", lb, 3 * s_)), writes=(("h", m, ti),))
            if final:
                lo_ = max(t0, HALO)

                def fyo(e):
                    return e.dma_start(out=yT_d[:, :, lo_ - HALO:t1 - HALO], in_=x[:, :, lo_:t1])
                pg.dma("sp", "outs", fyo, reads=tuple(("x", m, ti) for m in range(KC)))
        if defer > 0:
            pe_defer.append([defer, rest])
        else:
            rest()

    def out_phase(l, s_, wsrc2d, r0, nk, pre_loop=None, after_tile=None):
        lb = l % 2
        sl = []
        k = 0
        while k < nk:
            n = min(2, nk - k)
            sl.append((load_piece([(0, n * 8, wrows(wsrc2d, r0 + k, n))]), k, n))
            k += n
        if pre_loop is not None:
            pre_loop()
        for ti, (t0, t1) in enumerate(TILES):
            N = t1 - t0
            for m in range(KC):
                b = bank()

                def mm(e, m=m, b=b, t0=t0, t1=t1, N=N):
                    last = None
                    kk = 0
                    for (s, k0, n) in sl:
                        for j in range(n):
                            last = e.matmul(ps[b][:, 0:N], lhsT=slots[:, s, j * 8 + m, :], rhs=act[:, k0 + j, t0:t1],
                                            start=(kk == 0), stop=(kk == nk - 1))
                            kk += 1
                    return last
                pg.op("pe", mm, reads=tuple(("slot", s) for (s, _, _) in sl) + tuple(("act", k_, ti) for k_ in range(nk)),
                      writes=(("ps", b),))
                pe_tick()
                for (sg, lo, hi) in pieces(t0, t1):
                    def fx(e, m=m, b=b, sg=sg, lo=lo, hi=hi, t0=t0):
                        return e.scalar_tensor_tensor(out=x[:, m, lo:hi], in0=ps[b][:, lo - t0:hi - t0],
                                                      scalar=ghb[:, lb, s_, m, sg:sg + 1], in1=x[:, m, lo:hi],
                                                      op0=ALU.mult, op1=ALU.add)
                    pg.op("dve", fx, reads=(("ps", b), ("gh", lb, s_)), writes=(("x", m, ti),))
            if after_tile is not None:
                after_tile(ti)

    FFN_PARTS = [(0, 4), (4, 4), (8, 4), (12, 4), (16, 3), (19, 3)]

    def ffn_chunk_tile(s, jl, ti):
        t0, t1 = TILES[ti]
        N = t1 - t0
        b1, b2 = bank(), bank()

        def mm(e):
            last = None
            for k in range(KC):
                e.matmul(ps[b1][:, 0:N], lhsT=slots[:, s, k, :], rhs=h[:, k, t0:t1], start=(k == 0), stop=(k == KC - 1))
            for k in range(KC):
                last = e.matmul(ps[b2][:, 0:N], lhsT=slots[:, s, 8 + k, :], rhs=h[:, k, t0:t1],
                                start=(k == 0), stop=(k == KC - 1))
            return last
        pg.op("pe", mm, reads=(("slot", s),) + tuple(("h", k, ti) for k in range(KC)), writes=(("ps", b1), ("ps", b2)))
        pe_tick()
        tb = tmpbuf()

        def fs(e):
            return e.activation(out=tmp[:, tb, 0:N], in_=ps[b1][:, 0:N], func=AF.Silu)
        pg.op("act", fs, reads=(("ps", b1),), writes=(("tmp", tb),))

        def fm(e):
            return e.tensor_tensor(out=act[:, jl, t0:t1], in0=tmp[:, tb, 0:N], in1=ps[b2][:, 0:N], op=ALU.mult)
        pg.op("dve", fm, reads=(("tmp", tb), ("ps", b2)), writes=(("act", jl, ti),))

    def ffn_piece(S, j):
        wg = w_gu[S["l"], S["f"]]
        return load_piece([(0, 8, wcols(wg, j * 128)), (8, 8, wcols(wg, DFF + j * 128))])

    def ffn_preload(S):
        S["fs"] = [ffn_piece(S, j) for j in range(4)]
        held.update(S["fs"])

    def ffn_first_tile(S, ti):
        for jl in range(4):
            ffn_chunk_tile(S["fs"][jl], jl, ti)

    def ffn_rest(S, boundary):
        l, s_ = S["l"], S["s"]
        wd = w_dn[l, S["f"]]
        out_phase(l, s_, wd, 0, 4)
        for pi, (j0, nj) in enumerate(FFN_PARTS):
            if pi == 0:
                continue
            for jl in range(nj):
                s = ffn_piece(S, j0 + jl)
                for ti in range(NT):
                    ffn_chunk_tile(s, jl, ti)
                pump_mod(2)
            if pi == len(FFN_PARTS) - 1:
                out_phase(l, s_, wd, j0, nj, pre_loop=boundary[0], after_tile=boundary[1])
            else:
                out_phase(l, s_, wd, j0, nj)

    def c_chunk_tile(S, c, cl, zi, s1, s2, ti):
        o = S["l"] // 2
        t0, t1 = TILES[ti]
        N = t1 - t0
        pcs = pieces(t0, t1)
        bb, bc, bv = bank(), bank(), bank()
        zk = ("ztl", zi)

        def mm(e):
            last = None
            for (bk, s, o8) in ((bb, s1, 0), (bc, s1, 8), (bv, s2, 0)):
                for k in range(KC):
                    last = e.matmul(ps[bk][:, 0:N], lhsT=slots[:, s, o8 + k, :], rhs=h[:, k, t0:t1],
                                    start=(k == 0), stop=(k == KC - 1))
            return last
        pg.op("pe", mm, reads=(("slot", s1), ("slot", s2)) + tuple(("h", k, ti) for k in range(KC)),
              writes=(("ps", bb), ("ps", bc), ("ps", bv)))
        pe_tick()
        tv = tmpbuf()

        def fv(e):
            return e.activation(out=tmp[:, tv, 0:N], in_=ps[bv][:, 0:N], func=AF.Copy)
        pg.op("act", fv, reads=(("ps", bv),), writes=(("tmp", tv),))
        offs = []
        off = 0
        for (sg, lo, hi) in pcs:
            off += 2
            offs.append(off)
            if sg == 0:
                if ti == 0:
                    pg.op("dve", lambda e: e.memset(ztl[:, zi, 0:2], 0.0), writes=(zk,))
                else:
                    pN = TILES[ti - 1][1] - TILES[ti - 1][0]
                    pg.op("dve", lambda e, pN=pN: e.tensor_copy(out=ztl[:, zi, 0:2], in_=ztl[:, zi, pN:pN + 2]),
                          reads=(zk,), writes=(zk,))
            else:
                pg.op("dve", lambda e, off=off, sg=sg: e.tensor_copy(out=ztl[:, zi, off - 2:off], in_=scs[:, o, sg - 1, c, :]),
                      reads=("st_c",), writes=(zk,))
            off += hi - lo
        tot = off
        for (sg, lo, hi), of in zip(pcs, offs):
            def fzz(e, lo=lo, hi=hi, of=of):
                return e.tensor_tensor(out=ztl[:, zi, of:of + hi - lo], in0=ps[bc][:, lo - t0:hi - t0],
                                       in1=tmp[:, tv, lo - t0:hi - t0], op=ALU.mult)
            pg.op("dve", fzz, reads=(("ps", bc), ("tmp", tv)), writes=(zk,))
        if ti == 0:
            pg.op("dve", lambda e: e.tensor_tensor(out=ztl[:, zi, 2:2 + HALO], in0=ztl[:, zi, 2:2 + HALO],
                                                   in1=P("hmask", HALO), op=ALU.mult), reads=("prm",), writes=(zk,))
        Wb = tot - 2
        ta, tb_ = tmpbuf(), tmpbuf()
        cw = _off["cconvw"] + (o * 8 + c) * 3

        def f0(e):
            return e.activation(out=tmp[:, ta, 0:Wb], in_=ztl[:, zi, 0:Wb], func=AF.Identity, scale=prm[:, cw:cw + 1])
        pg.op("act", f0, reads=(zk, "prm"), writes=(("tmp", ta),))

        def f1(e):
            return e.scalar_tensor_tensor(out=tmp[:, tb_, 0:Wb], in0=ztl[:, zi, 1:1 + Wb], scalar=prm[:, cw + 1:cw + 2],
                                          in1=tmp[:, ta, 0:Wb], op0=ALU.mult, op1=ALU.add)
        pg.op("dve", f1, reads=(zk, ("tmp", ta)), writes=(("tmp", tb_),))

        def f2(e):
            return e.scalar_tensor_tensor(out=tmp[:, ta, 0:Wb], in0=ztl[:, zi, 2:2 + Wb], scalar=prm[:, cw + 2:cw + 3],
                                          in1=tmp[:, tb_, 0:Wb], op0=ALU.mult, op1=ALU.add)
        pg.op("dve", f2, reads=(zk, ("tmp", tb_)), writes=(("tmp", ta),))
        tbg = tmpbuf()
        pg.op("act", lambda e: e.activation(out=tmp[:, tbg, 0:N], in_=ps[bb][:, 0:N], func=AF.Copy),
              reads=(("ps", bb),), writes=(("tmp", tbg),))
        for (sg, lo, hi), of in zip(pcs, offs):
            def fo(e, lo=lo, hi=hi, of=of):
                return e.tensor_tensor(out=act[:, cl, lo:hi], in0=tmp[:, ta, of - 2:of - 2 + hi - lo],
                                       in1=tmp[:, tbg, lo - t0:hi - t0], op=ALU.mult)
            pg.op("dve", fo, reads=(("tmp", ta), ("tmp", tbg)), writes=(("act", cl, ti),))
        if ti == NT - 1:
            for (sg, lo, hi), of in zip(pcs, offs):
                e0 = of + (hi - lo) - 2

                def fd(e, sg=sg, e0=e0):
                    return e.dma_start(out=oc_d[:, o, sg, c, :], in_=ztl[:, zi, e0:e0 + 2])
                pg.dma("sp", "outs", fd, reads=(zk,))

    def c_pieces(S, c):
        win = c_in[S["l"] // 2]
        s1 = load_piece([(0, 8, wcols(win, c * 128)), (8, 8, wcols(win, D + c * 128))])
        s2 = load_piece([(0, 8, wcols(win, 2 * D + c * 128))])
        return s1, s2

    def c_preload(S):
        S["fs"] = [c_pieces(S, 0), c_pieces(S, 1)]
        held.update(S["fs"][0] + S["fs"][1])

    def c_first_tile(S, ti):
        for c in range(2):
            c_chunk_tile(S, c, c, c, S["fs"][c][0], S["fs"][c][1], ti)

    def c_rest(S, boundary):
        l = S["l"]
        o = l // 2
        for hf in range(2):
            for cl in range(4):
                c = hf * 4 + cl
                if hf == 0 and cl < 2:
                    continue
                s1, s2 = c_pieces(S, c)
                for ti in range(NT):
                    c_chunk_tile(S, c, cl, 0, s1, s2, ti)
                pump_mod(2)
            if hf == 1:
                out_phase(l, 1, c_out[o], hf * 4, 4, pre_loop=boundary[0], after_tile=boundary[1])
            else:
                out_phase(l, 1, c_out[o], hf * 4, 4)

    def ab_preload(S):
        win = ab_in[S["l"] // 2]
        S["fs"] = [load_piece([(0, 8, wcols(win, c * 128)), (8, 8, wcols(win, 512 + c * 128))]) for c in range(4)]
        held.update(S["fs"])

    def ab_first_tile(S, ti):
        e_ = S["l"] // 2
        sA = S["fs"]
        t0, t1 = TILES[ti]
        N = t1 - t0
        pcs = pieces(t0, t1)
        offs = []
        off = 0
        for (sg, lo, hi) in pcs:
            off += 30
            offs.append(off)
            off += hi - lo
        tot = off
        Wb = tot - 30

        def inproj_glu(c):
            bu_, bg_ = bank(), bank()

            def mm(e, s=sA[c]):
                last = None
                for (bk, o8) in ((bu_, 0), (bg_, 8)):
                    for k in range(KC):
                        last = e.matmul(ps[bk][:, 0:N], lhsT=slots[:, s, o8 + k, :], rhs=h[:, k, t0:t1],
                                        start=(k == 0), stop=(k == KC - 1))
                return last
            pg.op("pe", mm, reads=(("slot", sA[c]),) + tuple(("h", k, ti) for k in range(KC)),
                  writes=(("ps", bu_), ("ps", bg_)))
            pe_tick()
            tsg = tmpbuf()
            pg.op("act", lambda e: e.activation(out=tmp[:, tsg, 0:N], in_=ps[bg_][:, 0:N], func=AF.Tanh, scale=0.5),
                  reads=(("ps", bg_),), writes=(("tmp", tsg),))
            ak = ("abf", c)
            for (sg, lo, hi), of in zip(pcs, offs):
                if sg == 0:
                    if ti == 0:
                        pg.op("dve", lambda e: e.memset(abf[:, c, 0:30], 0.0), writes=(ak,))
                    else:
                        pN = TILES[ti - 1][1] - TILES[ti - 1][0]
                        pg.op("dve", lambda e, pN=pN: e.tensor_copy(out=abf[:, c, 0:30], in_=abf[:, c, pN:pN + 30]),
                              reads=(ak,), writes=(ak,))
                else:
                    pg.op("dve", lambda e, of=of, sg=sg: e.tensor_scalar(out=abf[:, c, of - 30:of], in0=sa[:, e_, sg - 1, c, :],
                                                                         scalar1=2.0, scalar2=None, op0=ALU.mult),
                          reads=("st_a",), writes=(ak,))

                def fa(e, lo=lo, hi=hi, of=of):
                    return e.scalar_tensor_tensor(out=abf[:, c, of:of + hi - lo], in0=tmp[:, tsg, lo - t0:hi - t0], scalar=1.0,
                                                  in1=ps[bu_][:, lo - t0:hi - t0], op0=ALU.add, op1=ALU.mult)
                pg.op("dve", fa, reads=(("ps", bu_), ("tmp", tsg)), writes=(ak,))
                if ti == NT - 1:
                    def fst(e, hi=hi, sg=sg):
                        return e.scalar_tensor_tensor(out=astg[:, sg, c, :], in0=tmp[:, tsg, hi - 30 - t0:hi - t0], scalar=1.0,
                                                      in1=ps[bu_][:, hi - 30 - t0:hi - t0], op0=ALU.add, op1=ALU.mult)
                    pg.op("dve", fst, reads=(("ps", bu_), ("tmp", tsg)), writes=("astg",))
                    pg.op("dve", lambda e, sg=sg: e.tensor_scalar(out=astg[:, sg, c, :], in0=astg[:, sg, c, :], scalar1=0.5,
                                                                  scalar2=None, op0=ALU.mult), reads=("astg",), writes=("astg",))
            if ti == 0:
                pg.op("dve", lambda e: e.tensor_tensor(out=abf[:, c, 30:30 + HALO], in0=abf[:, c, 30:30 + HALO],
                                                       in1=P("hmask", HALO), op=ALU.mult), reads=("prm",), writes=(ak,))

        def gen(c):
            for hf_, (k0, k1) in enumerate(((0, 16), (16, 31))):
                def fld(e, k0=k0, k1=k1, hf_=hf_):
                    n = k1 - k0
                    return e.dma_start(out=dg[:, k0:k1, :].rearrange("p k j -> p (k j)"), in_=dgd[e_, c, hf_, :, 0:n * 128])
                pg.dma("sp", "dgl%d" % hf_, fld, reads=(("dgd", e_, hf_),), writes=(("dg", hf_),))

        def conv(c):
            ak = ("abf", c)
            bcv = bank()

            def mc0(e):
                last = None
                for k in range(16):
                    last = e.matmul(ps[bcv][:, 0:Wb], lhsT=dg[:, k, :], rhs=abf[:, c, k:k + Wb], start=(k == 0), stop=False)
                return last
            pg.op("pe", mc0, reads=(("dg", 0), ak), writes=(("ps", bcv),))

            def mc1(e):
                last = None
                for k in range(16, 31):
                    last = e.matmul(ps[bcv][:, 0:Wb], lhsT=dg[:, k, :], rhs=abf[:, c, k:k + Wb], start=False, stop=(k == 30))
                return last
            pg.op("pe", mc1, reads=(("dg", 1), ak), writes=(("ps", bcv),))
            pe_tick()
            cbo = _off["aconvb"] + e_ * 4 + c
            pg.op("act", lambda e: e.activation(out=cob[:, c, 0:Wb], in_=ps[bcv][:, 0:Wb], func=AF.Identity,
                                                bias=prm[:, cbo:cbo + 1], scale=1.0),
                  reads=(("ps", bcv), "prm"), writes=(("cob", c),))

        inproj_glu(0)
        gen(0)
        for c in range(4):
            if c + 1 < 4:
                inproj_glu(c + 1)
            conv(c)
            if c + 1 < 4:
                gen(c + 1)
        if ti == NT - 1:
            for (sg, lo, hi) in pcs:
                def fd(e, sg=sg):
                    return e.dma_start(out=oa_d[:, e_, sg, :, :], in_=astg[:, sg, :, :])
                pg.dma("sp", "outs", fd, reads=("astg",))

        def ln_tail():
            b1, b2 = bank(), bank()

            def mm1(e):
                last = None
                for c in range(4):
                    last = e.matmul(ps[b1][:, 0:Wb], lhsT=ones_f[:, :], rhs=cob[:, c, 0:Wb], start=(c == 0), stop=(c == 3))
                return last
            pg.op("pe", mm1, reads=tuple(("cob", c) for c in range(4)) + ("ones",), writes=(("ps", b1),))
            sqt = []
            for c in range(4):
                tq = tmpbuf()
                sqt.append(tq)
                pg.op("act", lambda e, c=c, tq=tq: e.activation(out=tmp[:, tq, 0:Wb], in_=cob[:, c, 0:Wb], func=AF.Square),
                      reads=(("cob", c),), writes=(("tmp", tq),))

            def mm2(e):
                last = None
                for c in range(4):
                    last = e.matmul(ps[b2][:, 0:Wb], lhsT=ones_f[:, :], rhs=tmp[:, sqt[c], 0:Wb], start=(c == 0), stop=(c == 3))
                return last
            pg.op("pe", mm2, reads=tuple(("tmp", tq) for tq in sqt) + ("ones",), writes=(("ps", b2),))
            pg.op("dve", lambda e: e.tensor_scalar(out=lnm[:, 0, 0:Wb], in0=ps[b1][:, 0:Wb], scalar1=1.0 / 512, scalar2=None, op0=ALU.mult),
                  reads=(("ps", b1),), writes=("lnmean",))
            tm = tmpbuf()
            pg.op("dve", lambda e: e.tensor_tensor(out=tmp[:, tm, 0:Wb], in0=lnm[:, 0, 0:Wb], in1=lnm[:, 0, 0:Wb], op=ALU.mult),
                  reads=("lnmean",), writes=(("tmp", tm),))
            pg.op("dve", lambda e: e.scalar_tensor_tensor(out=lnm[:, 1, 0:Wb], in0=ps[b2][:, 0:Wb], scalar=1.0 / 512, in1=tmp[:, tm, 0:Wb],
                                                          op0=ALU.mult, op1=ALU.subtract),
                  reads=(("ps", b2), ("tmp", tm)), writes=("lnrstd",))
            pg.op("act", lambda e: e.activation(out=lnm[:, 1, 0:Wb], in_=lnm[:, 1, 0:Wb], func=AF.Sqrt, bias=P("epsc", 1), scale=1.0),
                  reads=("lnrstd", "prm"), writes=("lnrstd",))
            pg.op("dve", lambda e: e.reciprocal(out=lnm[:, 1, 0:Wb], in_=lnm[:, 1, 0:Wb]), reads=("lnrstd",), writes=("lnrstd",))
            for c in range(4):
                t1_, t2_ = tmpbuf(), tmpbuf()
                pg.op("dve", lambda e, c=c, t1_=t1_: e.tensor_tensor(out=tmp[:, t1_, 0:Wb], in0=cob[:, c, 0:Wb], in1=lnm[:, 0, 0:Wb], op=ALU.subtract),
                      reads=(("cob", c), "lnmean"), writes=(("tmp", t1_),))
                pg.op("dve", lambda e, t1_=t1_, t2_=t2_: e.tensor_tensor(out=tmp[:, t2_, 0:Wb], in0=tmp[:, t1_, 0:Wb], in1=lnm[:, 1, 0:Wb], op=ALU.mult),
                      reads=(("tmp", t1_), "lnrstd"), writes=(("tmp", t2_),))
                go = _off["alng"] + e_ * 4 + c
                bo = _off["alnb"] + e_ * 4 + c
                for (sg, lo, hi), of in zip(pcs, offs):
                    def fsl(e, c=c, t2_=t2_, lo=lo, hi=hi, of=of, go=go, bo=bo):
                        return e.activation(out=act[:, c, lo:hi], in_=tmp[:, t2_, of - 30:of - 30 + hi - lo], func=AF.Silu,
                                            bias=prm[:, bo:bo + 1], scale=prm[:, go:go + 1])
                    pg.op("act", fsl, reads=(("tmp", t2_), "prm"), writes=(("act", c, ti),))
        pe_defer.append([2, ln_tail])

    def ab_rest(S, boundary):
        l = S["l"]
        e_ = l // 2
        win = ab_in[e_]
        out_phase(l, 1, ab_out[e_], 0, 4)
        for g in range(4):
            wnd = 2 << g
            s = load_piece([(0, 8, wcols(win, 1024 + g * 128)), (8, 1, b_wg[e_, g].rearrange("p (o c) -> p o c", o=1))])
            for ti, (t0, t1) in enumerate(TILES):
                N = t1 - t0
                pcs = pieces(t0, t1)
                offs = []
                off = 0
                for (sg, lo, hi) in pcs:
                    off += 15
                    offs.append(off)
                    off += hi - lo
                tot = off
                Wb = tot - 15
                bb = bank()
                bi = st["bub"]
                st["bub"] = 1 - bi
                bk_ = ("bub", bi)

                def mm(e, s=s, bb=bb, t0=t0, t1=t1, N=N):
                    last = None
                    for k in range(KC):
                        last = e.matmul(ps[bb][:, 0:N], lhsT=slots[:, s, k, :], rhs=h[:, k, t0:t1], start=(k == 0), stop=(k == KC - 1))
                    return last
                pg.op("pe", mm, reads=(("slot", s),) + tuple(("h", k, ti) for k in range(KC)), writes=(("ps", bb),))
                pe_tick()
                for (sg, lo, hi), of in zip(pcs, offs):
                    if sg == 0:
                        if ti == 0:
                            pg.op("dve", lambda e, bi=bi: e.memset(bub[:, bi, 0:15], 0.0), writes=(bk_,))
                        else:
                            pN = TILES[ti - 1][1] - TILES[ti - 1][0]
                            pg.op("dve", lambda e, bi=bi, pN=pN: e.tensor_copy(out=bub[:, bi, 0:15], in_=bub[:, 1 - bi, pN:pN + 15]),
                                  reads=(("bub", 1 - bi),), writes=(bk_,))
                    else:
                        pg.op("dve", lambda e, bi=bi, of=of, sg=sg, g=g: e.tensor_copy(out=bub[:, bi, of - 15:of], in_=sbs[:, e_, sg - 1, g, :]),
                              reads=("st_b",), writes=(bk_,))
                    pg.op("act", lambda e, bi=bi, bb=bb, lo=lo, hi=hi, of=of, t0=t0: e.activation(
                        out=bub[:, bi, of:of + hi - lo], in_=ps[bb][:, lo - t0:hi - t0], func=AF.Copy),
                        reads=(("ps", bb),), writes=(bk_,))
                if ti == 0:
                    pg.op("dve", lambda e, bi=bi: e.tensor_tensor(out=bub[:, bi, 15:15 + HALO], in0=bub[:, bi, 15:15 + HALO],
                                                                  in1=P("hmask", HALO), op=ALU.mult), reads=("prm",), writes=(bk_,))
                cur = None
                sh = 1
                for lev in range(g + 1):
                    tn = tmpbuf()
                    if cur is None:
                        pg.op("dve", lambda e, bi=bi, tn=tn, sh=sh, tot=tot: e.tensor_tensor(
                            out=tmp[:, tn, sh:tot], in0=bub[:, bi, sh:tot], in1=bub[:, bi, 0:tot - sh], op=ALU.add),
                            reads=(bk_,), writes=(("tmp", tn),))
                    else:
                        pg.op("dve", lambda e, tn=tn, cur=cur, sh=sh, tot=tot: e.tensor_tensor(
                            out=tmp[:, tn, sh:tot], in0=tmp[:, cur, sh:tot], in1=tmp[:, cur, 0:tot - sh], op=ALU.add),
                            reads=(("tmp", cur),), writes=(("tmp", tn),))
                    cur = tn
                    sh *= 2
                di = st["dbf"]
                st["dbf"] = 1 - di
                pg.op("dve", lambda e, bi=bi, cur=cur, di=di, Wb=Wb, wnd=wnd: e.scalar_tensor_tensor(
                    out=dbf[:, di, 0:Wb], in0=tmp[:, cur, 15:15 + Wb], scalar=1.0 / wnd, in1=bub[:, bi, 15:15 + Wb],
                    op0=ALU.mult, op1=ALU.subtract), reads=(("tmp", cur), bk_), writes=(("dbf", di),))
                if ti == 0:
                    tf = tmpbuf()
                    io = _off["invcnt"] + g * 16
                    pg.op("dve", lambda e, cur=cur, tf=tf, io=io: e.tensor_tensor(
                        out=tmp[:, tf, 0:16], in0=tmp[:, cur, 15 + HALO:15 + HALO + 16], in1=prm[:, io:io + 16], op=ALU.mult),
                        reads=(("tmp", cur), "prm"), writes=(("tmp", tf),))
                    pg.op("dve", lambda e, bi=bi, tf=tf, di=di: e.tensor_tensor(
                        out=dbf[:, di, HALO:HALO + 16], in0=tmp[:, tf, 0:16], in1=bub[:, bi, 15 + HALO:15 + HALO + 16], op=ALU.subtract),
                        reads=(("tmp", tf), bk_), writes=(("dbf", di),))
                if ti == NT - 1:
                    for (sg, lo, hi), of in zip(pcs, offs):
                        e0 = of + (hi - lo) - 15

                        def fd(e, bi=bi, sg=sg, e0=e0, g=g):
                            return e.dma_start(out=ob_d[:, e_, sg, g, :], in_=bub[:, bi, e0:e0 + 15])
                        pg.dma("sp", "outs", fd, reads=(bk_,))

                def grp(s=s, di=di, Wb=Wb, g=g, ti=ti, pcs=pcs, offs=offs):
                    bd = bank()
                    pg.op("pe", lambda e: e.matmul(ps[bd][:, 0:Wb], lhsT=slots[:, s, 8, :], rhs=dbf[:, di, 0:Wb], start=True, stop=True),
                          reads=(("slot", s), ("dbf", di)), writes=(("ps", bd),))
                    so = _off["bscale"] + e_ * 4 + g
                    for (sg, lo, hi), of in zip(pcs, offs):
                        pg.op("act", lambda e, lo=lo, hi=hi, of=of: e.activation(
                            out=act[:, g, lo:hi], in_=ps[bd][:, of - 15:of - 15 + hi - lo], func=AF.Identity, scale=prm[:, so:so + 1]),
                            reads=(("ps", bd), "prm"), writes=(("act", g, ti),))
                pe_defer.append([1, grp])
            pump_mod(1)
        pe_flush()
        out_phase(l, 1, ab_out[e_], 4, 4, pre_loop=boundary[0], after_tile=boundary[1])

    def f_prm(e):
        return e.dma_start(out=prm[:, :], in_=prm_d)
    pg.dma("sp", "ld_prm", f_prm, writes=("prm",))
    pg.dma("sp", "ld_cv", lambda e: e.dma_start(out=cv[:, :, :], in_=cv_d), writes=("cv",))
    pg.dma("sp", "ld_id", lambda e: e.dma_start(out=tmp[:, 0, 0:128], in_=id_d), writes=(("tmp", 0),))
    x_loads = []
    for ti, (t0, t1) in enumerate(TILES):
        def fxl(e, t0=t0, t1=t1):
            return e.dma_start(out=x[:, :, t0:t1], in_=xT_d[:, :, t0:t1])
        x_loads.append((ti, fxl))
    ti0, f0_ = x_loads[0]
    pg.dma("sp", "ld_x0", f0_, writes=tuple(("x", m, 0) for m in range(KC)))
    pg.dma("sp", "ld_sa", lambda e: e.dma_start(out=sa[:, :, :, :, :].rearrange("p a b c d -> p (a b c d)"), in_=sa_d), writes=("st_a",))
    pg.dma("sp", "ld_sb", lambda e: e.dma_start(out=sbs[:, :, :, :, :].rearrange("p a b c d -> p (a b c d)"), in_=sb_d), writes=("st_b",))
    pg.dma("sp", "ld_sc", lambda e: e.dma_start(out=scs[:, :, :, :, :].rearrange("p a b c d -> p (a b c d)"), in_=sc_d), writes=("st_c",))

    pg.op("dve", lambda e: e.memset(ones_bf[:, :], 1.0), writes=("ones",))
    pg.op("dve", lambda e: e.memset(ones_f[:, :], 1.0), writes=("ones",))
    pg.op("dve", lambda e: e.tensor_copy(out=ident_bf[:, :], in_=tmp[:, 0, 0:128]), reads=(("tmp", 0),), writes=("ident",))
    st["tmp"] = 1
    pg.op("act", lambda e: e.activation(out=cact[:, :, :], in_=cv[:, :, :], func=AF.Silu), reads=("cv",), writes=("cact",))

    def gen_diag(e2, anchor):
        o_ = _off["aconvw"] + e2 * 124
        pg.op("dve", lambda e, o_=o_: e.tensor_scalar(out=awh[:, :], in0=prm[:, o_:o_ + 124], scalar1=0.5, scalar2=None, op0=ALU.mult),
              reads=("prm", anchor), writes=("awh",))
        for c in range(4):
            for hf_, (k0, k1) in enumerate(((0, 16), (16, 31))):
                def fdg(e, c=c, k0=k0, k1=k1):
                    n = k1 - k0
                    return e.tensor_tensor(out=dg[:, k0:k1, :],
                                           in0=ident_bf[:, :].unsqueeze(1).to_broadcast([128, n, 128]),
                                           in1=awh[:, c * 31 + k0:c * 31 + k1].unsqueeze(2).to_broadcast([128, n, 128]),
                                           op=ALU.mult)
                pg.op("dve", fdg, reads=("ident", "awh"), writes=(("dg", hf_),))

                def fst_(e, e2=e2, c=c, k0=k0, k1=k1, hf_=hf_):
                    n = k1 - k0
                    return e.dma_start(out=dgd[e2, c, hf_, :, 0:n * 128], in_=dg[:, k0:k1, :].rearrange("p k j -> p (k j)"))
                pg.dma("sp", "dgw%d" % hf_, fst_, reads=(("dg", hf_),), writes=(("dgd", e2, hf_),))

    subs = []
    for l in range(DEPTH):
        subs.append({"kind": "ffn", "l": l, "s": 0, "f": 0})
        subs.append({"kind": "ab" if l % 2 == 0 else "c", "l": l, "s": 1})
        subs.append({"kind": "ffn", "l": l, "s": 2, "f": 1})
    PRE = {"ffn": ffn_preload, "ab": ab_preload, "c": c_preload}
    FIRST = {"ffn": ffn_first_tile, "ab": ab_first_tile, "c": c_first_tile}
    REST = {"ffn": ffn_rest, "ab": ab_rest, "c": c_rest}
    LAG = 2
    NDEF = 5

    for i in range(36):
        mod_pending.append((0, i))
    S0 = subs[0]
    for i in range(12):
        if i == 8:
            PRE["ffn"](S0)
        s_before = st["slot"]
        pump_mod(1)
        if 1 <= i <= 5:
            ti_, fx_ = x_loads[i]
            pg.dma("sp", "ld_x%d" % ti_, fx_, reads=(("slot", s_before),), writes=tuple(("x", m, ti_) for m in range(KC)))
    for i in range(36):
        mod_pending.append((1, i))

    for ti in range(NT):
        norm_tile(0, 0, ti)
        if ti >= 1:
            FIRST["ffn"](S0, ti - 1)
    FIRST["ffn"](S0, NT - 1)
    held.clear()

    for i, S in enumerate(subs):
        nxt = subs[i + 1] if i + 1 < len(subs) else None
        if i == 0:
            gen_diag(0, ("act", 0, NT - 1))
        if S["kind"] == "ffn" and S["s"] == 0 and S["l"] == 1:
            gen_diag(1, ("act", 0, NT - 1))
        if S["kind"] == "ffn" and S["s"] == 0 and S["l"] >= 1 and S["l"] + 1 < DEPTH:
            for i_ in range(36):
                mod_pending.append((S["l"] + 1, i_))

        def pre(nxt=nxt):
            if nxt is not None:
                PRE[nxt["kind"]](nxt)

        def cb(ti, nxt=nxt):
            if nxt is None:
                norm_tile(DEPTH - 1, 0, ti, final=True, defer=3)
            else:
                norm_tile(nxt["l"], nxt["s"], ti, defer=NDEF)
                if ti - LAG >= 0:
                    FIRST[nxt["kind"]](nxt, ti - LAG)
        REST[S["kind"]](S, (pre, cb))
        pe_flush()
        if nxt is not None:
            for ti in range(NT - LAG, NT):
                FIRST[nxt["kind"]](nxt, ti)
        held.clear()
        pe_flush()

    pg.final_wait("sp", ["outs"])
    sim = pg.finalize()
    _SIM["time_us"] = sim
    _SIM["pg"] = pg
    _SIM["busy"] = {e: sum(pg.ops[i]["dur"] for i in pg.order[e]) for e in pg.ENGS}

    with nc.Block() as block:
        @block.tensor
        def _(e):
            pg.run("pe", e)

        @block.scalar
        def _(e):
            pg.run("act", e)

        @block.vector
        def _(e):
            pg.run("dve", e)

        @block.gpsimd
        def _(e):
            pg.run("pool", e)

        @block.sync
        def _(e):
            pg.run("sp", e)


def _fm(v):
    v = np.asarray(v, dtype=np.float32)
    n = v.shape[-1] // 128
    v = v.reshape(v.shape[:-1] + (n, 128))
    return np.ascontiguousarray(np.moveaxis(v, -1, 0))


_NC_CACHE = {}


def kernel(x_prompt, x_sample, state_conv_a, state_pool_b, state_conv_c, c_prompt, c_sample,
           ada_w, ada_b, norm_g, ffn_w_gu, ffn_w_down, ab_w_in, a_conv_w, a_conv_b, a_ln_g,
           a_ln_b, b_w_group, b_scale, ab_w_out, c_w_in, c_conv_w, c_w_out, final_g):
    f32 = np.float32
    xp = np.asarray(x_prompt, f32)[0]
    xs = np.asarray(x_sample, f32)
    prm_base = np.zeros((128, NPRM), f32)

    def put(name, arr):
        arr = np.ascontiguousarray(arr, dtype=f32).reshape(128, -1)
        prm_base[:, _off[name]:_off[name] + arr.shape[1]] = arr
    put("normg", _fm(norm_g))
    put("finalg", _fm(final_g))
    put("adab", _fm(ada_b))
    put("aconvw", np.transpose(_fm(a_conv_w), (0, 1, 3, 2)))
    put("aconvb", _fm(a_conv_b))
    put("alng", _fm(a_ln_g))
    put("alnb", _fm(a_ln_b))
    put("bscale", _fm(b_scale))
    put("cconvw", np.transpose(_fm(c_conv_w), (0, 1, 3, 2)))
    sca = np.asarray(state_conv_a, f32)
    spb = np.asarray(state_pool_b, f32)
    scc = np.asarray(state_conv_c, f32)
    cpr = np.asarray(c_prompt, f32)
    csa = np.asarray(c_sample, f32)
    shared = {
        "ada_w": np.ascontiguousarray(ada_w, dtype=f32), "ffn_w_gu": np.ascontiguousarray(ffn_w_gu, dtype=f32),
        "ffn_w_down": np.ascontiguousarray(ffn_w_down, dtype=f32), "ab_w_in": np.ascontiguousarray(ab_w_in, dtype=f32),
        "b_w_group": np.ascontiguousarray(b_w_group, dtype=f32), "ab_w_out": np.ascontiguousarray(ab_w_out, dtype=f32),
        "c_w_in": np.ascontiguousarray(c_w_in, dtype=f32), "c_w_out": np.ascontiguousarray(c_w_out, dtype=f32),
    }
    in_maps = []
    for i in range(NCORES):
        toks = np.zeros((T, D), f32)
        if i == 0:
            toks[HALO:HALO + MAIN] = xp[0:MAIN]
        else:
            toks[0:HALO + MAIN] = xp[i * MAIN - HALO:(i + 1) * MAIN]
        toks[HALO + MAIN:HALO + MAIN + LS] = xs[2 * i]
        toks[HALO + MAIN + LS:] = xs[2 * i + 1]
        xT = np.ascontiguousarray(toks.reshape(T, KC, 128).transpose(2, 1, 0))
        cvec = np.stack([cpr[0], csa[2 * i], csa[2 * i + 1]], 0)
        cvf = np.ascontiguousarray(cvec.reshape(3, KC, 128).transpose(2, 1, 0))
        prm = prm_base.copy()
        prm[:, _off["hmask"]:_off["hmask"] + HALO] = 0.0 if i == 0 else 1.0
        ic = np.zeros((4, 16), f32)
        for g in range(4):
            w = 2 << g
            for p_ in range(16):
                ic[g, p_] = (1.0 / min(w, p_ + 1)) if i == 0 else (1.0 / w)
        prm[:, _off["invcnt"]:_off["invcnt"] + 64] = ic.reshape(1, 64)
        prm[:, _off["epsc"]] = EPS
        sa_i = np.ascontiguousarray(np.transpose(_fm(sca[:, 2 * i:2 * i + 2]), (0, 1, 2, 4, 3))).reshape(128, -1)
        sb_i = np.ascontiguousarray(np.transpose(_fm(spb[:, 2 * i:2 * i + 2]), (0, 1, 2, 4, 3))).reshape(128, -1)
        sc_i = np.ascontiguousarray(np.transpose(_fm(scc[:, 2 * i:2 * i + 2]), (0, 1, 2, 4, 3))).reshape(128, -1)
        m = {"xT": xT, "cvec": cvf, "prm": prm, "sa": sa_i, "sb": sb_i, "scn": sc_i, "ident": np.eye(128, dtype=f32)}
        m.update(shared)
        in_maps.append(m)

    if "nc" not in _NC_CACHE:
        _NC_CACHE["nc"] = build_nc()
    nc = _NC_CACHE["nc"]
    res = run_bass_kernel_spmd(nc, in_maps, core_ids=list(range(NCORES)))
    R = res.results

    y_prompt = np.zeros((1, NCORES * MAIN, D), f32)
    y_sample = np.zeros((2 * NCORES, LS, D), f32)
    na_s = np.zeros((2, 2 * NCORES, 30, 512), f32)
    nb_s = np.zeros((2, 2 * NCORES, 15, 512), f32)
    nc_s = np.zeros((2, 2 * NCORES, 2, 1024), f32)
    for i in range(NCORES):
        yT = np.asarray(R[i]["yT"], f32)
        rows = yT.transpose(2, 1, 0).reshape(MAIN + 2 * LS, D)
        y_prompt[0, i * MAIN:(i + 1) * MAIN] = rows[:MAIN]
        y_sample[2 * i] = rows[MAIN:MAIN + LS]
        y_sample[2 * i + 1] = rows[MAIN + LS:]
        oa = np.asarray(R[i]["oa"], f32).transpose(1, 2, 4, 3, 0).reshape(2, 3, 30, 512)
        ob = np.asarray(R[i]["ob"], f32).transpose(1, 2, 4, 3, 0).reshape(2, 3, 15, 512)
        oc = np.asarray(R[i]["oc"], f32).transpose(1, 2, 4, 3, 0).reshape(2, 3, 2, 1024)
        na_s[:, 2 * i:2 * i + 2] = oa[:, 1:3]
        nb_s[:, 2 * i:2 * i + 2] = ob[:, 1:3]
        nc_s[:, 2 * i:2 * i + 2] = oc[:, 1:3]
        if i == NCORES - 1:
            na_p = oa[:, 0:1].copy()
            nb_p = ob[:, 0:1].copy()
            nc_p = oc[:, 0:1].copy()
    return (y_prompt, y_sample, na_p, nb_p, nc_p, na_s, nb_s, nc_s)
```

```python
import numpy as np
from contextlib import ExitStack
import concourse.bass as bass
import concourse.mybir as mybir
from concourse.bass_utils import run_bass_kernel_spmd

F32 = mybir.dt.float32
BF16 = mybir.dt.bfloat16
AF = mybir.ActivationFunctionType
ALU = mybir.AluOpType

NCORES = 8
D = 1024
KC = 8
DFF = 2816
NJ = 22
DEPTH = 4
HALO = 64
MAIN = 2048
LS = 32
T = HALO + MAIN + 2 * LS
SEGS = [(0, HALO + MAIN), (HALO + MAIN, HALO + MAIN + LS), (HALO + MAIN + LS, T)]
TILES = [(0, 384), (384, 768), (768, 1152), (1152, 1536), (1536, 1856), (1856, 2176)]
NMAX = 384
EPS = 1e-6
NSLOT = 6
SLOTW = 2048
ACTK = 4

_off = {}
_o = 0
for _n, _w in [("normg", 96), ("finalg", 8), ("adab", 288), ("aconvw", 248), ("aconvb", 8), ("alng", 8),
               ("alnb", 8), ("bscale", 8), ("cconvw", 48), ("hmask", 64), ("invcnt", 64), ("epsc", 1)]:
    _off[_n] = _o
    _o += _w
NPRM = _o


def pieces(t0, t1):
    out = []
    for s, (a, b) in enumerate(SEGS):
        lo, hi = max(a, t0), min(b, t1)
        if lo < hi:
            out.append((s, lo, hi))
    return out


class _FakeIns:
    def then_inc(self, *a, **k):
        return self


class _FakeEng:
    def __init__(self):
        self.cost = 0.0
        self.tag = None
        self.nbytes = 0

    @staticmethod
    def _free(ap):
        n = 1
        for d in ap.shape[1:]:
            n *= int(d)
        return n

    def matmul(self, out, lhsT=None, rhs=None, **k):
        c = max(self._free(out) / 2400.0 + 0.004, 0.035)
        if lhsT is not None and lhsT.dtype == F32:
            c *= 2.2
        self.cost += c
        return _FakeIns()

    def activation(self, out=None, in_=None, func=None, **k):
        self.cost += 0.22 + self._free(out) * 0.001
        if func in (AF.Silu, AF.Tanh):
            self.tag = "A"
        elif func == AF.Sqrt:
            self.tag = "B"
        return _FakeIns()

    def reciprocal(self, out=None, in_=None, **k):
        self.cost += 0.06 + self._free(out) * 0.0064
        return _FakeIns()

    def scalar_tensor_tensor(self, out=None, **k):
        self.cost += 0.06 + self._free(out) * 0.00146
        return _FakeIns()

    def dma_start(self, out=None, in_=None, **k):
        n = 1
        for d in out.shape:
            n *= int(d)
        self.nbytes += n * (4 if in_.dtype == F32 else 2)
        return _FakeIns()

    def __getattr__(self, name):
        def generic(*a, **k):
            out = k.get("out", a[0] if a else None)
            self.cost += 0.06 + (self._free(out) * 0.0012 if out is not None else 0.0)
            return _FakeIns()
        return generic


CP_WEIGHT = 20.0


class Prog:
    ENGS = ("pe", "act", "dve", "pool", "sp")

    def __init__(self, nc, ctx):
        self.nc = nc
        self.ctx = ctx
        self.semh = {}
        for n in ("pe", "act", "dve"):
            self.semh[n] = ctx.enter_context(nc.semaphore("s_" + n))
        self.ops = []
        self.lastw = {}
        self.readers = {}
        self.final_sems = []
        self.order = None

    def dma_sem(self, name):
        if name not in self.semh:
            self.semh[name] = self.ctx.enter_context(self.nc.semaphore("d_" + name))
        return name

    def _add(self, eng, fn, sem, inc, reads, writes, dur, xfer, tag):
        idx = len(self.ops)
        deps = set()
        for k in reads:
            w = self.lastw.get(k)
            if w is not None:
                deps.add(w)
        for k in writes:
            w = self.lastw.get(k)
            if w is not None:
                deps.add(w)
            deps.update(self.readers.get(k, ()))
        deps.discard(idx)
        self.ops.append({"eng": eng, "fn": fn, "sem": sem, "inc": inc, "deps": deps, "dur": dur, "xfer": xfer, "tag": tag})
        for k in reads:
            self.readers.setdefault(k, []).append(idx)
        for k in writes:
            self.lastw[k] = idx
            self.readers[k] = []
        return idx

    def op(self, eng, fn, reads=(), writes=()):
        fe = _FakeEng()
        fn(fe)
        return self._add(eng, fn, eng, 1, reads, writes, fe.cost, None, fe.tag)

    def dma(self, queue, semname, fn, reads=(), writes=()):
        self.dma_sem(semname)
        fe = _FakeEng()
        fn(fe)
        issue = 1.0 if queue == "pool" else 0.15
        return self._add(queue, fn, semname, 16, reads, writes, issue, fe.nbytes, None)

    def final_wait(self, queue, semnames):
        self.final_sems = [(queue, s) for s in semnames]

    def finalize(self):
        import heapq
        ops = self.ops
        n = len(ops)
        nd = [len(o["deps"]) for o in ops]
        users = [[] for _ in range(n)]
        for i, o in enumerate(ops):
            for d in o["deps"]:
                users[d].append(i)
        LAT = 0.2
        prio = [0.0] * n
        for i in range(n - 1, -1, -1):
            o = ops[i]
            d_ = o["dur"] + ((o["xfer"] / 200e3 + 2.0) if o["xfer"] is not None else 0.0)
            prio[i] = d_ + (max(prio[u] for u in users[i]) if users[i] else 0.0)
        PW = CP_WEIGHT
        done_t = [0.0] * n
        start_t = [0.0] * n
        free = {e: 0.0 for e in self.ENGS}
        pend = {e: [] for e in self.ENGS}
        avail = {e: [] for e in self.ENGS}
        act_set = [None]
        dma_pipe = [0.0]
        order = {e: [] for e in self.ENGS}
        for i, o in enumerate(ops):
            if nd[i] == 0:
                heapq.heappush(pend[o["eng"]], (0.0, i))
        left = n
        while left:
            best = None
            for e in self.ENGS:
                t = free[e]
                while pend[e] and pend[e][0][0] <= t:
                    j_ = heapq.heappop(pend[e])[1]
                    heapq.heappush(avail[e], (j_ - PW * prio[j_], j_))
                if avail[e]:
                    cand = (t, avail[e][0][1], e, True)
                elif pend[e]:
                    cand = (pend[e][0][0], pend[e][0][1], e, False)
                else:
                    continue
                if best is None or cand[:2] < best[:2]:
                    best = cand
            st_, i, e, from_avail = best
            if from_avail:
                heapq.heappop(avail[e])
            else:
                heapq.heappop(pend[e])
            o = ops[i]
            start_t[i] = st_
            dur = o["dur"]
            if e == "act" and o["tag"] is not None:
                if act_set[0] is not None and act_set[0] != o["tag"]:
                    dur += 1.8
                act_set[0] = o["tag"]
            if o["xfer"] is not None:
                free[e] = st_ + dur
                xs = max(st_ + dur, dma_pipe[0])
                dma_pipe[0] = xs + o["xfer"] / 200e3
                done_t[i] = dma_pipe[0] + 2.0
            else:
                free[e] = st_ + dur
                done_t[i] = st_ + dur
            order[e].append(i)
            left -= 1
            for u in users[i]:
                nd[u] -= 1
                if nd[u] == 0:
                    rt = max(done_t[d] for d in ops[u]["deps"]) + LAT
                    heapq.heappush(pend[ops[u]["eng"]], (rt, u))
        self.order = order
        self.start_t = start_t
        self.done_t = done_t
        self.sim_time = max(done_t) if n else 0.0
        cnt = {}
        self.val = [0] * n
        for e in self.ENGS:
            for i in order[e]:
                sname = ops[i]["sem"]
                cnt[sname] = cnt.get(sname, 0) + ops[i]["inc"]
                self.val[i] = cnt[sname]
        self.cnt = cnt
        return self.sim_time

    def run(self, eng_name, eng):
        ops = self.ops
        waited = {}
        for i in self.order[eng_name]:
            o = ops[i]
            need = {}
            for d in o["deps"]:
                sd = ops[d]["sem"]
                v = self.val[d]
                if need.get(sd, 0) < v:
                    need[sd] = v
            for sd, v in need.items():
                if waited.get(sd, 0) < v:
                    waited[sd] = v
                    eng.wait_ge(self.semh[sd], v)
            ins = o["fn"](eng)
            ins.then_inc(self.semh[o["sem"]], o["inc"])
        for (q, sname) in self.final_sems:
            if q == eng_name and self.cnt.get(sname, 0) > 0:
                eng.wait_ge(self.semh[sname], self.cnt[sname])


_SIM = {}


def build_nc():
    nc = bass.Bass("TRN2", target_bir_lowering=False)
    ctx = ExitStack()
    with ctx:
        _build(nc, ctx)
    return nc


def _build(nc, ctx):
    def din(name, shape):
        return nc.dram_tensor(name, list(shape), F32, kind="ExternalInput").ap()

    def dout(name, shape):
        return nc.dram_tensor(name, list(shape), F32, kind="ExternalOutput").ap()

    xT_d = din("xT", [128, KC, T])
    cv_d = din("cvec", [128, KC, 3])
    prm_d = din("prm", [128, NPRM])
    sa_d = din("sa", [128, 2 * 2 * 4 * 30])
    sb_d = din("sb", [128, 2 * 2 * 4 * 15])
    sc_d = din("scn", [128, 2 * 2 * 8 * 2])
    id_d = din("ident", [128, 128])
    ada_w = din("ada_w", [DEPTH, D, 9 * D])
    w_gu = din("ffn_w_gu", [DEPTH, 2, D, 2 * DFF])
    w_dn = din("ffn_w_down", [DEPTH, 2, DFF, D])
    ab_in = din("ab_w_in", [2, D, 1536])
    b_wg = din("b_w_group", [2, 4, 128, 128])
    ab_out = din("ab_w_out", [2, D, D])
    c_in = din("c_w_in", [2, D, 3 * D])
    c_out = din("c_w_out", [2, D, D])
    yT_d = dout("yT", [128, KC, MAIN + 2 * LS])
    oa_d = dout("oa", [128, 2, 3, 4, 30])
    ob_d = dout("ob", [128, 2, 3, 4, 15])
    oc_d = dout("oc", [128, 2, 3, 8, 2])
    dgd = nc.dram_tensor("dgd", [2, 4, 2, 128, 16 * 128], BF16, kind="Internal").ap()

    def sb_(name, shape, dt=F32):
        return ctx.enter_context(nc.sbuf_tensor(name, list(shape), dt))

    NT = len(TILES)
    x = sb_("x", [128, KC, T])
    h = sb_("h", [128, KC, T], BF16)
    act = sb_("act", [128, ACTK, T], BF16)
    slots = sb_("slots", [128, NSLOT, 16, 128], BF16)
    sqb = sb_("sqb", [128, 2, KC, NMAX], BF16)
    rbuf = sb_("rbuf", [128, 2, NMAX])
    NTMP = 6
    TMPW = 400
    tmp = sb_("tmp", [128, NTMP, TMPW])
    abf = sb_("abf", [128, 4, 30 + 384], BF16)
    astg = sb_("astg", [128, 3, 4, 30])
    dg = sb_("dg", [128, 31, 128], BF16)
    cob = sb_("cob", [128, 4, 384])
    lnm = sb_("lnm", [128, 2, 384])
    bub = sb_("bub", [128, 2, 45 + 384])
    dbf = sb_("dbf", [128, 2, 384], BF16)
    ztl = sb_("ztl", [128, 2, 6 + 384])
    prm = sb_("prm_s", [128, NPRM])
    cv = sb_("cv_s", [128, KC, 3])
    cact = sb_("cact", [128, KC, 3], BF16)
    sa = sb_("sa_s", [128, 2, 2, 4, 30])
    sbs = sb_("sb_s", [128, 2, 2, 4, 15])
    scs = sb_("sc_s", [128, 2, 2, 8, 2])
    modfm = sb_("modfm", [128, 2, 72, 3])
    gsb = sb_("gsb", [128, 2, 3, KC, 3])
    ghb = sb_("ghb", [128, 2, 3, KC, 3])
    ones_bf = sb_("ones_bf", [128, 128], BF16)
    ones_f = sb_("ones_f", [128, 128])
    ident_bf = sb_("ident_bf", [128, 128], BF16)
    awh = sb_("awh", [128, 124])
    ps = [ctx.enter_context(nc.psum_tensor("ps%d" % i, [128, 512], F32)) for i in range(8)]

    pg = Prog(nc, ctx)
    st = {"bank": 0, "slot": 0, "tmp": 0, "sq": 0, "r": 0, "dbf": 0, "bub": 0}

    def bank():
        b = st["bank"]
        st["bank"] = (b + 1) % 8
        return b

    def tmpbuf():
        i = st["tmp"]
        st["tmp"] = (i + 1) % NTMP
        return i

    def P(name, w=1, i=0):
        o = _off[name] + i
        return prm[:, o:o + w]

    held = set()

    def alloc_slot():
        for _ in range(NSLOT):
            s = st["slot"]
            st["slot"] = (s + 1) % NSLOT
            if s not in held:
                return s
        raise RuntimeError("no free weight slot")

    def load_piece(parts):
        s = alloc_slot()
        for (b0, nb, src) in parts:
            def f(e, s=s, b0=b0, nb=nb, src=src):
                dst = slots[:, s, b0:b0 + nb, :]
                if len(src.shape) == 3 and src.shape[2] == 1024:
                    dst = dst.rearrange("p (j m) c -> p j (m c)", m=8)
                return e.dma_start(out=dst, in_=src)
            pg.dma("pool", "slot%d" % s, f, reads=(), writes=(("slot", s),))
        return s

    def wcols(w2d, c0, n=128):
        return w2d.rearrange("(kc p) n -> p kc n", p=128)[:, :, c0:c0 + n]

    def wrows(w2d, r0, nr):
        return w2d.rearrange("(j p) n -> p j n", p=128)[:, r0:r0 + nr, :]

    pe_defer = []

    def pe_tick():
        for d in list(pe_defer):
            d[0] -= 1
            if d[0] <= 0:
                pe_defer.remove(d)
                d[1]()

    def pe_flush():
        while pe_defer:
            d = pe_defer.pop(0)
            d[1]()

    mod_pending = []

    def mod_piece(l, i):
        src = ada_w[l].rearrange("(kc p) n -> p kc n", p=128)[:, :, i * 256:(i + 1) * 256]
        s = alloc_slot()

        def f(e, s=s, src=src):
            return e.dma_start(out=slots[:, s, :, :].rearrange("p (k a) c -> p k (a c)", a=2), in_=src)
        pg.dma("pool", "slot%d" % s, f, writes=(("slot", s),))
        b = bank()

        def mm(e, s=s, b=b):
            last = None
            for ql in range(2):
                for k in range(KC):
                    last = e.matmul(ps[b][:, ql * 3:ql * 3 + 3], lhsT=slots[:, s, k * 2 + ql, :],
                                    rhs=cact[:, k, :], start=(k == 0), stop=(k == KC - 1))
            return last
        pg.op("pe", mm, reads=(("slot", s), "cact"), writes=(("ps", b),))
        for ql in range(2):
            q = 2 * i + ql

            def ev(e, b=b, ql=ql, q=q, l=l):
                return e.activation(out=modfm[:, l % 2, q, :], in_=ps[b][:, ql * 3:ql * 3 + 3],
                                    func=AF.Identity, bias=P("adab", 1, l * 72 + q), scale=1.0)
            pg.op("act", ev, reads=(("ps", b), "prm"), writes=(("mod", l % 2, q // 8),))

    def mod_derive(l, s_):
        lb = l % 2
        for m in range(KC):
            def f(e, m=m):
                o = _off["normg"] + (l * 3 + s_) * 8 + m
                return e.tensor_scalar(out=gsb[:, lb, s_, m, :], in0=modfm[:, lb, (3 * s_ + 1) * 8 + m, :],
                                       scalar1=1.0, scalar2=prm[:, o:o + 1], op0=ALU.add, op1=ALU.mult)
            pg.op("dve", f, reads=(("mod", lb, 3 * s_ + 1), "prm"), writes=(("gs", lb, s_),))

        def f2(e):
            return e.tensor_scalar(out=ghb[:, lb, s_, :, :], in0=modfm[:, lb, (3 * s_ + 2) * 8:(3 * s_ + 3) * 8, :],
                                   scalar1=(1.0 if s_ == 1 else 0.5), scalar2=None, op0=ALU.mult)
        pg.op("dve", f2, reads=(("mod", lb, 3 * s_ + 2),), writes=(("gh", lb, s_),))

    def pump_mod(n=1):
        for _ in range(n):
            if mod_pending:
                l, i = mod_pending.pop(0)
                mod_piece(l, i)
                if i % 12 == 11:
                    mod_derive(l, i // 12)

    def norm_tile(l, s_, ti, final=False, defer=0):
        t0, t1 = TILES[ti]
        N = t1 - t0
        lb = l % 2
        sq = st["sq"]
        st["sq"] = 1 - sq
        for m in range(KC):
            def f(e, m=m):
                return e.activation(out=sqb[:, sq, m, 0:N], in_=x[:, m, t0:t1], func=AF.Square)
            pg.op("act", f, reads=(("x", m, ti),), writes=(("sqb", sq),))

        def rest():
            b = bank()
            ri = st["r"]
            st["r"] = 1 - ri

            def mm(e):
                last = None
                for m in range(KC):
                    last = e.matmul(ps[b][:, 0:N], lhsT=ones_bf[:, :], rhs=sqb[:, sq, m, 0:N],
                                    start=(m == 0), stop=(m == KC - 1))
                return last
            pg.op("pe", mm, reads=(("sqb", sq), "ones"), writes=(("ps", b),))

            def fr0(e):
                return e.activation(out=rbuf[:, ri, 0:N], in_=ps[b][:, 0:N], func=AF.Sqrt, bias=P("epsc", 1), scale=1.0 / D)
            pg.op("act", fr0, reads=(("ps", b), "prm"), writes=(("r", ri),))

            def fr(e):
                return e.reciprocal(out=rbuf[:, ri, 0:N], in_=rbuf[:, ri, 0:N])
            pg.op("dve", fr, reads=(("r", ri),), writes=(("r", ri),))
            for m in range(KC):
                if final:
                    def fy(e, m=m):
                        o = _off["finalg"] + m
                        return e.scalar_tensor_tensor(out=x[:, m, t0:t1], in0=x[:, m, t0:t1], scalar=prm[:, o:o + 1],
                                                      in1=rbuf[:, ri, 0:N], op0=ALU.mult, op1=ALU.mult)
                    pg.op("dve", fy, reads=(("r", ri), "prm"), writes=(("x", m, ti),))
                    continue
                tb = tmpbuf()

                def ft(e, m=m, tb=tb):
                    return e.tensor_tensor(out=tmp[:, tb, 0:N], in0=x[:, m, t0:t1], in1=rbuf[:, ri, 0:N], op=ALU.mult)
                pg.op("dve", ft, reads=(("x", m, ti), ("r", ri)), writes=(("tmp", tb),))
                pcs_ = pieces(t0, t1)
                if len(pcs_) == 3:
                    a0 = pcs_[1][1] - t0

                    def fs1(e, m=m, tb=tb, a0=a0):
                        v = tmp[:, tb, a0:a0 + 2 * LS].rearrange("p (s t) -> p s t", t=LS)
                        return e.tensor_tensor(out=v, in0=v, in1=gsb[:, lb, s_, m, 1:3].unsqueeze(2).to_broadcast([128, 2, LS]),
                                               op=ALU.mult)
                    pg.op("dve", fs1, reads=(("tmp", tb), ("gs", lb, s_)), writes=(("tmp", tb),))

                    def fs2(e, m=m, tb=tb, a0=a0):
                        v = tmp[:, tb, a0:a0 + 2 * LS].rearrange("p (s t) -> p s t", t=LS)
                        o_ = h[:, m, t0 + a0:t0 + a0 + 2 * LS].rearrange("p (s t) -> p s t", t=LS)
                        return e.tensor_tensor(out=o_, in0=v,
                                               in1=modfm[:, lb, (3 * s_) * 8 + m, 1:3].unsqueeze(2).to_broadcast([128, 2, LS]),
                                               op=ALU.add)
                    pg.op("dve", fs2, reads=(("tmp", tb), ("mod", lb, 3 * s_)), writes=(("h", m, ti),))
                    pcs_ = pcs_[:1]
                for (sg, lo, hi) in pcs_:
                    def fh(e, m=m, tb=tb, sg=sg, lo=lo, hi=hi):
                        return e.activation(out=h[:, m, lo:hi], in_=tmp[:, tb, lo - t0:hi - t0], func=AF.Identity,
                                            bias=modfm[:, lb, (3 * s_) * 8 + m, sg:sg + 1],
                                            scale=gsb[:, lb, s_, m, sg:sg + 1])
                    pg.op("act", fh, reads=(("tmp", tb), ("gs", lb, s_), ("mod", lb, 3 * s_)), writes=(("h", m, ti),))
            if final:
                lo_ = max(t0, HALO)

                def fyo(e):
                    return e.dma_start(out=yT_d[:, :, lo_ - HALO:t1 - HALO], in_=x[:, :, lo_:t1])
                pg.dma("sp", "outs", fyo, reads=tuple(("x", m, ti) for m in range(KC)))
        if defer > 0:
            pe_defer.append([defer, rest])
        else:
            rest()

    def out_phase(l, s_, wsrc2d, r0, nk, pre_loop=None, after_tile=None):
        lb = l % 2
        sl = []
        k = 0
        while k < nk:
            n = min(2, nk - k)
            sl.append((load_piece([(0, n * 8, wrows(wsrc2d, r0 + k, n))]), k, n))
            k += n
        if pre_loop is not None:
            pre_loop()
        for ti, (t0, t1) in enumerate(TILES):
            N = t1 - t0
            for m in range(KC):
                b = bank()

                def mm(e, m=m, b=b, t0=t0, t1=t1, N=N):
                    last = None
                    kk = 0
                    for (s, k0, n) in sl:
                        for j in range(n):
                            last = e.matmul(ps[b][:, 0:N], lhsT=slots[:, s, j * 8 + m, :], rhs=act[:, k0 + j, t0:t1],
                                            start=(kk == 0), stop=(kk == nk - 1))
                            kk += 1
                    return last
                pg.op("pe", mm, reads=tuple(("slot", s) for (s, _, _) in sl) + tuple(("act", k_, ti) for k_ in range(nk)),
                      writes=(("ps", b),))
                pe_tick()
                for (sg, lo, hi) in pieces(t0, t1):
                    def fx(e, m=m, b=b, sg=sg, lo=lo, hi=hi, t0=t0):
                        return e.scalar_tensor_tensor(out=x[:, m, lo:hi], in0=ps[b][:, lo - t0:hi - t0],
                                                      scalar=ghb[:, lb, s_, m, sg:sg + 1], in1=x[:, m, lo:hi],
                                                      op0=ALU.mult, op1=ALU.add)
                    pg.op("dve", fx, reads=(("ps", b), ("gh", lb, s_)), writes=(("x", m, ti),))
            if after_tile is not None:
                after_tile(ti)

    FFN_PARTS = [(0, 4), (4, 4), (8, 4), (12, 4), (16, 3), (19, 3)]

    def ffn_chunk_tile(s, jl, ti):
        t0, t1 = TILES[ti]
        N = t1 - t0
        b1, b2 = bank(), bank()

        def mm(e):
            last = None
            for k in range(KC):
                e.matmul(ps[b1][:, 0:N], lhsT=slots[:, s, k, :], rhs=h[:, k, t0:t1], start=(k == 0), stop=(k == KC - 1))
            for k in range(KC):
                last = e.matmul(ps[b2][:, 0:N], lhsT=slots[:, s, 8 + k, :], rhs=h[:, k, t0:t1],
                                start=(k == 0), stop=(k == KC - 1))
            return last
        pg.op("pe", mm, reads=(("slot", s),) + tuple(("h", k, ti) for k in range(KC)), writes=(("ps", b1), ("ps", b2)))
        pe_tick()
        tb = tmpbuf()

        def fs(e):
            return e.activation(out=tmp[:, tb, 0:N], in_=ps[b1][:, 0:N], func=AF.Silu)
        pg.op("act", fs, reads=(("ps", b1),), writes=(("tmp", tb),))

        def fm(e):
            return e.tensor_tensor(out=act[:, jl, t0:t1], in0=tmp[:, tb, 0:N], in1=ps[b2][:, 0:N], op=ALU.mult)
        pg.op("dve", fm, reads=(("tmp", tb), ("ps", b2)), writes=(("act", jl, ti),))

    def ffn_piece(S, j):
        wg = w_gu[S["l"], S["f"]]
        return load_piece([(0, 8, wcols(wg, j * 128)), (8, 8, wcols(wg, DFF + j * 128))])

    def ffn_preload(S):
        S["fs"] = [ffn_piece(S, j) for j in range(4)]
        held.update(S["fs"])

    def ffn_first_tile(S, ti):
        for jl in range(4):
            ffn_chunk_tile(S["fs"][jl], jl, ti)

    def ffn_rest(S, boundary):
        l, s_ = S["l"], S["s"]
        wd = w_dn[l, S["f"]]
        out_phase(l, s_, wd, 0, 4)
        for pi, (j0, nj) in enumerate(FFN_PARTS):
            if pi == 0:
                continue
            for jl in range(nj):
                s = ffn_piece(S, j0 + jl)
                for ti in range(NT):
                    ffn_chunk_tile(s, jl, ti)
                pump_mod(2)
            if pi == len(FFN_PARTS) - 1:
                out_phase(l, s_, wd, j0, nj, pre_loop=boundary[0], after_tile=boundary[1])
            else:
                out_phase(l, s_, wd, j0, nj)

    def c_chunk_tile(S, c, cl, zi, s1, s2, ti):
        o = S["l"] // 2
        t0, t1 = TILES[ti]
        N = t1 - t0
        pcs = pieces(t0, t1)
        bb, bc, bv = bank(), bank(), bank()
        zk = ("ztl", zi)

        def mm(e):
            last = None
            for (bk, s, o8) in ((bb, s1, 0), (bc, s1, 8), (bv, s2, 0)):
                for k in range(KC):
                    last = e.matmul(ps[bk][:, 0:N], lhsT=slots[:, s, o8 + k, :], rhs=h[:, k, t0:t1],
                                    start=(k == 0), stop=(k == KC - 1))
            return last
        pg.op("pe", mm, reads=(("slot", s1), ("slot", s2)) + tuple(("h", k, ti) for k in range(KC)),
              writes=(("ps", bb), ("ps", bc), ("ps", bv)))
        pe_tick()
        tv = tmpbuf()

        def fv(e):
            return e.activation(out=tmp[:, tv, 0:N], in_=ps[bv][:, 0:N], func=AF.Copy)
        pg.op("act", fv, reads=(("ps", bv),), writes=(("tmp", tv),))
        offs = []
        off = 0
        for (sg, lo, hi) in pcs:
            off += 2
            offs.append(off)
            if sg == 0:
                if ti == 0:
                    pg.op("dve", lambda e: e.memset(ztl[:, zi, 0:2], 0.0), writes=(zk,))
                else:
                    pN = TILES[ti - 1][1] - TILES[ti - 1][0]
                    pg.op("dve", lambda e, pN=pN: e.tensor_copy(out=ztl[:, zi, 0:2], in_=ztl[:, zi, pN:pN + 2]),
                          reads=(zk,), writes=(zk,))
            else:
                pg.op("dve", lambda e, off=off, sg=sg: e.tensor_copy(out=ztl[:, zi, off - 2:off], in_=scs[:, o, sg - 1, c, :]),
                      reads=("st_c",), writes=(zk,))
            off += hi - lo
        tot = off
        for (sg, lo, hi), of in zip(pcs, offs):
            def fzz(e, lo=lo, hi=hi, of=of):
                return e.tensor_tensor(out=ztl[:, zi, of:of + hi - lo], in0=ps[bc][:, lo - t0:hi - t0],
                                       in1=tmp[:, tv, lo - t0:hi - t0], op=ALU.mult)
            pg.op("dve", fzz, reads=(("ps", bc), ("tmp", tv)), writes=(zk,))
        if ti == 0:
            pg.op("dve", lambda e: e.tensor_tensor(out=ztl[:, zi, 2:2 + HALO], in0=ztl[:, zi, 2:2 + HALO],
                                                   in1=P("hmask", HALO), op=ALU.mult), reads=("prm",), writes=(zk,))
        Wb = tot - 2
        ta, tb_ = tmpbuf(), tmpbuf()
        cw = _off["cconvw"] + (o * 8 + c) * 3

        def f0(e):
            return e.activation(out=tmp[:, ta, 0:Wb], in_=ztl[:, zi, 0:Wb], func=AF.Identity, scale=prm[:, cw:cw + 1])
        pg.op("act", f0, reads=(zk, "prm"), writes=(("tmp", ta),))

        def f1(e):
            return e.scalar_tensor_tensor(out=tmp[:, tb_, 0:Wb], in0=ztl[:, zi, 1:1 + Wb], scalar=prm[:, cw + 1:cw + 2],
                                          in1=tmp[:, ta, 0:Wb], op0=ALU.mult, op1=ALU.add)
        pg.op("dve", f1, reads=(zk, ("tmp", ta)), writes=(("tmp", tb_),))

        def f2(e):
            return e.scalar_tensor_tensor(out=tmp[:, ta, 0:Wb], in0=ztl[:, zi, 2:2 + Wb], scalar=prm[:, cw + 2:cw + 3],
                                          in1=tmp[:, tb_, 0:Wb], op0=ALU.mult, op1=ALU.add)
        pg.op("dve", f2, reads=(zk, ("tmp", tb_)), writes=(("tmp", ta),))
        tbg = tmpbuf()
        pg.op("act", lambda e: e.activation(out=tmp[:, tbg, 0:N], in_=ps[bb][:, 0:N], func=AF.Copy),
              reads=(("ps", bb),), writes=(("tmp", tbg),))
        for (sg, lo, hi), of in zip(pcs, offs):
            def fo(e, lo=lo, hi=hi, of=of):
                return e.tensor_tensor(out=act[:, cl, lo:hi], in0=tmp[:, ta, of - 2:of - 2 + hi - lo],
                                       in1=tmp[:, tbg, lo - t0:hi - t0], op=ALU.mult)
            pg.op("dve", fo, reads=(("tmp", ta), ("tmp", tbg)), writes=(("act", cl, ti),))
        if ti == NT - 1:
            for (sg, lo, hi), of in zip(pcs, offs):
                e0 = of + (hi - lo) - 2

                def fd(e, sg=sg, e0=e0):
                    return e.dma_start(out=oc_d[:, o, sg, c, :], in_=ztl[:, zi, e0:e0 + 2])
                pg.dma("sp", "outs", fd, reads=(zk,))

    def c_pieces(S, c):
        win = c_in[S["l"] // 2]
        s1 = load_piece([(0, 8, wcols(win, c * 128)), (8, 8, wcols(win, D + c * 128))])
        s2 = load_piece([(0, 8, wcols(win, 2 * D + c * 128))])
        return s1, s2

    def c_preload(S):
        S["fs"] = [c_pieces(S, 0), c_pieces(S, 1)]
        held.update(S["fs"][0] + S["fs"][1])

    def c_first_tile(S, ti):
        for c in range(2):
            c_chunk_tile(S, c, c, c, S["fs"][c][0], S["fs"][c][1], ti)

    def c_rest(S, boundary):
        l = S["l"]
        o = l // 2
        for hf in range(2):
            for cl in range(4):
                c = hf * 4 + cl
                if hf == 0 and cl < 2:
                    continue
                s1, s2 = c_pieces(S, c)
                for ti in range(NT):
                    c_chunk_tile(S, c, cl, 0, s1, s2, ti)
                pump_mod(2)
            if hf == 1:
                out_phase(l, 1, c_out[o], hf * 4, 4, pre_loop=boundary[0], after_tile=boundary[1])
            else:
                out_phase(l, 1, c_out[o], hf * 4, 4)

    def ab_preload(S):
        win = ab_in[S["l"] // 2]
        S["fs"] = [load_piece([(0, 8, wcols(win, c * 128)), (8, 8, wcols(win, 512 + c * 128))]) for c in range(4)]
        held.update(S["fs"])

    def ab_first_tile(S, ti):
        e_ = S["l"] // 2
        sA = S["fs"]
        t0, t1 = TILES[ti]
        N = t1 - t0
        pcs = pieces(t0, t1)
        offs = []
        off = 0
        for (sg, lo, hi) in pcs:
            off += 30
            offs.append(off)
            off += hi - lo
        tot = off
        Wb = tot - 30

        def inproj_glu(c):
            bu_, bg_ = bank(), bank()

            def mm(e, s=sA[c]):
                last = None
                for (bk, o8) in ((bu_, 0), (bg_, 8)):
                    for k in range(KC):
                        last = e.matmul(ps[bk][:, 0:N], lhsT=slots[:, s, o8 + k, :], rhs=h[:, k, t0:t1],
                                        start=(k == 0), stop=(k == KC - 1))
                return last
            pg.op("pe", mm, reads=(("slot", sA[c]),) + tuple(("h", k, ti) for k in range(KC)),
                  writes=(("ps", bu_), ("ps", bg_)))
            pe_tick()
            tsg = tmpbuf()
            pg.op("act", lambda e: e.activation(out=tmp[:, tsg, 0:N], in_=ps[bg_][:, 0:N], func=AF.Tanh, scale=0.5),
                  reads=(("ps", bg_),), writes=(("tmp", tsg),))
            ak = ("abf", c)
            for (sg, lo, hi), of in zip(pcs, offs):
                if sg == 0:
                    if ti == 0:
                        pg.op("dve", lambda e: e.memset(abf[:, c, 0:30], 0.0), writes=(ak,))
                    else:
                        pN = TILES[ti - 1][1] - TILES[ti - 1][0]
                        pg.op("dve", lambda e, pN=pN: e.tensor_copy(out=abf[:, c, 0:30], in_=abf[:, c, pN:pN + 30]),
                              reads=(ak,), writes=(ak,))
                else:
                    pg.op("dve", lambda e, of=of, sg=sg: e.tensor_scalar(out=abf[:, c, of - 30:of], in0=sa[:, e_, sg - 1, c, :],
                                                                         scalar1=2.0, scalar2=None, op0=ALU.mult),
                          reads=("st_a",), writes=(ak,))

                def fa(e, lo=lo, hi=hi, of=of):
                    return e.scalar_tensor_tensor(out=abf[:, c, of:of + hi - lo], in0=tmp[:, tsg, lo - t0:hi - t0], scalar=1.0,
                                                  in1=ps[bu_][:, lo - t0:hi - t0], op0=ALU.add, op1=ALU.mult)
                pg.op("dve", fa, reads=(("ps", bu_), ("tmp", tsg)), writes=(ak,))
                if ti == NT - 1:
                    def fst(e, hi=hi, sg=sg):
                        return e.scalar_tensor_tensor(out=astg[:, sg, c, :], in0=tmp[:, tsg, hi - 30 - t0:hi - t0], scalar=1.0,
                                                      in1=ps[bu_][:, hi - 30 - t0:hi - t0], op0=ALU.add, op1=ALU.mult)
                    pg.op("dve", fst, reads=(("ps", bu_), ("tmp", tsg)), writes=("astg",))
                    pg.op("dve", lambda e, sg=sg: e.tensor_scalar(out=astg[:, sg, c, :], in0=astg[:, sg, c, :], scalar1=0.5,
                                                                  scalar2=None, op0=ALU.mult), reads=("astg",), writes=("astg",))
            if ti == 0:
                pg.op("dve", lambda e: e.tensor_tensor(out=abf[:, c, 30:30 + HALO], in0=abf[:, c, 30:30 + HALO],
                                                       in1=P("hmask", HALO), op=ALU.mult), reads=("prm",), writes=(ak,))

        def gen(c):
            for hf_, (k0, k1) in enumerate(((0, 16), (16, 31))):
                def fld(e, k0=k0, k1=k1, hf_=hf_):
                    n = k1 - k0
                    return e.dma_start(out=dg[:, k0:k1, :].rearrange("p k j -> p (k j)"), in_=dgd[e_, c, hf_, :, 0:n * 128])
                pg.dma("sp", "dgl%d" % hf_, fld, reads=(("dgd", e_, hf_),), writes=(("dg", hf_),))

        def conv(c):
            ak = ("abf", c)
            bcv = bank()

            def mc0(e):
                last = None
                for k in range(16):
                    last = e.matmul(ps[bcv][:, 0:Wb], lhsT=dg[:, k, :], rhs=abf[:, c, k:k + Wb], start=(k == 0), stop=False)
                return last
            pg.op("pe", mc0, reads=(("dg", 0), ak), writes=(("ps", bcv),))

            def mc1(e):
                last = None
                for k in range(16, 31):
                    last = e.matmul(ps[bcv][:, 0:Wb], lhsT=dg[:, k, :], rhs=abf[:, c, k:k + Wb], start=False, stop=(k == 30))
                return last
            pg.op("pe", mc1, reads=(("dg", 1), ak), writes=(("ps", bcv),))
            pe_tick()
            cbo = _off["aconvb"] + e_ * 4 + c
            pg.op("act", lambda e: e.activation(out=cob[:, c, 0:Wb], in_=ps[bcv][:, 0:Wb], func=AF.Identity,
                                                bias=prm[:, cbo:cbo + 1], scale=1.0),
                  reads=(("ps", bcv), "prm"), writes=(("cob", c),))

        inproj_glu(0)
        gen(0)
        for c in range(4):
            if c + 1 < 4:
                inproj_glu(c + 1)
            conv(c)
            if c + 1 < 4:
                gen(c + 1)
        if ti == NT - 1:
            for (sg, lo, hi) in pcs:
                def fd(e, sg=sg):
                    return e.dma_start(out=oa_d[:, e_, sg, :, :], in_=astg[:, sg, :, :])
                pg.dma("sp", "outs", fd, reads=("astg",))

        def ln_tail():
            b1, b2 = bank(), bank()

            def mm1(e):
                last = None
                for c in range(4):
                    last = e.matmul(ps[b1][:, 0:Wb], lhsT=ones_f[:, :], rhs=cob[:, c, 0:Wb], start=(c == 0), stop=(c == 3))
                return last
            pg.op("pe", mm1, reads=tuple(("cob", c) for c in range(4)) + ("ones",), writes=(("ps", b1),))
            sqt = []
            for c in range(4):
                tq = tmpbuf()
                sqt.append(tq)
                pg.op("act", lambda e, c=c, tq=tq: e.activation(out=tmp[:, tq, 0:Wb], in_=cob[:, c, 0:Wb], func=AF.Square),
                      reads=(("cob", c),), writes=(("tmp", tq),))

            def mm2(e):
                last = None
                for c in range(4):
                    last = e.matmul(ps[b2][:, 0:Wb], lhsT=ones_f[:, :], rhs=tmp[:, sqt[c], 0:Wb], start=(c == 0), stop=(c == 3))
                return last
            pg.op("pe", mm2, reads=tuple(("tmp", tq) for tq in sqt) + ("ones",), writes=(("ps", b2),))
            pg.op("dve", lambda e: e.tensor_scalar(out=lnm[:, 0, 0:Wb], in0=ps[b1][:, 0:Wb], scalar1=1.0 / 512, scalar2=None, op0=ALU.mult),
                  reads=(("ps", b1),), writes=("lnmean",))
            tm = tmpbuf()
            pg.op("dve", lambda e: e.tensor_tensor(out=tmp[:, tm, 0:Wb], in0=lnm[:, 0, 0:Wb], in1=lnm[:, 0, 0:Wb], op=ALU.mult),
                  reads=("lnmean",), writes=(("tmp", tm),))
            pg.op("dve", lambda e: e.scalar_tensor_tensor(out=lnm[:, 1, 0:Wb], in0=ps[b2][:, 0:Wb], scalar=1.0 / 512, in1=tmp[:, tm, 0:Wb],
                                                          op0=ALU.mult, op1=ALU.subtract),
                  reads=(("ps", b2), ("tmp", tm)), writes=("lnrstd",))
            pg.op("act", lambda e: e.activation(out=lnm[:, 1, 0:Wb], in_=lnm[:, 1, 0:Wb], func=AF.Sqrt, bias=P("epsc", 1), scale=1.0),
                  reads=("lnrstd", "prm"), writes=("lnrstd",))
            pg.op("dve", lambda e: e.reciprocal(out=lnm[:, 1, 0:Wb], in_=lnm[:, 1, 0:Wb]), reads=("lnrstd",), writes=("lnrstd",))
            for c in range(4):
                t1_, t2_ = tmpbuf(), tmpbuf()
                pg.op("dve", lambda e, c=c, t1_=t1_: e.tensor_tensor(out=tmp[:, t1_, 0:Wb], in0=cob[:, c, 0:Wb], in1=lnm[:, 0, 0:Wb], op=ALU.subtract),
                      reads=(("cob", c), "lnmean"), writes=(("tmp", t1_),))
                pg.op("dve", lambda e, t1_=t1_, t2_=t2_: e.tensor_tensor(out=tmp[:, t2_, 0:Wb], in0=tmp[:, t1_, 0:Wb], in1=lnm[:, 1, 0:Wb], op=ALU.mult),
                      reads=(("tmp", t1_), "lnrstd"), writes=(("tmp", t2_),))
                go = _off["alng"] + e_ * 4 + c
                bo = _off["alnb"] + e_ * 4 + c
                for (sg, lo, hi), of in zip(pcs, offs):
                    def fsl(e, c=c, t2_=t2_, lo=lo, hi=hi, of=of, go=go, bo=bo):
                        return e.activation(out=act[:, c, lo:hi], in_=tmp[:, t2_, of - 30:of - 30 + hi - lo], func=AF.Silu,
                                            bias=prm[:, bo:bo + 1], scale=prm[:, go:go + 1])
                    pg.op("act", fsl, reads=(("tmp", t2_), "prm"), writes=(("act", c, ti),))
        pe_defer.append([2, ln_tail])

    def ab_rest(S, boundary):
        l = S["l"]
        e_ = l // 2
        win = ab_in[e_]
        out_phase(l, 1, ab_out[e_], 0, 4)
        for g in range(4):
            wnd = 2 << g
            s = load_piece([(0, 8, wcols(win, 1024 + g * 128)), (8, 1, b_wg[e_, g].rearrange("p (o c) -> p o c", o=1))])
            for ti, (t0, t1) in enumerate(TILES):
                N = t1 - t0
                pcs = pieces(t0, t1)
                offs = []
                off = 0
                for (sg, lo, hi) in pcs:
                    off += 15
                    offs.append(off)
                    off += hi - lo
                tot = off
                Wb = tot - 15
                bb = bank()
                bi = st["bub"]
                st["bub"] = 1 - bi
                bk_ = ("bub", bi)

                def mm(e, s=s, bb=bb, t0=t0, t1=t1, N=N):
                    last = None
                    for k in range(KC):
                        last = e.matmul(ps[bb][:, 0:N], lhsT=slots[:, s, k, :], rhs=h[:, k, t0:t1], start=(k == 0), stop=(k == KC - 1))
                    return last
                pg.op("pe", mm, reads=(("slot", s),) + tuple(("h", k, ti) for k in range(KC)), writes=(("ps", bb),))
                pe_tick()
                for (sg, lo, hi), of in zip(pcs, offs):
                    if sg == 0:
                        if ti == 0:
                            pg.op("dve", lambda e, bi=bi: e.memset(bub[:, bi, 0:15], 0.0), writes=(bk_,))
                        else:
                            pN = TILES[ti - 1][1] - TILES[ti - 1][0]
                            pg.op("dve", lambda e, bi=bi, pN=pN: e.tensor_copy(out=bub[:, bi, 0:15], in_=bub[:, 1 - bi, pN:pN + 15]),
                                  reads=(("bub", 1 - bi),), writes=(bk_,))
                    else:
                        pg.op("dve", lambda e, bi=bi, of=of, sg=sg, g=g: e.tensor_copy(out=bub[:, bi, of - 15:of], in_=sbs[:, e_, sg - 1, g, :]),
                              reads=("st_b",), writes=(bk_,))
                    pg.op("act", lambda e, bi=bi, bb=bb, lo=lo, hi=hi, of=of, t0=t0: e.activation(
                        out=bub[:, bi, of:of + hi - lo], in_=ps[bb][:, lo - t0:hi - t0], func=AF.Copy),
                        reads=(("ps", bb),), writes=(bk_,))
                if ti == 0:
                    pg.op("dve", lambda e, bi=bi: e.tensor_tensor(out=bub[:, bi, 15:15 + HALO], in0=bub[:, bi, 15:15 + HALO],
                                                                  in1=P("hmask", HALO), op=ALU.mult), reads=("prm",), writes=(bk_,))
                cur = None
                sh = 1
                for lev in range(g + 1):
                    tn = tmpbuf()
                    if cur is None:
                        pg.op("dve", lambda e, bi=bi, tn=tn, sh=sh, tot=tot: e.tensor_tensor(
                            out=tmp[:, tn, sh:tot], in0=bub[:, bi, sh:tot], in1=bub[:, bi, 0:tot - sh], op=ALU.add),
                            reads=(bk_,), writes=(("tmp", tn),))
                    else:
                        pg.op("dve", lambda e, tn=tn, cur=cur, sh=sh, tot=tot: e.tensor_tensor(
                            out=tmp[:, tn, sh:tot], in0=tmp[:, cur, sh:tot], in1=tmp[:, cur, 0:tot - sh], op=ALU.add),
                            reads=(("tmp", cur),), writes=(("tmp", tn),))
                    cur = tn
                    sh *= 2
                di = st["dbf"]
                st["dbf"] = 1 - di
                pg.op("dve", lambda e, bi=bi, cur=cur, di=di, Wb=Wb, wnd=wnd: e.scalar_tensor_tensor(
                    out=dbf[:, di, 0:Wb], in0=tmp[:, cur, 15:15 + Wb], scalar=1.0 / wnd, in1=bub[:, bi, 15:15 + Wb],
                    op0=ALU.mult, op1=ALU.subtract), reads=(("tmp", cur), bk_), writes=(("dbf", di),))
                if ti == 0:
                    tf = tmpbuf()
                    io = _off["invcnt"] + g * 16
                    pg.op("dve", lambda e, cur=cur, tf=tf, io=io: e.tensor_tensor(
                        out=tmp[:, tf, 0:16], in0=tmp[:, cur, 15 + HALO:15 + HALO + 16], in1=prm[:, io:io + 16], op=ALU.mult),
                        reads=(("tmp", cur), "prm"), writes=(("tmp", tf),))
                    pg.op("dve", lambda e, bi=bi, tf=tf, di=di: e.tensor_tensor(
                        out=dbf[:, di, HALO:HALO + 16], in0=tmp[:, tf, 0:16], in1=bub[:, bi, 15 + HALO:15 + HALO + 16], op=ALU.subtract),
                        reads=(("tmp", tf), bk_), writes=(("dbf", di),))
                if ti == NT - 1:
                    for (sg, lo, hi), of in zip(pcs, offs):
                        e0 = of + (hi - lo) - 15

                        def fd(e, bi=bi, sg=sg, e0=e0, g=g):
                            return e.dma_start(out=ob_d[:, e_, sg, g, :], in_=bub[:, bi, e0:e0 + 15])
                        pg.dma("sp", "outs", fd, reads=(bk_,))

                def grp(s=s, di=di, Wb=Wb, g=g, ti=ti, pcs=pcs, offs=offs):
                    bd = bank()
                    pg.op("pe", lambda e: e.matmul(ps[bd][:, 0:Wb], lhsT=slots[:, s, 8, :], rhs=dbf[:, di, 0:Wb], start=True, stop=True),
                          reads=(("slot", s), ("dbf", di)), writes=(("ps", bd),))
                    so = _off["bscale"] + e_ * 4 + g
                    for (sg, lo, hi), of in zip(pcs, offs):
                        pg.op("act", lambda e, lo=lo, hi=hi, of=of: e.activation(
                            out=act[:, g, lo:hi], in_=ps[bd][:, of - 15:of - 15 + hi - lo], func=AF.Identity, scale=prm[:, so:so + 1]),
                            reads=(("ps", bd), "prm"), writes=(("act", g, ti),))
                pe_defer.append([1, grp])
            pump_mod(1)
        pe_flush()
        out_phase(l, 1, ab_out[e_], 4, 4, pre_loop=boundary[0], after_tile=boundary[1])

    def f_prm(e):
        return e.dma_start(out=prm[:, :], in_=prm_d)
    pg.dma("sp", "ld_prm", f_prm, writes=("prm",))
    pg.dma("sp", "ld_cv", lambda e: e.dma_start(out=cv[:, :, :], in_=cv_d), writes=("cv",))
    pg.dma("sp", "ld_id", lambda e: e.dma_start(out=tmp[:, 0, 0:128], in_=id_d), writes=(("tmp", 0),))
    x_loads = []
    for ti, (t0, t1) in enumerate(TILES):
        def fxl(e, t0=t0, t1=t1):
            return e.dma_start(out=x[:, :, t0:t1], in_=xT_d[:, :, t0:t1])
        x_loads.append((ti, fxl))
    ti0, f0_ = x_loads[0]
    pg.dma("sp", "ld_x0", f0_, writes=tuple(("x", m, 0) for m in range(KC)))
    pg.dma("sp", "ld_sa", lambda e: e.dma_start(out=sa[:, :, :, :, :].rearrange("p a b c d -> p (a b c d)"), in_=sa_d), writes=("st_a",))
    pg.dma("sp", "ld_sb", lambda e: e.dma_start(out=sbs[:, :, :, :, :].rearrange("p a b c d -> p (a b c d)"), in_=sb_d), writes=("st_b",))
    pg.dma("sp", "ld_sc", lambda e: e.dma_start(out=scs[:, :, :, :, :].rearrange("p a b c d -> p (a b c d)"), in_=sc_d), writes=("st_c",))

    pg.op("dve", lambda e: e.memset(ones_bf[:, :], 1.0), writes=("ones",))
    pg.op("dve", lambda e: e.memset(ones_f[:, :], 1.0), writes=("ones",))
    pg.op("dve", lambda e: e.tensor_copy(out=ident_bf[:, :], in_=tmp[:, 0, 0:128]), reads=(("tmp", 0),), writes=("ident",))
    st["tmp"] = 1
    pg.op("act", lambda e: e.activation(out=cact[:, :, :], in_=cv[:, :, :], func=AF.Silu), reads=("cv",), writes=("cact",))

    def gen_diag(e2, anchor):
        o_ = _off["aconvw"] + e2 * 124
        pg.op("dve", lambda e, o_=o_: e.tensor_scalar(out=awh[:, :], in0=prm[:, o_:o_ + 124], scalar1=0.5, scalar2=None, op0=ALU.mult),
              reads=("prm", anchor), writes=("awh",))
        for c in range(4):
            for hf_, (k0, k1) in enumerate(((0, 16), (16, 31))):
                def fdg(e, c=c, k0=k0, k1=k1):
                    n = k1 - k0
                    return e.tensor_tensor(out=dg[:, k0:k1, :],
                                           in0=ident_bf[:, :].unsqueeze(1).to_broadcast([128, n, 128]),
                                           in1=awh[:, c * 31 + k0:c * 31 + k1].unsqueeze(2).to_broadcast([128, n, 128]),
                                           op=ALU.mult)
                pg.op("dve", fdg, reads=("ident", "awh"), writes=(("dg", hf_),))

                def fst_(e, e2=e2, c=c, k0=k0, k1=k1, hf_=hf_):
                    n = k1 - k0
                    return e.dma_start(out=dgd[e2, c, hf_, :, 0:n * 128], in_=dg[:, k0:k1, :].rearrange("p k j -> p (k j)"))
                pg.dma("sp", "dgw%d" % hf_, fst_, reads=(("dg", hf_),), writes=(("dgd", e2, hf_),))

    subs = []
    for l in range(DEPTH):
        subs.append({"kind": "ffn", "l": l, "s": 0, "f": 0})
        subs.append({"kind": "ab" if l % 2 == 0 else "c", "l": l, "s": 1})
        subs.append({"kind": "ffn", "l": l, "s": 2, "f": 1})
    PRE = {"ffn": ffn_preload, "ab": ab_preload, "c": c_preload}
    FIRST = {"ffn": ffn_first_tile, "ab": ab_first_tile, "c": c_first_tile}
    REST = {"ffn": ffn_rest, "ab": ab_rest, "c": c_rest}
    LAG = 2
    NDEF = 5

    for i in range(36):
        mod_pending.append((0, i))
    S0 = subs[0]
    for i in range(12):
        if i == 8:
            PRE["ffn"](S0)
        s_before = st["slot"]
        pump_mod(1)
        if 1 <= i <= 5:
            ti_, fx_ = x_loads[i]
            pg.dma("sp", "ld_x%d" % ti_, fx_, reads=(("slot", s_before),), writes=tuple(("x", m, ti_) for m in range(KC)))
    for i in range(36):
        mod_pending.append((1, i))

    for ti in range(NT):
        norm_tile(0, 0, ti)
        if ti >= 1:
            FIRST["ffn"](S0, ti - 1)
    FIRST["ffn"](S0, NT - 1)
    held.clear()

    for i, S in enumerate(subs):
        nxt = subs[i + 1] if i + 1 < len(subs) else None
        if i == 0:
            gen_diag(0, ("act", 0, NT - 1))
        if S["kind"] == "ffn" and S["s"] == 0 and S["l"] == 1:
            gen_diag(1, ("act", 0, NT - 1))
        if S["kind"] == "ffn" and S["s"] == 0 and S["l"] >= 1 and S["l"] + 1 < DEPTH:
            for i_ in range(36):
                mod_pending.append((S["l"] + 1, i_))

        def pre(nxt=nxt):
            if nxt is not None:
                PRE[nxt["kind"]](nxt)

        def cb(ti, nxt=nxt):
            if nxt is None:
                norm_tile(DEPTH - 1, 0, ti, final=True, defer=3)
            else:
                norm_tile(nxt["l"], nxt["s"], ti, defer=NDEF)
                if ti - LAG >= 0:
                    FIRST[nxt["kind"]](nxt, ti - LAG)
        REST[S["kind"]](S, (pre, cb))
        pe_flush()
        if nxt is not None:
            for ti in range(NT - LAG, NT):
                FIRST[nxt["kind"]](nxt, ti)
        held.clear()
        pe_flush()

    pg.final_wait("sp", ["outs"])
    sim = pg.finalize()
    _SIM["time_us"] = sim
    _SIM["pg"] = pg
    _SIM["busy"] = {e: sum(pg.ops[i]["dur"] for i in pg.order[e]) for e in pg.ENGS}

    with nc.Block() as block:
        @block.tensor
        def _(e):
            pg.run("pe", e)

        @block.scalar
        def _(e):
            pg.run("act", e)

        @block.vector
        def _(e):
            pg.run("dve", e)

        @block.gpsimd
        def _(e):
            pg.run("pool", e)

        @block.sync
        def _(e):
            pg.run("sp", e)


def _fm(v):
    v = np.asarray(v, dtype=np.float32)
    n = v.shape[-1] // 128
    v = v.reshape(v.shape[:-1] + (n, 128))
    return np.ascontiguousarray(np.moveaxis(v, -1, 0))


_NC_CACHE = {}


def kernel(x_prompt, x_sample, state_conv_a, state_pool_b, state_conv_c, c_prompt, c_sample,
           ada_w, ada_b, norm_g, ffn_w_gu, ffn_w_down, ab_w_in, a_conv_w, a_conv_b, a_ln_g,
           a_ln_b, b_w_group, b_scale, ab_w_out, c_w_in, c_conv_w, c_w_out, final_g):
    f32 = np.float32
    xp = np.asarray(x_prompt, f32)[0]
    xs = np.asarray(x_sample, f32)
    prm_base = np.zeros((128, NPRM), f32)

    def put(name, arr):
        arr = np.ascontiguousarray(arr, dtype=f32).reshape(128, -1)
        prm_base[:, _off[name]:_off[name] + arr.shape[1]] = arr
    put("normg", _fm(norm_g))
    put("finalg", _fm(final_g))
    put("adab", _fm(ada_b))
    put("aconvw", np.transpose(_fm(a_conv_w), (0, 1, 3, 2)))
    put("aconvb", _fm(a_conv_b))
    put("alng", _fm(a_ln_g))
    put("alnb", _fm(a_ln_b))
    put("bscale", _fm(b_scale))
    put("cconvw", np.transpose(_fm(c_conv_w), (0, 1, 3, 2)))
    sca = np.asarray(state_conv_a, f32)
    spb = np.asarray(state_pool_b, f32)
    scc = np.asarray(state_conv_c, f32)
    cpr = np.asarray(c_prompt, f32)
    csa = np.asarray(c_sample, f32)
    shared = {
        "ada_w": np.ascontiguousarray(ada_w, dtype=f32), "ffn_w_gu": np.ascontiguousarray(ffn_w_gu, dtype=f32),
        "ffn_w_down": np.ascontiguousarray(ffn_w_down, dtype=f32), "ab_w_in": np.ascontiguousarray(ab_w_in, dtype=f32),
        "b_w_group": np.ascontiguousarray(b_w_group, dtype=f32), "ab_w_out": np.ascontiguousarray(ab_w_out, dtype=f32),
        "c_w_in": np.ascontiguousarray(c_w_in, dtype=f32), "c_w_out": np.ascontiguousarray(c_w_out, dtype=f32),
    }
    in_maps = []
    for i in range(NCORES):
        toks = np.zeros((T, D), f32)
        if i == 0:
            toks[HALO:HALO + MAIN] = xp[0:MAIN]
        else:
            toks[0:HALO + MAIN] = xp[i * MAIN - HALO:(i + 1) * MAIN]
        toks[HALO + MAIN:HALO + MAIN + LS] = xs[2 * i]
        toks[HALO + MAIN + LS:] = xs[2 * i + 1]
        xT = np.ascontiguousarray(toks.reshape(T, KC, 128).transpose(2, 1, 0))
        cvec = np.stack([cpr[0], csa[2 * i], csa[2 * i + 1]], 0)
        cvf = np.ascontiguousarray(cvec.reshape(3, KC, 128).transpose(2, 1, 0))
        prm = prm_base.copy()
        prm[:, _off["hmask"]:_off["hmask"] + HALO] = 0.0 if i == 0 else 1.0
        ic = np.zeros((4, 16), f32)
        for g in range(4):
            w = 2 << g
            for p_ in range(16):
                ic[g, p_] = (1.0 / min(w, p_ + 1)) if i == 0 else (1.0 / w)
        prm[:, _off["invcnt"]:_off["invcnt"] + 64] = ic.reshape(1, 64)
        prm[:, _off["epsc"]] = EPS
        sa_i = np.ascontiguousarray(np.transpose(_fm(sca[:, 2 * i:2 * i + 2]), (0, 1, 2, 4, 3))).reshape(128, -1)
        sb_i = np.ascontiguousarray(np.transpose(_fm(spb[:, 2 * i:2 * i + 2]), (0, 1, 2, 4, 3))).reshape(128, -1)
        sc_i = np.ascontiguousarray(np.transpose(_fm(scc[:, 2 * i:2 * i + 2]), (0, 1, 2, 4, 3))).reshape(128, -1)
        m = {"xT": xT, "cvec": cvf, "prm": prm, "sa": sa_i, "sb": sb_i, "scn": sc_i, "ident": np.eye(128, dtype=f32)}
        m.update(shared)
        in_maps.append(m)

    if "nc" not in _NC_CACHE:
        _NC_CACHE["nc"] = build_nc()
    nc = _NC_CACHE["nc"]
    res = run_bass_kernel_spmd(nc, in_maps, core_ids=list(range(NCORES)))
    R = res.results

    y_prompt = np.zeros((1, NCORES * MAIN, D), f32)
    y_sample = np.zeros((2 * NCORES, LS, D), f32)
    na_s = np.zeros((2, 2 * NCORES, 30, 512), f32)
    nb_s = np.zeros((2, 2 * NCORES, 15, 512), f32)
    nc_s = np.zeros((2, 2 * NCORES, 2, 1024), f32)
    for i in range(NCORES):
        yT = np.asarray(R[i]["yT"], f32)
        rows = yT.transpose(2, 1, 0).reshape(MAIN + 2 * LS, D)
        y_prompt[0, i * MAIN:(i + 1) * MAIN] = rows[:MAIN]
        y_sample[2 * i] = rows[MAIN:MAIN + LS]
        y_sample[2 * i + 1] = rows[MAIN + LS:]
        oa = np.asarray(R[i]["oa"], f32).transpose(1, 2, 4, 3, 0).reshape(2, 3, 30, 512)
        ob = np.asarray(R[i]["ob"], f32).transpose(1, 2, 4, 3, 0).reshape(2, 3, 15, 512)
        oc = np.asarray(R[i]["oc"], f32).transpose(1, 2, 4, 3, 0).reshape(2, 3, 2, 1024)
        na_s[:, 2 * i:2 * i + 2] = oa[:, 1:3]
        nb_s[:, 2 * i:2 * i + 2] = ob[:, 1:3]
        nc_s[:, 2 * i:2 * i + 2] = oc[:, 1:3]
        if i == NCORES - 1:
            na_p = oa[:, 0:1].copy()
            nb_p = ob[:, 0:1].copy()
            nc_p = oc[:, 0:1].copy()
    return (y_prompt, y_sample, na_p, nb_p, nc_p, na_s, nb_s, nc_s)
```
